# Optimizing a Trainium2 kernel written in Bass

```python
import math
import jax, jax.numpy as jnp
from jax import lax
import numpy as np

D_MODEL = 4096
BATCH = 4
SEQ = 2048
DEPTH = 1
DEC_BATCH = 32
DEC_SEQ = 8
PAST_LEN = 8192
PAGE_SIZE = 128

D_MIX = D_MODEL
D_ATTN = D_MIX // 2
D_SSM = D_MIX - D_ATTN
ATTN_VHEAD = 128
ATTN_SUB = ATTN_VHEAD // 2
N_HEADS = D_ATTN // ATTN_VHEAD
N_KV_HEADS = max(1, N_HEADS // 4)
ROT_DIM = ATTN_SUB // 4
ROPE_THETA = 500000.0
Q_BLOCK = 128
SSM_HEAD_DIM = 64
N_SSM_HEADS = D_SSM // SSM_HEAD_DIM
N_SSM_GROUPS = 8
SSM_STATE = 128
CONV_WIDTH = 4
CONV_DIM = D_SSM + 2 * N_SSM_GROUPS * SSM_STATE
SSD_CHUNK = 128
D_FF = 4 * D_MODEL
EPS = 1e-6
D_Q = N_HEADS * 2 * ATTN_SUB
D_K = N_KV_HEADS * 2 * ATTN_SUB
D_V = N_KV_HEADS * ATTN_VHEAD
D_IN = D_Q + D_K + D_V + D_SSM + CONV_DIM + N_SSM_HEADS

kernel_name = "hymba_diffattn_ssd_decode_step"

F32 = jnp.float32


def rms_norm(x, w):
    x32 = x.astype(F32)
    y = x32 * lax.rsqrt(jnp.mean(x32 * x32, axis=-1, keepdims=True) + EPS)
    return (y * w.astype(F32)).astype(x.dtype)


def partial_rope(x, pos):
    half = ROT_DIM // 2
    inv = jnp.exp(-math.log(ROPE_THETA) * jnp.arange(half, dtype=F32) * 2.0 / ROT_DIM)
    ang = pos.astype(F32)[:, None] * inv[None, :]
    cos = jnp.cos(ang)[None, :, None, None, :]
    sin = jnp.sin(ang)[None, :, None, None, :]
    x32 = x.astype(F32)
    x1, x2 = x32[..., :half], x32[..., half:ROT_DIM]
    out = jnp.concatenate([x1 * cos - x2 * sin, x2 * cos + x1 * sin, x32[..., ROT_DIM:]], axis=-1)
    return out.astype(x.dtype)


def diff_attention(q, k, v, q_pos, k_pos, lam):
    b, lq = q.shape[0], q.shape[1]
    grp = N_HEADS // N_KV_HEADS
    qb = min(Q_BLOCK, lq)
    pad = (-lq) % qb
    q = q.reshape(b, lq, N_KV_HEADS, grp, 2, ATTN_SUB)
    if pad:
        q = jnp.pad(q, ((0, 0), (0, pad), (0, 0), (0, 0), (0, 0), (0, 0)))
        q_pos = jnp.pad(q_pos, (0, pad), mode="edge")
    nb = (lq + pad) // qb
    q_blocks = q.reshape(b, nb, qb, N_KV_HEADS, grp, 2, ATTN_SUB).transpose(1, 0, 2, 3, 4, 5, 6)
    pos_blocks = q_pos.reshape(nb, qb)
    scale = ATTN_SUB ** -0.5

    def one_block(args):
        qblk, qp = args
        s = jnp.einsum("bqkgcd,bskcd->bkgcqs", qblk, k, preferred_element_type=F32) * scale
        mask = k_pos[None, :] <= qp[:, None]
        s = jnp.where(mask, s, -jnp.inf)
        p = jax.nn.softmax(s, axis=-1)
        a = p[:, :, :, 0] - lam * p[:, :, :, 1]
        return jnp.einsum("bkgqs,bskd->bqkgd", a.astype(v.dtype), v)

    out = lax.map(one_block, (q_blocks, pos_blocks))
    out = out.transpose(1, 0, 2, 3, 4, 5).reshape(b, nb * qb, N_HEADS, ATTN_VHEAD)
    return out[:, :lq]


def causal_conv(xbc, prev, w, bias):
    xp = jnp.concatenate([prev.astype(xbc.dtype), xbc], axis=1)
    out = lax.conv_general_dilated(xp, w[:, None, :].astype(xbc.dtype), window_strides=(1,),
                                   padding="VALID", dimension_numbers=("NWC", "WIO", "NWC"),
                                   feature_group_count=CONV_DIM)
    return out + bias.astype(xbc.dtype), xp[:, -(CONV_WIDTH - 1):]


def ssd_chunked(x, dt, a, bm, cm, state0):
    b, L, h, p = x.shape
    cl = min(SSD_CHUNK, L)
    pad = (-L) % cl
    x = x.astype(F32)
    bm = jnp.repeat(bm.astype(F32), h // N_SSM_GROUPS, axis=2)
    cm = jnp.repeat(cm.astype(F32), h // N_SSM_GROUPS, axis=2)
    if pad:
        pw = ((0, 0), (0, pad), (0, 0), (0, 0))
        x, bm, cm = jnp.pad(x, pw), jnp.pad(bm, pw), jnp.pad(cm, pw)
        dt = jnp.pad(dt, ((0, 0), (0, pad), (0, 0)))
    nc = (L + pad) // cl
    xdt = (x * dt[..., None]).reshape(b, nc, cl, h, p)
    bm = bm.reshape(b, nc, cl, h, SSM_STATE)
    cm = cm.reshape(b, nc, cl, h, SSM_STATE)
    cs = jnp.cumsum((dt * a).reshape(b, nc, cl, h), axis=2)
    seg = cs[:, :, :, None, :] - cs[:, :, None, :, :]
    tri = jnp.tril(jnp.ones((cl, cl), dtype=bool))[None, None, :, :, None]
    lmat = jnp.exp(jnp.where(tri, seg, -jnp.inf))
    scores = jnp.einsum("bclhn,bcshn->bclsh", cm, bm) * lmat
    y_diag = jnp.einsum("bclsh,bcshp->bclhp", scores, xdt)
    decay_to_end = jnp.exp(cs[:, :, -1:, :] - cs)
    chunk_states = jnp.einsum("bclhn,bclh,bclhp->bchpn", bm, decay_to_end, xdt)
    chunk_decay = jnp.exp(cs[:, :, -1, :])

    def step(s, inp):
        dec, cst = inp
        return dec[:, :, None, None] * s + cst, s

    final, prev = lax.scan(step, state0.astype(F32),
                           (chunk_decay.transpose(1, 0, 2), chunk_states.transpose(1, 0, 2, 3, 4)))
    prev = prev.transpose(1, 0, 2, 3, 4)
    y_off = jnp.einsum("bclhn,bchpn,bclh->bclhp", cm, prev, jnp.exp(cs))
    y = (y_diag + y_off).reshape(b, nc * cl, h, p)[:, :L]
    return y, final


def decoder_layer(x, pos, k_past, v_past, past_pos, conv_prev, ssm_prev,
                  ln1_w, w_in, q_norm_w, k_norm_w, lambda_q1, lambda_k1, lambda_q2, lambda_k2,
                  subln_w, conv_w, conv_b, dt_bias, a_log, d_skip, ssm_norm_w, w_out,
                  ln2_w, w_up, w_down, lam_init):
    b, L, _ = x.shape
    h = rms_norm(x, ln1_w)
    proj = h @ w_in
    idx = [int(i) for i in np.cumsum([D_Q, D_K, D_V, D_SSM, CONV_DIM])]
    q, k, v, z, xbc, dt = jnp.split(proj, idx, axis=-1)

    q = partial_rope(rms_norm(q.reshape(b, L, N_HEADS, 2, ATTN_SUB), q_norm_w), pos)
    k = partial_rope(rms_norm(k.reshape(b, L, N_KV_HEADS, 2, ATTN_SUB), k_norm_w), pos)
    v = v.reshape(b, L, N_KV_HEADS, ATTN_VHEAD)
    k_rows = k.reshape(b, L, N_KV_HEADS, 2 * ATTN_SUB)
    if k_past is None:
        k_all, v_all, k_pos = k, v, pos
    else:
        k_all = jnp.concatenate([k_past.reshape(b, -1, N_KV_HEADS, 2, ATTN_SUB).astype(k.dtype), k], axis=1)
        v_all = jnp.concatenate([v_past.reshape(b, -1, N_KV_HEADS, ATTN_VHEAD).astype(v.dtype), v], axis=1)
        k_pos = jnp.concatenate([past_pos, pos])
    lam = (jnp.exp(jnp.sum(lambda_q1.astype(F32) * lambda_k1.astype(F32)))
           - jnp.exp(jnp.sum(lambda_q2.astype(F32) * lambda_k2.astype(F32))) + lam_init)
    o = diff_attention(q, k_all, v_all, pos, k_pos, lam)
    o = (rms_norm(o, subln_w) * (1.0 - lam_init)).reshape(b, L, D_ATTN)

    xbc, conv_new = causal_conv(xbc, conv_prev, conv_w, conv_b)
    xbc = jax.nn.silu(xbc)
    xs, bm, cm = jnp.split(xbc, [D_SSM, D_SSM + N_SSM_GROUPS * SSM_STATE], axis=-1)
    xs = xs.reshape(b, L, N_SSM_HEADS, SSM_HEAD_DIM)
    bm = bm.reshape(b, L, N_SSM_GROUPS, SSM_STATE)
    cm = cm.reshape(b, L, N_SSM_GROUPS, SSM_STATE)
    dtp = jax.nn.softplus(dt.astype(F32) + dt_bias.astype(F32))
    a = -jnp.exp(a_log.astype(F32))
    y, ssm_new = ssd_chunked(xs, dtp, a, bm, cm, ssm_prev)
    y = y + d_skip.astype(F32)[:, None] * xs.astype(F32)
    y = y.reshape(b, L, D_SSM) * jax.nn.silu(z.astype(F32))
    gs = D_SSM // N_SSM_GROUPS
    y = rms_norm(y.reshape(b, L, N_SSM_GROUPS, gs), ssm_norm_w.reshape(N_SSM_GROUPS, gs)).reshape(b, L, D_SSM)

    x = x + jnp.concatenate([o, y.astype(o.dtype)], axis=-1) @ w_out
    x = x + jnp.square(jax.nn.relu(rms_norm(x, ln2_w) @ w_up)) @ w_down
    return x, k_rows, v, ssm_new.astype(ssm_prev.dtype), conv_new


def setup_inputs(seed: int = 0) -> dict:
    key = jax.random.key(seed)
    ks = jax.random.split(key, 32)
    n_pages = PAST_LEN // PAGE_SIZE
    n_used = DEC_BATCH * n_pages
    n_pool = n_used + max(1, n_used // 4)
    nrm = lambda k, shape, s: jax.random.normal(k, shape, F32) * s
    page_table = jax.random.permutation(ks[0], n_pool)[:n_used].reshape(DEC_BATCH, n_pages).astype(jnp.int32)
    dt0 = jnp.exp(jax.random.uniform(ks[1], (DEPTH, N_SSM_HEADS), F32, math.log(1e-3), math.log(1e-1)))
    return {
        "x_prompt": nrm(ks[2], (BATCH, SEQ, D_MODEL), 1.0),
        "x_sample": nrm(ks[3], (DEC_BATCH, DEC_SEQ, D_MODEL), 1.0),
        "cache_k": nrm(ks[4], (DEPTH, n_pool, PAGE_SIZE, N_KV_HEADS, 2 * ATTN_SUB), 1.0),
        "cache_v": nrm(ks[5], (DEPTH, n_pool, PAGE_SIZE, N_KV_HEADS, ATTN_VHEAD), 1.0),
        "state_ssm": nrm(ks[6], (DEPTH, DEC_BATCH, N_SSM_HEADS, SSM_HEAD_DIM, SSM_STATE), 0.1),
        "state_conv": nrm(ks[7], (DEPTH, DEC_BATCH, CONV_WIDTH - 1, CONV_DIM), 1.0),
        "page_table": page_table,
        "ln1_w": 1.0 + nrm(ks[8], (DEPTH, D_MODEL), 0.02),
        "w_in": nrm(ks[9], (DEPTH, D_MODEL, D_IN), D_MODEL ** -0.5),
        "q_norm_w": 1.0 + nrm(ks[10], (DEPTH, ATTN_SUB), 0.02),
        "k_norm_w": 1.0 + nrm(ks[11], (DEPTH, ATTN_SUB), 0.02),
        "lambda_q1": nrm(ks[12], (DEPTH, ATTN_SUB), 0.1),
        "lambda_k1": nrm(ks[13], (DEPTH, ATTN_SUB), 0.1),
        "lambda_q2": nrm(ks[14], (DEPTH, ATTN_SUB), 0.1),
        "lambda_k2": nrm(ks[15], (DEPTH, ATTN_SUB), 0.1),
        "subln_w": 1.0 + nrm(ks[16], (DEPTH, ATTN_VHEAD), 0.02),
        "conv_w": nrm(ks[17], (DEPTH, CONV_WIDTH, CONV_DIM), CONV_WIDTH ** -0.5),
        "conv_b": nrm(ks[18], (DEPTH, CONV_DIM), 0.02),
        "dt_bias": dt0 + jnp.log(-jnp.expm1(-dt0)),
        "a_log": jnp.log(jax.random.uniform(ks[19], (DEPTH, N_SSM_HEADS), F32, 1.0, 16.0)),
        "d_skip": 1.0 + nrm(ks[20], (DEPTH, N_SSM_HEADS), 0.1),
        "ssm_norm_w": 1.0 + nrm(ks[21], (DEPTH, D_SSM), 0.02),
        "w_out": nrm(ks[22], (DEPTH, D_MIX, D_MODEL), D_MIX ** -0.5),
        "ln2_w": 1.0 + nrm(ks[23], (DEPTH, D_MODEL), 0.02),
        "w_up": nrm(ks[24], (DEPTH, D_MODEL, D_FF), D_MODEL ** -0.5),
        "w_down": nrm(ks[25], (DEPTH, D_FF, D_MODEL), D_FF ** -0.5),
    }


def reference(x_prompt, x_sample, cache_k, cache_v, state_ssm, state_conv, page_table,
              ln1_w, w_in, q_norm_w, k_norm_w, lambda_q1, lambda_k1, lambda_q2, lambda_k2,
              subln_w, conv_w, conv_b, dt_bias, a_log, d_skip, ssm_norm_w, w_out,
              ln2_w, w_up, w_down):
    past_len = page_table.shape[1] * cache_k.shape[2]
    bp, seq = x_prompt.shape[0], x_prompt.shape[1]
    dseq = x_sample.shape[1]
    pos_p = jnp.arange(seq, dtype=jnp.int32)
    pos_s = past_len + jnp.arange(dseq, dtype=jnp.int32)
    past_pos = jnp.arange(past_len, dtype=jnp.int32)
    yp, ys = x_prompt, x_sample
    kp_l, vp_l, sp_l, cp_l, ks_l, vs_l, ss_l, cs_l = [], [], [], [], [], [], [], []
    for l in range(DEPTH):
        lam_init = 0.8 - 0.6 * math.exp(-0.3 * l)
        lw = (ln1_w[l], w_in[l], q_norm_w[l], k_norm_w[l], lambda_q1[l], lambda_k1[l],
              lambda_q2[l], lambda_k2[l], subln_w[l], conv_w[l], conv_b[l], dt_bias[l],
              a_log[l], d_skip[l], ssm_norm_w[l], w_out[l], ln2_w[l], w_up[l], w_down[l])
        conv0 = jnp.zeros((bp, CONV_WIDTH - 1, CONV_DIM), x_prompt.dtype)
        ssm0 = jnp.zeros((bp, N_SSM_HEADS, SSM_HEAD_DIM, SSM_STATE), state_ssm.dtype)
        yp, kp, vp, sp, cp = decoder_layer(yp, pos_p, None, None, None, conv0, ssm0, *lw, lam_init)
        k_past = cache_k[l][page_table]
        v_past = cache_v[l][page_table]
        ys, ks_, vs_, ss_, cs_ = decoder_layer(ys, pos_s, k_past, v_past, past_pos,
                                               state_conv[l], state_ssm[l], *lw, lam_init)
        kp_l.append(kp); vp_l.append(vp); sp_l.append(sp); cp_l.append(cp)
        ks_l.append(ks_); vs_l.append(vs_); ss_l.append(ss_); cs_l.append(cs_)
    k_prompt, v_prompt = jnp.stack(kp_l), jnp.stack(vp_l)
    ssm_prompt, conv_prompt = jnp.stack(sp_l), jnp.stack(cp_l)
    k_sample, v_sample = jnp.stack(ks_l), jnp.stack(vs_l)
    ssm_sample, conv_sample = jnp.stack(ss_l), jnp.stack(cs_l)
    return (yp, ys, k_prompt, v_prompt, ssm_prompt, conv_prompt, k_sample, v_sample, ssm_sample, conv_sample)
```

```python
import math, contextlib
import numpy as np
import concourse.bass as bass
import concourse.mybir as mybir
from concourse.bass_utils import run_bass_kernel_spmd

F32 = mybir.dt.float32; BF16 = mybir.dt.bfloat16; I32 = mybir.dt.int32
AF = mybir.ActivationFunctionType; ALU = mybir.AluOpType

D = 4096; DIN = 9248; DFF = 16384; NCH = 32
TP = 256; TS = 8; NPASS = 4; NB2 = 1; NQT = TP // 128
EPS = 1e-6
LAM_INIT = 0.8 - 0.6 * math.exp(0.0)
PAST = 8192; NPG = 64
STAGE = 99; NP_CTX = 4; NP_OWN = 4; NPG_RUN = 64; GMODE = 0; SA_CUT = 0; SA_SKIP = 0
NEG = -30000.0


class Res:
    __slots__ = ("w", "r")
    def __init__(self):
        self.w = None; self.r = {}


class Sched:
    ENG = ("pe", "act", "dve", "pool", "sp")
    def __init__(self, nc, es, dry=False):
        self.nc = nc; self.es = es; self.dry = dry
        self.q = {e: [] for e in self.ENG}
        self.sem = {e: (None if dry else es.enter_context(nc.semaphore("sem_" + e))) for e in self.ENG}
        if dry:
            self.sem = {e: ("dry", e) for e in self.ENG}
        self.cnt = {e: 0 for e in self.ENG}
        self.seen = {e: {} for e in self.ENG}
        self.dsem = {}
        self.out_toks = []
    def _deps(self, eng, reads, writes):
        need = {}
        def add(t):
            if t is None: return
            k = id(t[0])
            if k not in need or need[k][1] < t[1]: need[k] = t
        for r in reads: add(r.w)
        for w in writes:
            add(w.w)
            for t in w.r.values(): add(t)
        own = self.sem[eng]
        for k, (s, v) in need.items():
            if eng == "pe" and s is own: continue
            if self.seen[eng].get(k, 0) >= v: continue
            self.seen[eng][k] = v
            self.q[eng].append(("wait", s, v))
    def _upd(self, tok, reads, writes):
        k = id(tok[0])
        for r in reads:
            if k not in r.r or r.r[k][1] < tok[1]: r.r[k] = tok
        for w in writes:
            w.w = tok; w.r = {}
    def op(self, eng, name, *a, R=(), W=(), inc=True, **kw):
        self._deps(eng, R, W)
        if inc:
            self.cnt[eng] += 1
            tok = (self.sem[eng], self.cnt[eng])
            self.q[eng].append(("op_inc", name, a, kw))
        else:
            tok = (self.sem[eng], self.cnt[eng] + 1)
            self.q[eng].append(("op", name, a, kw))
        self._upd(tok, R, W)
        return tok
    def dma(self, eng, semname, out, in_, R=(), W=(), ind=None):
        nsl = 1 if semname.startswith("w") else 3
        if semname not in self.dsem:
            self.dsem[semname] = [[[("dry", semname, i) if self.dry else self.es.enter_context(self.nc.semaphore("d_%s%d" % (semname, i))), 0] for i in range(nsl)], 0]
        pool = self.dsem[semname]
        ent = pool[0][pool[1] % nsl]; pool[1] += 1
        if ent[1] > 0: self.wait_tok(eng, (ent[0], ent[1]))
        self._deps(eng, R, W)
        ent[1] += 16
        tok = (ent[0], ent[1])
        self.q[eng].append(("dma", out, in_, ent[0], ind))
        self._upd(tok, R, W)
        return tok
    def wait_tok(self, eng, tok):
        s, v = tok
        if self.seen[eng].get(id(s), 0) >= v: return
        self.seen[eng][id(s)] = v
        self.q[eng].append(("wait", s, v))
    def build(self):
        nc = self.nc
        block = self.es.enter_context(nc.Block())
        sem = self.sem
        def run(e, ename, lst):
            for it in lst:
                k = it[0]
                if k == "wait": e.wait_ge(it[1], it[2])
                elif k == "op_inc": getattr(e, it[1])(*it[2], **it[3]).then_inc(sem[ename], 1)
                elif k == "op": getattr(e, it[1])(*it[2], **it[3])
                elif k == "dma":
                    if it[4] is None:
                        e.dma_start(out=it[1], in_=it[2]).then_inc(it[3], 16)
                    else:
                        e.indirect_dma_start(out=it[1], out_offset=None, in_=it[2],
                                             in_offset=bass.IndirectOffsetOnAxis(ap=it[4], axis=0)).then_inc(it[3], 16)
        q = self.q
        @block.tensor
        def _(e): run(e, "pe", q["pe"])
        @block.scalar
        def _(e): run(e, "act", q["act"])
        @block.vector
        def _(e): run(e, "dve", q["dve"])
        @block.gpsimd
        def _(e): run(e, "pool", q["pool"])
        @block.sync
        def _(e): run(e, "sp", q["sp"])


class Rot:
    def __init__(self, es, nc, name, shape, dtype, n):
        self.t = [es.enter_context(nc.sbuf_tensor(f"{name}{i}", shape, dtype)) for i in range(n)]
        self.r = [Res() for _ in range(n)]
        self.i = 0
    def get(self):
        i = self.i % len(self.t); self.i += 1
        return self.t[i], self.r[i]


class WStream:
    NB = 6
    def __init__(self, es, nc):
        self.buf = [es.enter_context(nc.sbuf_tensor(f"wbuf{i}", [128, 32, 128], BF16)) for i in range(self.NB)]
        self.plan = []; self.planned = False
    def start(self, S):
        self.S = S; self.res = [Res() for _ in range(self.NB)]
        self.ci = 0; self.ii = 0
    def _issue(self):
        if self.ii >= len(self.plan): return
        src = self.plan[self.ii]
        b = self.ii % self.NB
        self.S.dma("pool", f"w{b}", self.buf[b][:, :, 0:src.shape[2]], src, W=[self.res[b]])
        self.ii += 1
    def get(self, src):
        if not self.planned:
            self.plan.append(src)
            b = len(self.plan) % self.NB
            return self.buf[b], self.res[b]
        while self.ii < min(self.ci + self.NB - 1, len(self.plan)):
            self._issue()
        if self.ii <= self.ci: self._issue()
        b = self.ci % self.NB; self.ci += 1
        return self.buf[b], self.res[b]


def build_program():
    nc = bass.Bass("TRN2", target_bir_lowering=False)
    def din(name, shape, dt=F32): return nc.dram_tensor(name, list(shape), dt, kind="ExternalInput")
    def dout(name, shape, dt=F32): return nc.dram_tensor(name, list(shape), dt, kind="ExternalOutput")
    xT_own = din("xT_own", [D, 1024]); xT_ctx = din("xT_ctx", [D, 1024]); xT_smp = din("xT_smp", [D, 32])
    ckT = din("ckT", [2560 * 128, 512]); cv = din("cv", [2560 * 128, 512])
    st_ssm = din("st_ssm", [4, 128, 2048]); st_conv = din("st_conv", [128, 4, 32, 3])
    ptab = din("ptab", [1, 256], I32)
    w_in = din("w_in", [73, 128, 32, 128]); w_out = din("w_out", [32, 128, 32, 128])
    w_up = din("w_up", [128, 128, 32, 128]); w_down = din("w_down", [32, 4, 128, 32, 128])
    cpk = din("cpk", [128, 320])
    cmat = din("cmat", [128, 5, 128])
    ropes = din("ropes", [128, 2, 2, 1024 + 8])
    rowp = din("rowp", [1, 128 + 32 + 32 + 256])
    tri8 = din("tri8", [8, 256])
    yT_own = dout("yT_own", [D, 1024]); yT_smp = dout("yT_smp", [D, 32])
    kT_own = dout("kT_own", [512, 1024]); vT_own = dout("vT_own", [512, 1024])
    kT_smp = dout("kT_smp", [512, 32]); vT_smp = dout("vT_smp", [512, 32])
    ssm_fin = dout("ssm_fin", [128, 2048]); conv_fin = dout("conv_fin", [128, 32, 3])
    ssm_smp = dout("ssm_smp", [4, 128, 2048]); conv_smp = dout("conv_smp", [128, 4, 32, 3])
    dbg_mix = dout("dbg_mix", [4, 128, 32, TP + TS], BF16) if STAGE == 5 else None
    if STAGE == 5:
        dbg_dtk = dout("dbg_dtk", [128, NQT + NB2, 32]); dbg_xs = dout("dbg_xs", [128, 2, TP + TS], BF16)
        dbg_B = dout("dbg_B", [128, TP + TS], BF16); dbg_C = dout("dbg_C", [128, TP + TS], BF16)

    es = contextlib.ExitStack()
    with es:
        def sb(name, shape, dt=F32): return es.enter_context(nc.sbuf_tensor(name, list(shape), dt))
        T = TP + TS
        xn = sb("xn", [128, NCH, T], BF16); r_xn = [Res() for _ in range(NCH)]
        mix = sb("mix", [128, NCH, T], BF16); r_mix = [Res() for _ in range(NCH)]
        KT = sb("KT", [128, 4, 2048], BF16); r_KT = [Res() for _ in range(4)]
        VA = sb("VA", [128, 16, 4, 130], BF16); r_VA = [Res() for _ in range(4)]
        ST = sb("ST", [128, 2048]); r_ST = [Res() for _ in range(8)]
        halo = sb("halo", [128, 32, 3]); r_halo = [Res() for _ in range(32)]
        convo = sb("convo", [128, 4, 32, 3]); r_convo = Res()
        cp = sb("cp", [128, 320]); r_c = Res()
        cm = sb("cm", [128, 5, 128]); cmb = sb("cmb", [128, 2, 128], BF16)
        rp = sb("rp", [128, 128 + 32 + 32 + 256])
        t8 = sb("t8", [8, 256], BF16); t8f = sb("t8f", [8, 256])
        rope = sb("rope", [128, 2, T]); r_rope = Res()
        rstd = sb("rstd", [128, T]); r_rstd = Res()
        rstd2 = sb("rstd2", [128, T]); r_rstd2 = Res()
        qT = sb("qT", [128, 4, TP], BF16); r_qT = [Res() for _ in range(4)]
        qsm = sb("qsm", [128, NB2, 16, 8], BF16); r_qsm = Res()
        ksm = sb("ksm", [128, 4, TS], BF16); r_ksm = Res()
        vsm = [sb(f"vsm{i}", [8, 4, 130], BF16) for i in range(NB2)]; r_vsm = Res()
        zT = sb("zT", [128, 2, T], BF16); r_zT = [Res(), Res()]
        xsT = sb("xsT", [128, 2, T], BF16); r_xsT = [Res(), Res()]
        BT = sb("BT", [128, T], BF16); r_BT = Res()
        CTt = sb("CTt", [128, T], BF16); r_CT = Res()
        dtT = sb("dtT", [32, T]); r_dtT = Res()
        dtk = sb("dtk", [128, NQT + NB2, 32]); r_dtk = Res()
        lamt = sb("lamt", [128, 4]); r_lam = Res()
        idx = sb("idx", [128, 256], I32); idxf = sb("idxf", [128, 256]); r_idx = Res()
        smacc = sb("smacc", [128, NCH, TS]); r_smacc = Res()
        ones1 = sb("ones1", [128, 1])
        ws = WStream(es, nc)
        tmpA = Rot(es, nc, "tmpA", [128, T], F32, 3)
        tmpB = Rot(es, nc, "tmpB", [128, T], F32, 3)
        tmpC = Rot(es, nc, "tmpC", [128, T], F32, 2)
        xpb = Rot(es, nc, "xpb", [128, 3 + TP + NB2 * 11], F32, 2)
        pT = Rot(es, nc, "pT", [128, 2, TP], BF16, 2)
        sm1 = Rot(es, nc, "sm1", [128, 512], F32, 2)
        sm2 = Rot(es, nc, "sm2", [128, 512], F32, 2)
        smb = Rot(es, nc, "smb", [128, 512], BF16, 4)
        col = Rot(es, nc, "col", [128, 64], F32, 6)
        kpg = Rot(es, nc, "kpg", [128, 512], F32, 2); vpg = Rot(es, nc, "vpg", [128, 512], F32, 2)
        kpb = Rot(es, nc, "kpb", [128, 512], BF16, 2); vpb = Rot(es, nc, "vpb", [128, 4, 130], BF16, 2)
        stg = Rot(es, nc, "stg", [128, 256], F32, 2)
        q_xs = Rot(es, nc, "q_xs", [128, 256], F32, 2); q_Bt = Rot(es, nc, "q_Bt", [128, 128], BF16, 2)
        q_xdt = Rot(es, nc, "q_xdt", [128, 256], BF16, 2); q_xdte = Rot(es, nc, "q_xdte", [128, 256], BF16, 2)
        q_Sb = Rot(es, nc, "q_Sb", [128, 256], BF16, 1); q_y = Rot(es, nc, "q_y", [128, 256], F32, 1)
        q_Dm = Rot(es, nc, "q_Dm", [128, 512], F32, 1); q_Lm = Rot(es, nc, "q_Lm", [128, 512], F32, 1)
        q_Wm = Rot(es, nc, "q_Wm", [128, 512], BF16, 1); q_zz = Rot(es, nc, "q_zz", [128, 256], F32, 1)
        q_yb = Rot(es, nc, "q_yb", [128, 256], BF16, 1)
        PSD = [es.enter_context(nc.psum_tensor(f"psd{i}", [128, 2, 512], F32)) for i in range(4)]
        r_PSD = [[Res(), Res()] for _ in range(4)]
        cnt = {"dn": 0, "sc": 0}
        def dn():
            i = cnt["dn"] % 2; cnt["dn"] += 1
            return PSD[i], r_PSD[i]
        def sc():
            i = cnt["sc"] % 4; cnt["sc"] += 1
            return PSD[2 + i // 2][:, i % 2, :], r_PSD[2 + i // 2][i % 2]

        ident = cm[:, 0, :]; tri = cm[:, 1, :]; bones = cm[:, 2, :]; Rm = cm[:, 3, :]; negm = cm[:, 4, :]
        identb = cmb[:, 0, :]; trib = cmb[:, 1, :]
        C_LN1 = 0; C_LN2 = 32; C_CW = 64; C_CB = 192; C_QW = 224; C_KW = 225; C_DTB = 226; C_FLAG = 227; C_CBIAS = 228; C_ONE = 229
        OFF_SUB = 0; OFF_AL = 128; OFF_DS = 160; OFF_LAM = 192; C_SN = 240
        all_ones = sb("all_ones", [128, 128])

        def emit(S):
            ws.start(S)
            I = S.op
            S.dma("sp", "c0", cp[:], cpk.ap(), W=[r_c]); S.dma("sp", "c0", cm[:], cmat.ap(), W=[r_c])
            S.dma("sp", "c0", rp[:], rowp.ap().partition_broadcast(128), W=[r_c])
            S.dma("sp", "c0", t8f[:], tri8.ap(), W=[r_c])
            S.dma("sp", "c0", idx[:], ptab.ap().partition_broadcast(128), W=[r_idx])
            S.dma("sp", "c0", convo[:], st_conv.ap(), W=[r_convo])
            I("dve", "tensor_copy", cmb[:, 0, :], cm[:, 0, :], R=[r_c], W=[r_c])
            I("dve", "tensor_copy", cmb[:, 1, :], cm[:, 1, :], R=[r_c], W=[r_c])
            I("dve", "tensor_copy", t8[:], t8f[:], R=[r_c], W=[r_c])
            I("dve", "memset", all_ones[:], 1.0, W=[r_c])
            I("dve", "memset", ST[:], 0.0, W=r_ST)
            I("dve", "memset", halo[:], 0.0, W=r_halo)
            I("dve", "memset", VA[:], 1.0, W=r_VA)
            I("dve", "memset", KT[:], 0.0, W=r_KT)
            for v in vsm: I("dve", "memset", v[:], 1.0, W=[r_vsm])
            for rr in (vpb.t): pass
            for i in range(2): I("dve", "memset", vpb.t[i][:], 1.0, W=[vpb.r[i]])
            I("act", "activation", rp[:, OFF_AL:OFF_AL + 32], rp[:, OFF_AL:OFF_AL + 32], AF.Exp, R=[r_c], W=[r_c])
            I("dve", "tensor_scalar", rp[:, OFF_AL:OFF_AL + 32], rp[:, OFF_AL:OFF_AL + 32], -1.0, None, ALU.mult, R=[r_c], W=[r_c])
            L0 = OFF_LAM
            I("dve", "tensor_tensor", lamt[:, 0:1].to_broadcast([128, 1]) if False else rp[:, L0:L0 + 64], rp[:, L0:L0 + 64], rp[:, L0 + 64:L0 + 128], ALU.mult, R=[r_c], W=[r_c])
            I("dve", "tensor_tensor", rp[:, L0 + 128:L0 + 192], rp[:, L0 + 128:L0 + 192], rp[:, L0 + 192:L0 + 256], ALU.mult, R=[r_c], W=[r_c])
            I("dve", "reduce_sum", lamt[:, 1:2], rp[:, L0:L0 + 64], mybir.AxisListType.X, R=[r_c], W=[r_lam])
            I("dve", "reduce_sum", lamt[:, 2:3], rp[:, L0 + 128:L0 + 192], mybir.AxisListType.X, R=[r_c], W=[r_lam])
            I("act", "activation", lamt[:, 1:3], lamt[:, 1:3], AF.Exp, R=[r_lam], W=[r_lam])
            I("dve", "tensor_tensor", lamt[:, 0:1], lamt[:, 2:3], lamt[:, 1:2], ALU.subtract, R=[r_lam], W=[r_lam])
            I("dve", "tensor_scalar", lamt[:, 0:1], lamt[:, 0:1], -LAM_INIT, None, ALU.add, R=[r_lam], W=[r_lam])
            I("dve", "tensor_copy", idxf[:], idx[:], R=[r_idx], W=[r_idx])
            I("dve", "tensor_scalar", idxf[:], idxf[:], 128.0, cp[:, C_ONE + 1:C_ONE + 2], ALU.mult, ALU.add, R=[r_idx, r_c], W=[r_idx])
            I("dve", "tensor_copy", idx[:], idxf[:], R=[r_idx], W=[r_idx])
            I("dve", "memset", smacc[:], 0.0, W=[r_smacc])

            def load_x(src_list, Tn, lnc):
                pss, rps = sc()
                pss2, rps2 = sc()
                for c in range(NCH):
                    xs_, rx = tmpA.get()
                    for (src, c0, n) in src_list:
                        S.dma("sp", "xld", xs_[:, c0:c0 + n], src[c * 128:(c + 1) * 128, :], W=[rx])
                    I("dve", "tensor_scalar", xn[:, c, 0:Tn], xs_[:, 0:Tn], cp[:, lnc + c:lnc + c + 1], None, ALU.mult, R=[rx, r_c], W=[r_xn[c]])
                    sq, rq = tmpB.get()
                    I("act", "activation", sq[:, 0:Tn], xs_[:, 0:Tn], AF.Square, R=[rx], W=[rq])
                    I("pe", "matmul", pss[:, 0:min(Tn, 512)], all_ones[:], sq[:, 0:min(Tn, 512)], start=(c == 0), stop=(c == NCH - 1), R=[rq, r_c], W=[rps], inc=(Tn <= 512))
                    if Tn > 512:
                        I("pe", "matmul", pss2[:, 0:Tn - 512], all_ones[:], sq[:, 512:Tn], start=(c == 0), stop=(c == NCH - 1), R=[rq, r_c], W=[rps2])
                fin_rstd(rstd, r_rstd, pss, rps, pss2, rps2, Tn)

            def fin_rstd(dst, rdst, pss, rps, pss2, rps2, Tn):
                I("act", "activation", dst[:, 0:min(Tn, 512)], pss[:, 0:min(Tn, 512)], AF.Sqrt, bias=cp[:, C_ONE + 2:C_ONE + 3], scale=1.0 / D, R=[rps, r_c], W=[rdst])
                if Tn > 512:
                    I("act", "activation", dst[:, 512:Tn], pss2[:, 0:Tn - 512], AF.Sqrt, bias=cp[:, C_ONE + 2:C_ONE + 3], scale=1.0 / D, R=[rps2, r_c], W=[rdst])
                I("dve", "reciprocal", dst[:, 0:Tn], dst[:, 0:Tn], R=[rdst], W=[rdst])

            def dense(srcs, rhsbuf, r_rhs, Tn, M=128):
                ps, rps = dn()
                nb = len(srcs)
                for bi, src in enumerate(srcs):
                    wb, wr = ws.get(src)
                    for k in range(32):
                        kc = bi * 32 + k
                        st = (kc == 0); sp = (kc == nb * 32 - 1)
                        I("pe", "matmul", ps[0:M, 0, 0:min(Tn, 512)], wb[:, k, 0:M], rhsbuf[:, kc, 0:min(Tn, 512)], start=st, stop=sp,
                          R=[wr, r_rhs[kc]], W=[rps[0]], inc=(sp and Tn <= 512))
                        if Tn > 512:
                            I("pe", "matmul", ps[0:M, 1, 0:Tn - 512], wb[:, k, 0:M], rhsbuf[:, kc, 512:Tn], start=st, stop=sp,
                              R=[wr, r_rhs[kc]], W=[rps[1]], inc=sp)
                return ps, rps

            def halves(Tn):
                hs = [(0, 0, min(Tn, 512))]
                if Tn > 512: hs.append((1, 512, Tn - 512))
                return hs

            def scaled(ps, rps, Tn, M=128):
                a, ra = tmpA.get()
                for (h, c0, n) in halves(Tn):
                    I("dve", "tensor_tensor", a[0:M, c0:c0 + n], ps[0:M, h, 0:n], rstd[0:M, c0:c0 + n], ALU.mult, R=[rps[h], r_rstd], W=[ra])
                return a, ra

            def norm_rope(a, ra, Tn, wc):
                o, ro = tmpC.get()
                for (h, c0, n) in halves(Tn):
                    sq, rq = tmpB.get()
                    I("act", "activation", sq[:, 0:n], a[:, c0:c0 + n], AF.Square, R=[ra], W=[rq])
                    p1, rp1 = sc()
                    I("pe", "matmul", p1[:, 0:n], bones, sq[:, 0:n], start=True, stop=True, R=[rq, r_c], W=[rp1])
                    I("act", "activation", sq[:, 0:n], p1[:, 0:n], AF.Sqrt, bias=cp[:, C_ONE + 2:C_ONE + 3], scale=1.0 / 64, R=[rp1, r_c], W=[rq])
                    I("dve", "reciprocal", sq[:, 0:n], sq[:, 0:n], R=[rq], W=[rq])
                    I("dve", "scalar_tensor_tensor", sq[:, 0:n], a[:, c0:c0 + n], cp[:, wc:wc + 1], sq[:, 0:n], ALU.mult, ALU.mult, R=[ra, rq, r_c], W=[rq])
                    p2, rp2 = sc()
                    I("pe", "matmul", p2[:, 0:n], Rm, sq[:, 0:n], start=True, stop=True, R=[rq, r_c], W=[rp2])
                    I("dve", "tensor_tensor", o[:, c0:c0 + n], p2[:, 0:n], rope[:, 1, c0:c0 + n], ALU.mult, R=[rp2, r_rope], W=[ro])
                    I("dve", "tensor_tensor", sq[:, 0:n], sq[:, 0:n], rope[:, 0, c0:c0 + n], ALU.mult, R=[rq, r_rope], W=[rq])
                    I("dve", "tensor_tensor", o[:, c0:c0 + n], o[:, c0:c0 + n], sq[:, 0:n], ALU.add, R=[rq, ro], W=[ro])
                return o, ro

            def transpose_to(dst_ap, src_ap, rsrc, rdst, npart_in, nfree_in, bf=False, usedn=False, scale=None):
                if usedn:
                    pp_, rr_ = dn(); p = pp_[:, 0, :]; rpp = rr_[0]
                else:
                    p, rpp = sc()
                kw = {} if scale is None else {"scale": scale}
                if bf:
                    pv = p.bitcast(BF16)
                    I("pe", "transpose", pv[0:nfree_in, 0:npart_in], src_ap, identb[0:npart_in, 0:npart_in], R=[rsrc, r_c], W=[rpp])
                    I("act", "activation", dst_ap, pv[0:nfree_in, 0:npart_in], AF.Copy, R=[rpp, r_c], W=[rdst], **kw)
                else:
                    I("pe", "transpose", p[0:nfree_in, 0:npart_in], src_ap, ident[0:npart_in, 0:npart_in], R=[rsrc, r_c], W=[rpp])
                    I("act", "activation", dst_ap, p[0:nfree_in, 0:npart_in], AF.Copy, R=[rpp], W=[rdst])

            def run_pass(kind, p):
                own = (kind == "own")
                Tn = T if own else TP
                xsrc = xT_own if own else xT_ctx
                tok0 = p * TP
                kpos0 = (1024 if own else 0) + tok0
                srcl = [(xsrc[:, tok0:tok0 + TP], 0, TP)]
                if own: srcl.append((xT_smp[:, p * TS:(p + 1) * TS], TP, TS))
                S.dma("sp", "rope", rope[:, :, 0:TP], ropes[:, :, 1 if own else 0, tok0:tok0 + TP], W=[r_rope])
                if own:
                    for b2 in range(NB2):
                        S.dma("sp", "rope", rope[:, :, TP + 8 * b2:TP + 8 * b2 + 8], ropes[:, :, 1, 1024:1032], W=[r_rope])
                load_x(srcl, Tn, C_LN1)

                ps, rps = dense([w_in[72][:, :, 0:32]], xn, r_xn, Tn, M=32)
                a, ra = scaled(ps, rps, Tn, M=32)
                I("act", "activation", a[0:32, 0:Tn], a[0:32, 0:Tn], AF.Exp, bias=cp[0:32, C_DTB:C_DTB + 1], scale=1.0, R=[ra, r_c], W=[ra])
                I("act", "activation", dtT[:, 0:Tn], a[0:32, 0:Tn], AF.Ln, bias=cp[0:32, C_ONE:C_ONE + 1], scale=1.0, R=[ra, r_c], W=[r_dtT])
                nchunk = 4
                segs = [(i * 128, 128) for i in range(NQT)] + ([(TP + 8 * b_, 8) for b_ in range(NB2)] if own else [])
                for si, (c0, L) in enumerate(segs):
                    transpose_to(dtk[0:L, si, :], dtT[:, c0:c0 + L], r_dtT, r_dtk, 32, L)

                if STAGE <= 1: return
                for j in range(4):
                    ps, rps = dense([w_in[16 + j]], xn, r_xn, Tn)
                    a, ra = scaled(ps, rps, Tn)
                    o, ro = norm_rope(a, ra, Tn, C_KW)
                    I("act", "activation", KT[:, j, kpos0:kpos0 + TP], o[:, 0:TP], AF.Copy, R=[ro], W=[r_KT[j]])
                    if own:
                        I("act", "activation", ksm[:, j, :], o[:, TP:T], AF.Copy, R=[ro], W=[r_ksm])
                        S.out_toks.append(S.dma("sp", "ko", kT_own[j * 128:(j + 1) * 128, tok0:tok0 + TP], o[:, 0:TP], R=[ro]))
                        S.out_toks.append(S.dma("sp", "ko", kT_smp[j * 128:(j + 1) * 128, p * TS:(p + 1) * TS], o[:, TP:T], R=[ro]))
                    ps, rps = dense([w_in[20 + j]], xn, r_xn, Tn)
                    a, ra = scaled(ps, rps, Tn)
                    if own:
                        S.out_toks.append(S.dma("sp", "vo", vT_own[j * 128:(j + 1) * 128, tok0:tok0 + TP], a[:, 0:TP], R=[ra]))
                        S.out_toks.append(S.dma("sp", "vo", vT_smp[j * 128:(j + 1) * 128, p * TS:(p + 1) * TS], a[:, TP:T], R=[ra]))
                    for tt in range(NQT):
                        transpose_to(VA[:, kpos0 // 128 + tt, j, 0:128], a[:, tt * 128:(tt + 1) * 128], ra, r_VA[j], 128, 128)
                    if own:
                        for b2 in range(NB2):
                            transpose_to(vsm[b2][:, j, 0:128], a[:, TP + 8 * b2:TP + 8 * b2 + 8], ra, r_vsm, 128, 8)
                    if not own or STAGE <= 2: continue
                    for hl in range(4):
                        hh = 4 * j + hl
                        ps, rps = dense([w_in[hh]], xn, r_xn, Tn)
                        a, ra = scaled(ps, rps, Tn)
                        o, ro = norm_rope(a, ra, Tn, C_QW)
                        I("act", "activation", qT[:, hl, :], o[:, 0:TP], AF.Copy, R=[ro], W=[r_qT[hl]])
                        for b2 in range(NB2): I("act", "activation", qsm[:, b2, hh, :], o[:, TP + 8 * b2:TP + 8 * b2 + 8], AF.Copy, R=[ro], W=[r_qsm])
                    nst = kpos0 // 128 + NQT
                    for hl in range(4):
                        hh = 4 * j + hl
                        Oa = [PSD[2][:, 0, :], PSD[2][:, 1, :], PSD[3][:, 0, :], PSD[3][:, 1, :]]
                        rOa = [r_PSD[2][0], r_PSD[2][1], r_PSD[3][0], r_PSD[3][1]]
                        for st in range(nst):
                            oi = st - kpos0 // 128
                            q0 = max(oi, 0) * 128
                            nq = TP - q0
                            ps, rps = dn()
                            for c in range(2):
                                I("pe", "matmul", ps[:, c, 0:nq], KT[c * 64:(c + 1) * 64, j, st * 128:(st + 1) * 128], qT[c * 64:(c + 1) * 64, hl, q0:TP],
                                  start=True, stop=True, R=[r_KT[j], r_qT[hl]], W=[rps[c]])
                            pt, rpt = pT.get()
                            bias_ap = cp[:, C_CBIAS:C_CBIAS + 1] if st < 8 else cp[:, C_ONE + 3:C_ONE + 4]
                            for c in range(2):
                                I("act", "activation", pt[:, c, 0:nq], ps[:, c, 0:nq], AF.Exp, bias=bias_ap, scale=0.125, R=[rps[c], r_c], W=[rpt])
                            if oi >= 0:
                                for c in range(2):
                                    I("dve", "tensor_tensor", pt[:, c, 0:128], pt[:, c, 0:128], trib, ALU.mult, R=[rpt, r_c], W=[rpt])
                            for qi in range(max(oi, 0), NQT):
                                last = (st == kpos0 // 128 + qi)
                                for c in range(2):
                                    I("pe", "matmul", Oa[qi][:, c * 130:c * 130 + 129], pt[:, c, qi * 128 - q0:(qi + 1) * 128 - q0], VA[:, st, j, 0:129],
                                      start=(st == 0 and c == 0), stop=(last and c == 1), skip_group_check=True, R=[rpt, r_VA[j]], W=[rOa[qi]], inc=(c == 1))
                        for qi in range(NQT):
                            attn_finish(Oa[qi][:, 0:130], Oa[qi][:, 130:260], rOa[qi], 128, mix[:, hh, qi * 128:(qi + 1) * 128], r_mix[hh])

                if own and STAGE >= 4:
                    for b2 in range(NB2):
                        sample_attention(p, b2)
                if STAGE <= 4: return

                for g in range(8):
                    tiles = [("xs", 0, 40 + 2 * g), ("xs", 1, 41 + 2 * g), ("B", 0, 56 + g)]
                    tiles += [("C", 0, 64 + g)]
                    if own: tiles += [("z", 0, 24 + 2 * g), ("z", 1, 25 + 2 * g)]
                    for (kd, il, wt) in tiles:
                        ps, rps = dense([w_in[wt]], xn, r_xn, Tn)
                        if kd == "z":
                            for (h, c0, n) in halves(Tn):
                                I("dve", "tensor_tensor", zT[:, il, c0:c0 + n], ps[:, h, 0:n], rstd[:, c0:c0 + n], ALU.mult, R=[rps[h], r_rstd], W=[r_zT[il]])
                            continue
                        ci = wt - 40
                        xp, rxp = xpb.get()
                        I("dve", "tensor_copy", xp[:, 0:3], halo[:, ci, :], R=[r_halo[ci]], W=[rxp])
                        I("dve", "tensor_tensor", xp[:, 3:3 + TP], ps[:, 0, 0:TP], rstd[:, 0:TP], ALU.mult, R=[rps[0], r_rstd], W=[rxp])
                        I("dve", "tensor_copy", halo[:, ci, :], xp[:, TP:TP + 3], R=[rxp], W=[r_halo[ci]])
                        dst = {"xs": xsT[:, il, :], "B": BT, "C": CTt}[kd]
                        rd = {"xs": r_xsT[il], "B": r_BT, "C": r_CT}[kd]
                        def conv(o0, n, d0):
                            acc, racc = tmpB.get()
                            I("dve", "tensor_scalar", acc[:, 0:n], xp[:, o0:o0 + n], cp[:, C_CW + ci * 4:C_CW + ci * 4 + 1], cp[:, C_CB + ci:C_CB + ci + 1], ALU.mult, ALU.add, R=[rxp, r_c], W=[racc])
                            for k in range(1, 4):
                                I("dve", "scalar_tensor_tensor", acc[:, 0:n], xp[:, o0 + k:o0 + k + n], cp[:, C_CW + ci * 4 + k:C_CW + ci * 4 + k + 1], acc[:, 0:n], ALU.mult, ALU.add, R=[rxp, racc, r_c], W=[racc])
                            I("act", "activation", dst[:, d0:d0 + n], acc[:, 0:n], AF.Silu, R=[racc], W=[rd])
                        conv(0, TP, 0)
                        if own:
                            for b2 in range(NB2):
                                bb = NB2 * p + b2
                                o0 = 3 + TP + 11 * b2
                                I("dve", "tensor_copy", xp[:, o0:o0 + 3], convo[:, bb, ci, :], R=[r_convo], W=[rxp])
                                I("dve", "tensor_tensor", xp[:, o0 + 3:o0 + 11], ps[:, 0, TP + 8 * b2:TP + 8 * b2 + 8], rstd[:, TP + 8 * b2:TP + 8 * b2 + 8], ALU.mult, R=[rps[0], r_rstd], W=[rxp])
                                I("dve", "tensor_copy", convo[:, bb, ci, :], xp[:, o0 + 8:o0 + 11], R=[rxp], W=[r_convo])
                                conv(o0, 8, TP + 8 * b2)
                    ssd_group(kind, p, g, segs)

                if not own: return
                if STAGE == 5:
                    S.out_toks.append(S.dma("sp", "dbg", dbg_mix[p], mix[:, :, :], R=r_mix))
                    S.out_toks.append(S.dma("sp", "dbg", dbg_dtk.ap(), dtk[:, :, :], R=[r_dtk]))
                    S.out_toks.append(S.dma("sp", "dbg", dbg_xs.ap(), xsT[:, :, :], R=r_xsT))
                    S.out_toks.append(S.dma("sp", "dbg", dbg_B.ap(), BT[:, :], R=[r_BT]))
                    S.out_toks.append(S.dma("sp", "dbg", dbg_C.ap(), CTt[:, :], R=[r_CT]))
                    return
                pss, rpsa = sc(); pss2, rpsb = sc()
                for ct in range(32):
                    ps, rps = dense([w_out[ct]], mix, r_mix, T)
                    xr, rxr = tmpA.get()
                    S.dma("sp", "xld", xr[:, 0:TP], xT_own[ct * 128:(ct + 1) * 128, tok0:tok0 + TP], W=[rxr])
                    S.dma("sp", "xld", xr[:, TP:T], xT_smp[ct * 128:(ct + 1) * 128, p * TS:(p + 1) * TS], W=[rxr])
                    for (h, c0, n) in halves(T):
                        I("dve", "tensor_tensor", xr[:, c0:c0 + n], xr[:, c0:c0 + n], ps[:, h, 0:n], ALU.add, R=[rxr, rps[h]], W=[rxr])
                    S.dma("sp", "yb", yT_own[ct * 128:(ct + 1) * 128, tok0:tok0 + TP], xr[:, 0:TP], R=[rxr], W=[r_yb[ct]])
                    I("act", "activation", smacc[:, ct, :], xr[:, TP:T], AF.Copy, R=[rxr], W=[r_smacc])
                    I("dve", "tensor_scalar", xn[:, ct, :], xr[:, :], cp[:, C_LN2 + ct:C_LN2 + ct + 1], None, ALU.mult, R=[rxr, r_c], W=[r_xn[ct]])
                    sq, rq = tmpB.get()
                    I("act", "activation", sq[:, :], xr[:, :], AF.Square, R=[rxr], W=[rq])
                    I("pe", "matmul", pss[:, 0:T], all_ones[:], sq[:, 0:T], start=(ct == 0), stop=(ct == 31), R=[rq, r_c], W=[rpsa])
                fin_rstd(rstd2, r_rstd2, pss, rpsa, pss2, rpsb, T)
                if STAGE <= 6:
                    S.out_toks.append(S.dma("sp", "yso", yT_smp.ap().rearrange("(c k) t -> k c t", k=128)[:, :, p * TS:(p + 1) * TS], smacc[:], R=[r_smacc]))
                    return
                for fc in range(4):
                    for i in range(32):
                        ps, rps = dense([w_up[fc * 32 + i]], xn, r_xn, T)
                        t1, rt1 = tmpA.get()
                        for (h, c0, n) in halves(T):
                            I("dve", "tensor_tensor", t1[:, c0:c0 + n], ps[:, h, 0:n], rstd2[:, c0:c0 + n], ALU.mult, R=[rps[h], r_rstd2], W=[rt1])
                        I("dve", "scalar_tensor_tensor", mix[:, i, :], t1[:, :], 0.0, t1[:, :], ALU.max, ALU.mult, R=[rt1], W=[r_mix[i]])
                    for ct in range(32):
                        ps, rps = dense([w_down[ct, fc]], mix, r_mix, T)
                        yr, ryr = tmpC.get()
                        S.dma("sp", "yld", yr[:, 0:TP], yT_own[ct * 128:(ct + 1) * 128, tok0:tok0 + TP], R=[r_yb[ct]], W=[ryr])
                        I("dve", "tensor_tensor", yr[:, 0:TP], yr[:, 0:TP], ps[:, 0, 0:TP], ALU.add, R=[ryr, rps[0]], W=[ryr])
                        tk = S.dma("sp", "yb", yT_own[ct * 128:(ct + 1) * 128, tok0:tok0 + TP], yr[:, 0:TP], R=[ryr], W=[r_yb[ct]])
                        I("dve", "tensor_tensor", smacc[:, ct, :], smacc[:, ct, :], ps[:, 0, TP:T], ALU.add, R=[rps[0], r_smacc], W=[r_smacc])
                        if fc == 3: S.out_toks.append(tk)
                S.out_toks.append(S.dma("sp", "yso", yT_smp.ap().rearrange("(c k) t -> k c t", k=128)[:, :, p * TS:(p + 1) * TS], smacc[:], R=[r_smacc]))

            r_yb = [Res() for _ in range(32)]

            def attn_finish(O0, O1, rO, M, dstT, rdst):
                cc, rcc = col.get()
                I("dve", "reciprocal", cc[0:M, 0:1], O0[0:M, 128:129], R=[rO], W=[rcc])
                I("dve", "reciprocal", cc[0:M, 1:2], O1[0:M, 128:129], R=[rO], W=[rcc])
                I("dve", "tensor_tensor", cc[0:M, 1:2], cc[0:M, 1:2], lamt[0:M, 0:1], ALU.mult, R=[rcc, r_lam], W=[rcc])
                o, ro = sm1.get()
                I("dve", "tensor_scalar", o[0:M, 0:128], O0[0:M, 0:128], cc[0:M, 0:1], None, ALU.mult, R=[rO, rcc], W=[ro])
                I("dve", "scalar_tensor_tensor", o[0:M, 0:128], O1[0:M, 0:128], cc[0:M, 1:2], o[0:M, 0:128], ALU.mult, ALU.add, R=[rO, rcc, ro], W=[ro])
                jk, rjk = sm2.get()
                I("act", "activation", jk[0:M, 0:128], o[0:M, 0:128], AF.Square, accum_out=cc[0:M, 2:3], R=[ro], W=[rjk, rcc])
                I("act", "activation", cc[0:M, 2:3], cc[0:M, 2:3], AF.Sqrt, bias=cp[0:M, C_ONE + 2:C_ONE + 3], scale=1.0 / 128, R=[rcc, r_c], W=[rcc])
                I("dve", "reciprocal", cc[0:M, 2:3], cc[0:M, 2:3], R=[rcc], W=[rcc])
                I("dve", "tensor_scalar", cc[0:M, 2:3], cc[0:M, 2:3], 1.0 - LAM_INIT, None, ALU.mult, R=[rcc], W=[rcc])
                ob, rob = smb.get()
                I("dve", "scalar_tensor_tensor", ob[0:M, 0:128], o[0:M, 0:128], cc[0:M, 2:3], rp[0:M, OFF_SUB:OFF_SUB + 128], ALU.mult, ALU.mult, R=[ro, rcc, r_c], W=[rob])
                if dstT is not None:
                    transpose_to(dstT, ob[0:M, 0:128], rob, rdst, M, 128, bf=True, usedn=True)
                return ob, rob

            def sample_attention(p, b2):
                bb = NB2 * p + b2
                Oc = [PSD[2], PSD[3]]
                for pg in range(NPG_RUN + 1):
                    new = (pg == NPG_RUN)
                    L = 8 if new else 128
                    if not new:
                        kf, rkf = kpg.get(); vf, rvf = vpg.get()
                        if GMODE == 0:
                            S.dma("pool", "kg", kf[:, :], ckT.ap()[:, :], W=[rkf], R=[r_idx], ind=idx[:, bb * 64 + pg:bb * 64 + pg + 1])
                            S.dma("pool", "vg", vf[:, :], cv.ap()[:, :], W=[rvf], R=[r_idx], ind=idx[:, bb * 64 + pg:bb * 64 + pg + 1])
                        else:
                            S.dma("sp", "kg", kf[:, :], ckT.ap()[pg * 128:(pg + 1) * 128, :], W=[rkf])
                            S.dma("sp", "vg", vf[:, :], cv.ap()[pg * 128:(pg + 1) * 128, :], W=[rvf])
                        kb, rkb = kpb.get(); vb, rvb = vpb.get()
                        I("act", "activation", kb[:, :], kf[:, :], AF.Copy, R=[rkf], W=[rkb])
                        I("dve", "tensor_copy", vb[:, :, 0:128], vf[:, :].rearrange("p (j d) -> p j d", j=4), R=[rvf], W=[rvb])
                    ps, rps = dn()
                    for j in range(4):
                        for c in range(2):
                            lhs = ksm[c * 64:(c + 1) * 64, j, 8 * b2:8 * b2 + 8] if new else kb[c * 64:(c + 1) * 64, j * 128:(j + 1) * 128]
                            I("pe", "matmul", ps[0:L, c, j * 32:(j + 1) * 32], lhs, qsm[c * 64:(c + 1) * 64, b2, 4 * j:4 * j + 4, :].rearrange("p h t -> p (h t)"),
                              start=True, stop=True, R=[(r_ksm if new else rkb), r_qsm], W=[rps[c]], inc=(j == 3))
                    pt, rpt = smb.get()
                    I("act", "activation", pt[0:L, 0:256].rearrange("p (c n) -> p c n", c=2), ps[0:L, :, 0:128], AF.Exp, bias=cp[0:L, C_ONE + 3:C_ONE + 4], scale=0.125, R=[rps[0], rps[1], r_c], W=[rpt])
                    if new and not (SA_SKIP & 4):
                        I("dve", "tensor_tensor", pt[0:8, 0:256], pt[0:8, 0:256], t8[:, :], ALU.mult, R=[rpt, r_c], W=[rpt])
                    for j in range(4):
                        if SA_SKIP & 2: break
                        for c in range(2):
                            rhs = vsm[b2][:, j, 0:129] if new else vb[:, j, 0:129]
                            I("pe", "matmul", Oc[c][0:32, j // 2, (j % 2) * 130:(j % 2) * 130 + 129], pt[0:L, c * 128 + j * 32:c * 128 + (j + 1) * 32], rhs,
                              start=(pg == 0 and j % 2 == 0), stop=(new and j % 2 == 1), skip_group_check=True, R=[rpt, (r_vsm if new else rvb)], W=[r_PSD[2 + c][j // 2]], inc=(j == 3 and c == 1))
                if SA_CUT == 1: return
                for j in range(4):
                    O0 = Oc[0][:, j // 2, (j % 2) * 130:(j % 2) * 130 + 130]; O1 = Oc[1][:, j // 2, (j % 2) * 130:(j % 2) * 130 + 130]
                    rr = Res(); rr.w = None
                    ob, rob = attn_finish2(O0, O1, r_PSD[2][j // 2], r_PSD[3][j // 2])
                    if SA_CUT == 2: continue
                    pp_, rr_ = dn(); pp = pp_[:, 0, :]; rpp = rr_[0]
                    pv = pp.bitcast(BF16)
                    I("pe", "transpose", pv[:, 0:32], ob[0:32, 0:128], identb[0:32, 0:32], R=[rob, r_c], W=[rpp])
                    for hl in range(4):
                        I("act", "activation", mix[:, 4 * j + hl, TP + 8 * b2:TP + 8 * b2 + 8], pv[:, hl * 8:hl * 8 + 8], AF.Copy, R=[rpp], W=[r_mix[4 * j + hl]])

            def attn_finish2(O0, O1, rO0, rO1):
                M = 32
                cc, rcc = col.get()
                I("dve", "reciprocal", cc[0:M, 0:1], O0[0:M, 128:129], R=[rO0], W=[rcc])
                I("dve", "reciprocal", cc[0:M, 1:2], O1[0:M, 128:129], R=[rO1], W=[rcc])
                I("dve", "tensor_tensor", cc[0:M, 1:2], cc[0:M, 1:2], lamt[0:M, 0:1], ALU.mult, R=[rcc, r_lam], W=[rcc])
                o, ro = sm1.get()
                I("dve", "tensor_scalar", o[0:M, 0:128], O0[0:M, 0:128], cc[0:M, 0:1], None, ALU.mult, R=[rO0, rcc], W=[ro])
                I("dve", "scalar_tensor_tensor", o[0:M, 0:128], O1[0:M, 0:128], cc[0:M, 1:2], o[0:M, 0:128], ALU.mult, ALU.add, R=[rO1, rcc, ro], W=[ro])
                jk, rjk = sm2.get()
                I("act", "activation", jk[0:M, 0:128], o[0:M, 0:128], AF.Square, accum_out=cc[0:M, 2:3], R=[ro], W=[rjk, rcc])
                I("act", "activation", cc[0:M, 2:3], cc[0:M, 2:3], AF.Sqrt, bias=cp[0:M, C_ONE + 2:C_ONE + 3], scale=1.0 / 128, R=[rcc, r_c], W=[rcc])
                I("dve", "reciprocal", cc[0:M, 2:3], cc[0:M, 2:3], R=[rcc], W=[rcc])
                I("dve", "tensor_scalar", cc[0:M, 2:3], cc[0:M, 2:3], 1.0 - LAM_INIT, None, ALU.mult, R=[rcc], W=[rcc])
                ob, rob = smb.get()
                I("dve", "scalar_tensor_tensor", ob[0:M, 0:128], o[0:M, 0:128], cc[0:M, 2:3], rp[0:M, OFF_SUB:OFF_SUB + 128], ALU.mult, ALU.mult, R=[ro, rcc, r_c], W=[rob])
                return ob, rob

            def ssd_group(kind, p, g, segs):
                own = (kind == "own")
                A = rp[:, OFF_AL + 4 * g:OFF_AL + 4 * g + 4]
                for si, (c0, L) in enumerate(segs):
                    smp = (si >= NQT)
                    bb = NB2 * p + (si - NQT) if smp else None
                    if smp:
                        stt, rst = stg.get()
                        S.dma("sp", "stl", stt[:, :], st_ssm[bb][:, g * 256:(g + 1) * 256], W=[rst])
                        Sg = stt[:, :]
                    else:
                        Sg = ST[:, g * 256:(g + 1) * 256]; rst = r_ST[g]
                    dt4 = dtk[0:L, si, 4 * g:4 * g + 4]
                    xs, rxs = q_xs.get()
                    for il in range(2):
                        transpose_to(xs[0:L, il * 128:(il + 1) * 128], xsT[:, il, c0:c0 + L], r_xsT[il], rxs, 128, L, bf=True)
                    Bt, rBt = q_Bt.get()
                    transpose_to(Bt[0:L, 0:128], BT[:, c0:c0 + L], r_BT, rBt, 128, L, bf=True)
                    cc, rcc = col.get()
                    I("dve", "tensor_tensor", cc[0:L, 0:4], dt4, A[0:L, :], ALU.mult, R=[r_dtk, r_c], W=[rcc])
                    p1, rp1 = sc()
                    I("pe", "matmul", p1[0:L, 0:4], tri[0:L, 0:L], cc[0:L, 0:4], start=True, stop=True, R=[rcc, r_c], W=[rp1])
                    I("pe", "matmul", p1[:, 4:8], all_ones[0:L, :], cc[0:L, 0:4], start=True, stop=True, R=[rcc, r_c], W=[rp1])
                    I("dve", "tensor_copy", cc[0:L, 4:8], p1[0:L, 0:4], R=[rp1], W=[rcc])
                    I("dve", "tensor_copy", cc[:, 8:12], p1[:, 4:8], R=[rp1], W=[rcc])
                    I("dve", "tensor_tensor", cc[0:L, 12:16], cc[0:L, 8:12], cc[0:L, 4:8], ALU.subtract, R=[rcc], W=[rcc])
                    I("act", "activation", cc[0:L, 12:16], cc[0:L, 12:16], AF.Exp, R=[rcc], W=[rcc])
                    I("act", "activation", cc[:, 16:20], cc[:, 8:12], AF.Exp, R=[rcc], W=[rcc])
                    I("act", "activation", cc[0:L, 20:24], cc[0:L, 4:8], AF.Exp, R=[rcc], W=[rcc])
                    I("dve", "tensor_scalar", cc[0:L, 24:28], cc[0:L, 4:8], -1.0, None, ALU.mult, R=[rcc], W=[rcc])
                    xdt, rxdt = q_xdt.get(); xdte, rxdte = q_xdte.get()
                    for h4 in range(4):
                        I("dve", "tensor_scalar", xdt[0:L, h4 * 64:(h4 + 1) * 64], xs[0:L, h4 * 64:(h4 + 1) * 64], dt4[:, h4:h4 + 1], None, ALU.mult, R=[rxs, r_dtk], W=[rxdt])
                        I("dve", "tensor_scalar", xdte[0:L, h4 * 64:(h4 + 1) * 64], xdt[0:L, h4 * 64:(h4 + 1) * 64], cc[0:L, 12 + h4:13 + h4], None, ALU.mult, R=[rxdt, rcc], W=[rxdte])
                    if own:
                        Sb, rSb = q_Sb.get()
                        I("act", "activation", Sb[:, 0:256], Sg, AF.Copy, R=[rst], W=[rSb])
                        pyo, rpyo = sc()
                        I("pe", "matmul", pyo[0:L, 0:256], CTt[:, c0:c0 + L], Sb[:, 0:256], start=True, stop=True, R=[r_CT, rSb], W=[rpyo])
                        y, ry = q_y.get()
                        for h4 in range(4):
                            I("dve", "tensor_scalar", y[0:L, h4 * 64:(h4 + 1) * 64], pyo[0:L, h4 * 64:(h4 + 1) * 64], cc[0:L, 20 + h4:21 + h4], None, ALU.mult, R=[rpyo, rcc], W=[ry])
                        pcb, rpcb = sc()
                        I("pe", "matmul", pcb[0:L, 0:L], BT[:, c0:c0 + L], CTt[:, c0:c0 + L], start=True, stop=True, R=[r_BT, r_CT], W=[rpcb])
                        Dm, rDm = q_Dm.get()
                        for h4 in range(4):
                            I("dve", "tensor_scalar", Dm[0:L, h4 * 128:h4 * 128 + L], tri[0:L, 0:L], cc[0:L, h4:h4 + 1], None, ALU.mult, R=[rcc, r_c], W=[rDm])
                        pcr, rpcr = sc()
                        for h4 in range(4):
                            I("pe", "matmul", pcr[0:L, h4 * 128:h4 * 128 + L], all_ones[0:L, 0:L], Dm[0:L, h4 * 128:h4 * 128 + L], start=True, stop=True, R=[rDm, r_c], W=[rpcr], inc=(h4 == 3))
                        Lm, rLm = q_Lm.get()
                        Wm, rWm = q_Wm.get()
                        pyd, rpyd = sc()
                        for h4 in range(4):
                            I("dve", "tensor_tensor", Lm[0:L, h4 * 128:h4 * 128 + L], pcr[0:L, h4 * 128:h4 * 128 + L], negm[0:L, 0:L], ALU.add, R=[rpcr, r_c], W=[rLm])
                            I("act", "activation", Lm[0:L, h4 * 128:h4 * 128 + L], Lm[0:L, h4 * 128:h4 * 128 + L], AF.Exp, bias=cc[0:L, 24 + h4:25 + h4], scale=1.0, R=[rLm, rcc], W=[rLm])
                            I("dve", "tensor_tensor", Wm[0:L, h4 * 128:h4 * 128 + L], Lm[0:L, h4 * 128:h4 * 128 + L], pcb[0:L, 0:L], ALU.mult, R=[rLm, rpcb], W=[rWm])
                            I("pe", "matmul", pyd[0:L, h4 * 64:(h4 + 1) * 64], Wm[0:L, h4 * 128:h4 * 128 + L], xdt[0:L, h4 * 64:(h4 + 1) * 64], start=True, stop=True, R=[rWm, rxdt], W=[rpyd])
                        I("dve", "tensor_tensor", y[0:L, 0:256], y[0:L, 0:256], pyd[0:L, 0:256], ALU.add, R=[ry, rpyd], W=[ry])
                        for h4 in range(4):
                            I("dve", "scalar_tensor_tensor", y[0:L, h4 * 64:(h4 + 1) * 64], xs[0:L, h4 * 64:(h4 + 1) * 64], rp[0:L, OFF_DS + 4 * g + h4:OFF_DS + 4 * g + h4 + 1], y[0:L, h4 * 64:(h4 + 1) * 64], ALU.mult, ALU.add, R=[rxs, r_c, ry], W=[ry])
                        zz, rzz = q_zz.get()
                        for il in range(2):
                            pz, rpz = sc()
                            pzv = pz.bitcast(BF16)
                            I("pe", "transpose", pzv[0:L, 0:128], zT[:, il, c0:c0 + L], identb, R=[r_zT[il], r_c], W=[rpz])
                            I("act", "activation", zz[0:L, il * 128:(il + 1) * 128], pzv[0:L, 0:128], AF.Silu, R=[rpz], W=[rzz])
                        I("dve", "tensor_tensor", y[0:L, 0:256], y[0:L, 0:256], zz[0:L, 0:256], ALU.mult, R=[ry, rzz], W=[ry])
                        I("act", "activation", zz[0:L, 0:256], y[0:L, 0:256], AF.Square, accum_out=cc[0:L, 28:29], R=[ry], W=[rzz, rcc])
                        I("act", "activation", cc[0:L, 28:29], cc[0:L, 28:29], AF.Sqrt, bias=cp[0:L, C_ONE + 2:C_ONE + 3], scale=1.0 / 256, R=[rcc, r_c], W=[rcc])
                        I("dve", "reciprocal", cc[0:L, 28:29], cc[0:L, 28:29], R=[rcc], W=[rcc])
                        yb, ryb = q_yb.get()
                        I("dve", "tensor_scalar", yb[0:L, 0:256], y[0:L, 0:256], cc[0:L, 28:29], None, ALU.mult, R=[ry, rcc], W=[ryb])
                        for il in range(2):
                            transpose_to(mix[:, 16 + 2 * g + il, c0:c0 + L], yb[0:L, il * 128:(il + 1) * 128], ryb, r_mix[16 + 2 * g + il], L, 128, bf=True, scale=cp[:, C_SN + 2 * g + il:C_SN + 2 * g + il + 1])
                    pst, rpst = sc()
                    I("pe", "matmul", pst[:, 0:256], Bt[0:L, 0:128], xdte[0:L, 0:256], start=True, stop=True, R=[rBt, rxdte], W=[rpst])
                    for h4 in range(4):
                        I("dve", "scalar_tensor_tensor", Sg[:, h4 * 64:(h4 + 1) * 64], Sg[:, h4 * 64:(h4 + 1) * 64], cc[:, 16 + h4:17 + h4], pst[:, h4 * 64:(h4 + 1) * 64], ALU.mult, ALU.add, R=[rst, rcc, rpst], W=[rst])
                    if smp:
                        S.out_toks.append(S.dma("sp", "sto", ssm_smp[bb][:, g * 256:(g + 1) * 256], Sg, R=[rst]))

            for p in range(NP_CTX):
                run_pass("ctx", p)
            I("dve", "tensor_scalar", ST[:, :], ST[:, :], cp[:, C_FLAG:C_FLAG + 1], None, ALU.mult, R=r_ST + [r_c], W=r_ST)
            I("dve", "tensor_scalar", halo[:, :, :], halo[:, :, :], cp[:, C_FLAG:C_FLAG + 1], None, ALU.mult, R=r_halo + [r_c], W=r_halo)
            for p in range(NP_OWN):
                run_pass("own", p)
            S.out_toks.append(S.dma("sp", "fin", ssm_fin.ap(), ST[:, :], R=r_ST))
            S.out_toks.append(S.dma("sp", "fin", conv_fin.ap(), halo[:, :, :], R=r_halo))
            S.out_toks.append(S.dma("sp", "fin", conv_smp.ap(), convo[:, :, :, :], R=[r_convo]))
            for t in S.out_toks: S.wait_tok("sp", t)

        dry = Sched(nc, es, dry=True)
        emit(dry)
        ws.planned = True
        for k in cnt: cnt[k] = 0
        for obj in list(locals().values()):
            pass
        S = Sched(nc, es)
        _reset_all(locals())
        emit(S)
        S.build()
    return nc


def _reset_all(ns):
    def rs(o):
        if isinstance(o, Res): o.w = None; o.r = {}
        elif isinstance(o, Rot):
            o.i = 0
            for r in o.r: rs(r)
        elif isinstance(o, (list, tuple)):
            for x in o: rs(x)
    for v in ns.values(): rs(v)


def _rope_tab(pos):
    half = 8
    inv = np.exp(-math.log(500000.0) * np.arange(half, dtype=np.float32) * 2.0 / 16).astype(np.float32)
    ang = pos.astype(np.float32)[None, :] * inv[:, None]
    cos = np.ones((128, len(pos)), np.float32); sin = np.zeros((128, len(pos)), np.float32)
    for m in range(128):
        d = m % 64
        if d < 16:
            cos[m] = np.cos(ang[d % 8]); sin[m] = np.sin(ang[d % 8])
    return cos, sin


_PROG = None


def _prep(x_prompt, x_sample, cache_k, cache_v, state_ssm, state_conv, page_table,
          ln1_w, w_in, q_norm_w, k_norm_w, lambda_q1, lambda_k1, lambda_q2, lambda_k2,
          subln_w, conv_w, conv_b, dt_bias, a_log, d_skip, ssm_norm_w, w_out,
          ln2_w, w_up, w_down, cores=range(8)):
    f = lambda a: np.ascontiguousarray(np.asarray(a))
    x_prompt = f(x_prompt); x_sample = f(x_sample)
    w = np.zeros((D, 73 * 128), np.float32); w[:, :DIN] = np.asarray(w_in)[0]
    w_in_l = f(w.reshape(32, 128, 73, 128).transpose(2, 1, 0, 3))
    w_out_l = f(np.asarray(w_out)[0].reshape(32, 128, 32, 128).transpose(2, 1, 0, 3))
    w_up_l = f(np.asarray(w_up)[0].reshape(32, 128, 128, 128).transpose(2, 1, 0, 3))
    w_down_l = f(np.asarray(w_down)[0].reshape(4, 32, 128, 32, 128).transpose(3, 0, 2, 1, 4))
    cpk = np.zeros((128, 320), np.float32)
    cpk[:, 240:256] = np.asarray(ssm_norm_w)[0].reshape(16, 128).T
    cpk[:, 0:32] = np.asarray(ln1_w)[0].reshape(32, 128).T
    cpk[:, 32:64] = np.asarray(ln2_w)[0].reshape(32, 128).T
    cpk[:, 64:192] = np.asarray(conv_w)[0].reshape(4, 32, 128).transpose(2, 1, 0).reshape(128, 128)
    cpk[:, 192:224] = np.asarray(conv_b)[0].reshape(32, 128).T
    cpk[:, 224] = np.tile(np.asarray(q_norm_w)[0], 2); cpk[:, 225] = np.tile(np.asarray(k_norm_w)[0], 2)
    cpk[0:32, 226] = np.asarray(dt_bias)[0]
    cpk[:, 229] = 1.0; cpk[:, 230] = np.arange(128); cpk[:, 231] = EPS; cpk[:, 232] = 0.0
    cmat = np.zeros((128, 5, 128), np.float32)
    cmat[:, 0] = np.eye(128)
    cmat[:, 1] = np.triu(np.ones((128, 128)))
    cmat[:, 2] = np.kron(np.eye(2), np.ones((64, 64)))
    Rm = np.zeros((128, 128), np.float32)
    for m in range(128):
        d = m % 64
        if d < 8: Rm[m + 8, m] = -1.0
        elif d < 16: Rm[m - 8, m] = 1.0
    cmat[:, 3] = Rm
    cmat[:, 4] = (cmat[:, 1] - 1.0) * 30000.0
    tri8 = np.zeros((8, 256), np.float32)
    for s in range(8):
        for t in range(8):
            if s <= t: tri8[s, t::8] = 1.0
    rowp = np.concatenate([np.asarray(subln_w)[0], np.asarray(a_log)[0], np.asarray(d_skip)[0],
                           np.asarray(lambda_q1)[0], np.asarray(lambda_k1)[0], np.asarray(lambda_q2)[0], np.asarray(lambda_k2)[0]]).astype(np.float32)[None, :]
    ck = np.asarray(cache_k)[0]; cvv = np.asarray(cache_v)[0]
    ckT = f(ck.transpose(0, 3, 2, 1).reshape(2560 * 128, 512)) if False else f(ck.reshape(2560, 128, 4, 128).transpose(0, 3, 2, 1).reshape(2560 * 128, 512))
    cvf = f(cvv.reshape(2560 * 128, 512))
    pt = np.asarray(page_table).astype(np.int32)
    sssm = np.asarray(state_ssm)[0]; sconv = np.asarray(state_conv)[0]
    cs_smp, sn_smp = _rope_tab(PAST + np.arange(8))
    in_maps = []
    for c in cores:
        b, hf = c // 2, c % 2
        xo = f(x_prompt[b, hf * 1024:(hf + 1) * 1024].T)
        xc = f(x_prompt[b, 0:1024].T)
        xs = f(x_sample[4 * c:4 * c + 4].reshape(32, D).T)
        cpc = cpk.copy(); cpc[:, 227] = float(hf); cpc[:, 228] = 0.0 if hf else NEG
        ropes = np.zeros((128, 2, 2, 1032), np.float32)
        cc_, sc_ = _rope_tab(np.arange(1024)); ropes[:, 0, 0, :1024] = cc_; ropes[:, 1, 0, :1024] = sc_
        co_, so_ = _rope_tab(hf * 1024 + np.arange(1024)); ropes[:, 0, 1, :1024] = co_; ropes[:, 1, 1, :1024] = so_
        ropes[:, 0, 1, 1024:] = cs_smp; ropes[:, 1, 1, 1024:] = sn_smp
        stl = f(sssm[4 * c:4 * c + 4].reshape(4, 2048, 128).transpose(0, 2, 1))
        scl = f(sconv[4 * c:4 * c + 4].reshape(4, 3, 32, 128).transpose(3, 0, 2, 1))
        in_maps.append(dict(xT_own=xo, xT_ctx=xc, xT_smp=xs, ckT=ckT, cv=cvf, st_ssm=stl, st_conv=scl,
                            ptab=f(pt[4 * c:4 * c + 4].reshape(1, 256)), w_in=w_in_l, w_out=w_out_l, w_up=w_up_l, w_down=w_down_l,
                            cpk=cpc, cmat=cmat, ropes=ropes, rowp=rowp, tri8=tri8))
    return in_maps


def kernel(**inputs):
    global _PROG
    in_maps = _prep(**inputs)
    if _PROG is None:
        _PROG = build_program()
    res = run_bass_kernel_spmd(_PROG, in_maps, core_ids=list(range(8))).results
    return _post(res)


def _post(res, cores=range(8)):
    y_p = np.zeros((4, 2048, D), np.float32); y_s = np.zeros((32, 8, D), np.float32)
    k_p = np.zeros((1, 4, 2048, 4, 128), np.float32); v_p = np.zeros_like(k_p)
    s_p = np.zeros((1, 4, 32, 64, 128), np.float32); c_p = np.zeros((1, 4, 3, 4096), np.float32)
    k_s = np.zeros((1, 32, 8, 4, 128), np.float32); v_s = np.zeros_like(k_s)
    s_s = np.zeros((1, 32, 32, 64, 128), np.float32); c_s = np.zeros((1, 32, 3, 4096), np.float32)
    for ci_, c in enumerate(cores):
        r = res[ci_]; b, hf = c // 2, c % 2
        sl = slice(hf * 1024, (hf + 1) * 1024)
        y_p[b, sl] = r["yT_own"].T
        y_s[4 * c:4 * c + 4] = r["yT_smp"].T.reshape(4, 8, D)
        k_p[0, b, sl] = r["kT_own"].T.reshape(1024, 4, 128); v_p[0, b, sl] = r["vT_own"].T.reshape(1024, 4, 128)
        k_s[0, 4 * c:4 * c + 4] = r["kT_smp"].T.reshape(4, 8, 4, 128); v_s[0, 4 * c:4 * c + 4] = r["vT_smp"].T.reshape(4, 8, 4, 128)
        if hf == 1:
            s_p[0, b] = r["ssm_fin"].T.reshape(32, 64, 128)
            c_p[0, b] = r["conv_fin"].transpose(2, 1, 0).reshape(3, 4096)
        s_s[0, 4 * c:4 * c + 4] = r["ssm_smp"].transpose(0, 2, 1).reshape(4, 32, 64, 128)
        c_s[0, 4 * c:4 * c + 4] = r["conv_smp"].transpose(1, 3, 2, 0).reshape(4, 3, 4096)
    return (y_p, y_s, k_p, v_p, s_p, c_p, k_s, v_s, s_s, c_s)
```

```python
import math, contextlib
import numpy as np
import concourse.bass as bass
import concourse.mybir as mybir
from concourse.bass_utils import run_bass_kernel_spmd

F32 = mybir.dt.float32; BF16 = mybir.dt.bfloat16; I32 = mybir.dt.int32
AF = mybir.ActivationFunctionType; ALU = mybir.AluOpType

D = 4096; DIN = 9248; DFF = 16384; NCH = 32
TP = 256; TS = 8; NPASS = 4; NB2 = 1; NQT = TP // 128
EPS = 1e-6
LAM_INIT = 0.8 - 0.6 * math.exp(0.0)
PAST = 8192; NPG = 64
WCACHE = True
STAGE = 99; NP_CTX = 4; NP_OWN = 4; NPG_RUN = 64; GMODE = 0; SA_CUT = 0; SA_SKIP = 0
NEG = -30000.0


class Res:
    __slots__ = ("w", "r")
    def __init__(self):
        self.w = None; self.r = {}


class Sched:
    ENG = ("pe", "act", "dve", "pool", "sp")
    def __init__(self, nc, es, dry=False):
        self.nc = nc; self.es = es; self.dry = dry
        self.q = {e: [] for e in self.ENG}
        self.sem = {e: (None if dry else es.enter_context(nc.semaphore("sem_" + e))) for e in self.ENG}
        if dry:
            self.sem = {e: ("dry", e) for e in self.ENG}
        self.cnt = {e: 0 for e in self.ENG}
        self.seen = {e: {} for e in self.ENG}
        self.dsem = {}
        self.out_toks = []
    def _deps(self, eng, reads, writes):
        need = {}
        def add(t):
            if t is None: return
            k = id(t[0])
            if k not in need or need[k][1] < t[1]: need[k] = t
        for r in reads: add(r.w)
        for w in writes:
            add(w.w)
            for t in w.r.values(): add(t)
        own = self.sem[eng]
        for k, (s, v) in need.items():
            if eng == "pe" and s is own: continue
            if self.seen[eng].get(k, 0) >= v: continue
            self.seen[eng][k] = v
            self.q[eng].append(("wait", s, v))
    def _upd(self, tok, reads, writes):
        k = id(tok[0])
        for r in reads:
            if k not in r.r or r.r[k][1] < tok[1]: r.r[k] = tok
        for w in writes:
            w.w = tok; w.r = {}
    def op(self, eng, name, *a, R=(), W=(), inc=True, **kw):
        self._deps(eng, R, W)
        if inc:
            self.cnt[eng] += 1
            tok = (self.sem[eng], self.cnt[eng])
            self.q[eng].append(("op_inc", name, a, kw))
        else:
            tok = (self.sem[eng], self.cnt[eng] + 1)
            self.q[eng].append(("op", name, a, kw))
        self._upd(tok, R, W)
        return tok
    def dma(self, eng, semname, out, in_, R=(), W=(), ind=None):
        nsl = 1 if semname.startswith("w") else 3
        if semname not in self.dsem:
            self.dsem[semname] = [[[("dry", semname, i) if self.dry else self.es.enter_context(self.nc.semaphore("d_%s%d" % (semname, i))), 0] for i in range(nsl)], 0]
        pool = self.dsem[semname]
        ent = pool[0][pool[1] % nsl]; pool[1] += 1
        if ent[1] > 0: self.wait_tok(eng, (ent[0], ent[1]))
        self._deps(eng, R, W)
        ent[1] += 16
        tok = (ent[0], ent[1])
        self.q[eng].append(("dma", out, in_, ent[0], ind))
        self._upd(tok, R, W)
        return tok
    def wait_tok(self, eng, tok):
        s, v = tok
        if self.seen[eng].get(id(s), 0) >= v: return
        self.seen[eng][id(s)] = v
        self.q[eng].append(("wait", s, v))
    def build(self):
        nc = self.nc
        block = self.es.enter_context(nc.Block())
        sem = self.sem
        def run(e, ename, lst):
            for it in lst:
                k = it[0]
                if k == "wait": e.wait_ge(it[1], it[2])
                elif k == "op_inc": getattr(e, it[1])(*it[2], **it[3]).then_inc(sem[ename], 1)
                elif k == "op": getattr(e, it[1])(*it[2], **it[3])
                elif k == "dma":
                    if it[4] is None:
                        e.dma_start(out=it[1], in_=it[2]).then_inc(it[3], 16)
                    else:
                        e.indirect_dma_start(out=it[1], out_offset=None, in_=it[2],
                                             in_offset=bass.IndirectOffsetOnAxis(ap=it[4], axis=0)).then_inc(it[3], 16)
        q = self.q
        @block.tensor
        def _(e): run(e, "pe", q["pe"])
        @block.scalar
        def _(e): run(e, "act", q["act"])
        @block.vector
        def _(e): run(e, "dve", q["dve"])
        @block.gpsimd
        def _(e): run(e, "pool", q["pool"])
        @block.sync
        def _(e): run(e, "sp", q["sp"])


class Rot:
    def __init__(self, es, nc, name, shape, dtype, n):
        self.t = [es.enter_context(nc.sbuf_tensor(f"{name}{i}", shape, dtype)) for i in range(n)]
        self.r = [Res() for _ in range(n)]
        self.i = 0
    def get(self):
        i = self.i % len(self.t); self.i += 1
        return self.t[i], self.r[i]


class WStream:
    NB = 6
    def __init__(self, es, nc):
        self.buf = [es.enter_context(nc.sbuf_tensor(f"wbuf{i}", [128, 32, 128], BF16)) for i in range(self.NB)]
        self.plan = []; self.planned = False
    def start(self, S):
        self.S = S; self.res = [Res() for _ in range(self.NB)]
        self.ci = 0; self.ii = 0
        self.cached = {}
    def _issue(self):
        if self.ii >= len(self.plan): return
        key, src = self.plan[self.ii]
        b = self.ii % self.NB
        M = src.shape[2]
        if WCACHE and key in self.cached:
            self.S.dma("pool", f"w{b}", self.buf[b][:, :, 0:M], self.scr[key][:, :, 0:M], R=[self.cached[key]], W=[self.res[b]])
        else:
            self.S.dma("pool", f"w{b}", self.buf[b][:, :, 0:M], src, W=[self.res[b]])
            if WCACHE:
                kr = Res(); self.cached[key] = kr
                self.S.dma("sp", "wbk", self.scr[key][:, :, 0:M], self.buf[b][:, :, 0:M], R=[self.res[b]], W=[kr])
        self.ii += 1
    def get(self, key, src):
        if not self.planned:
            self.plan.append((key, src))
            b = len(self.plan) % self.NB
            return self.buf[b], self.res[b]
        while self.ii < min(self.ci + self.NB - 1, len(self.plan)):
            self._issue()
        if self.ii <= self.ci: self._issue()
        b = self.ci % self.NB; self.ci += 1
        return self.buf[b], self.res[b]


def build_program():
    nc = bass.Bass("TRN2", target_bir_lowering=False)
    def din(name, shape, dt=F32): return nc.dram_tensor(name, list(shape), dt, kind="ExternalInput")
    def dout(name, shape, dt=F32): return nc.dram_tensor(name, list(shape), dt, kind="ExternalOutput")
    xT_own = din("xT_own", [D, 1024]); xT_ctx = din("xT_ctx", [D, 1024]); xT_smp = din("xT_smp", [D, 32])
    ckT = din("ckT", [2560 * 128, 512]); cv = din("cv", [2560 * 128, 512])
    st_ssm = din("st_ssm", [4, 128, 2048]); st_conv = din("st_conv", [128, 4, 32, 3])
    ptab = din("ptab", [1, 256], I32)
    w_in = din("w_in", [73, 128, 32, 128]); w_out = din("w_out", [32, 128, 32, 128])
    w_up = din("w_up", [128, 128, 32, 128]); w_down = din("w_down", [32, 4, 128, 32, 128])
    cpk = din("cpk", [128, 320])
    cmat = din("cmat", [128, 5, 128])
    ropes = din("ropes", [128, 2, 2, 1024 + 8])
    rowp = din("rowp", [1, 128 + 32 + 32 + 256])
    tri8 = din("tri8", [8, 256])
    yT_own = dout("yT_own", [D, 1024]); yT_smp = dout("yT_smp", [D, 32])
    kT_own = dout("kT_own", [512, 1024]); vT_own = dout("vT_own", [512, 1024])
    kT_smp = dout("kT_smp", [512, 32]); vT_smp = dout("vT_smp", [512, 32])
    ssm_fin = dout("ssm_fin", [128, 2048]); conv_fin = dout("conv_fin", [128, 32, 3])
    ssm_smp = dout("ssm_smp", [4, 128, 2048]); conv_smp = dout("conv_smp", [128, 4, 32, 3])
    dbg_mix = dout("dbg_mix", [4, 128, 32, TP + TS], BF16) if STAGE == 5 else None
    if STAGE == 5:
        dbg_dtk = dout("dbg_dtk", [128, NQT + NB2, 32]); dbg_xs = dout("dbg_xs", [128, 2, TP + TS], BF16)
        dbg_B = dout("dbg_B", [128, TP + TS], BF16); dbg_C = dout("dbg_C", [128, TP + TS], BF16)

    es = contextlib.ExitStack()
    with es:
        def sb(name, shape, dt=F32): return es.enter_context(nc.sbuf_tensor(name, list(shape), dt))
        T = TP + TS
        xn = sb("xn", [128, NCH, T], BF16); r_xn = [Res() for _ in range(NCH)]
        mix = sb("mix", [128, NCH, T], BF16); r_mix = [Res() for _ in range(NCH)]
        KT = sb("KT", [128, 4, 2048], BF16); r_KT = [Res() for _ in range(4)]
        VA = sb("VA", [128, 16, 4, 130], BF16); r_VA = [Res() for _ in range(4)]
        ST = sb("ST", [128, 2048]); r_ST = [Res() for _ in range(8)]
        halo = sb("halo", [128, 32, 3]); r_halo = [Res() for _ in range(32)]
        convo = sb("convo", [128, 4, 32, 3]); r_convo = Res()
        cp = sb("cp", [128, 320]); r_c = Res()
        cm = sb("cm", [128, 5, 128]); cmb = sb("cmb", [128, 2, 128], BF16)
        rp = sb("rp", [128, 128 + 32 + 32 + 256])
        t8 = sb("t8", [8, 256], BF16); t8f = sb("t8f", [8, 256])
        rope = sb("rope", [128, 2, T]); r_rope = Res()
        rstd = sb("rstd", [128, T]); r_rstd = Res()
        rstd2 = sb("rstd2", [128, T]); r_rstd2 = Res()
        qT = sb("qT", [128, 4, TP], BF16); r_qT = [Res() for _ in range(4)]
        qsm = sb("qsm", [128, NB2, 16, 8], BF16); r_qsm = Res()
        ksm = sb("ksm", [128, 4, TS], BF16); r_ksm = Res()
        vsm = [sb(f"vsm{i}", [8, 4, 130], BF16) for i in range(NB2)]; r_vsm = Res()
        zT = sb("zT", [128, 2, T], BF16); r_zT = [Res(), Res()]
        xsT = sb("xsT", [128, 2, T], BF16); r_xsT = [Res(), Res()]
        BT = sb("BT", [128, T], BF16); r_BT = Res()
        CTt = sb("CTt", [128, T], BF16); r_CT = Res()
        dtT = sb("dtT", [32, T]); r_dtT = Res()
        dtk = sb("dtk", [128, NQT + NB2, 32]); r_dtk = Res()
        lamt = sb("lamt", [128, 4]); r_lam = Res()
        idx = sb("idx", [128, 256], I32); idxf = sb("idxf", [128, 256]); r_idx = Res()
        smacc = sb("smacc", [128, NCH, TS]); r_smacc = Res()
        ones1 = sb("ones1", [128, 1])
        ws = WStream(es, nc)
        _scr = [nc.dram_tensor("wscr_in", [73, 128, 32, 128], BF16, kind="Internal"), nc.dram_tensor("wscr_out", [32, 128, 32, 128], BF16, kind="Internal"),
                nc.dram_tensor("wscr_up", [128, 128, 32, 128], BF16, kind="Internal"), nc.dram_tensor("wscr_dn", [128, 128, 32, 128], BF16, kind="Internal")]
        class _Scr:
            def __getitem__(self, k):
                if k < 73: return _scr[0][k]
                if k < 105: return _scr[1][k - 73]
                if k < 233: return _scr[2][k - 105]
                return _scr[3][k - 233]
        ws.scr = _Scr()
        tmpA = Rot(es, nc, "tmpA", [128, T], F32, 3)
        tmpB = Rot(es, nc, "tmpB", [128, T], F32, 3)
        tmpC = Rot(es, nc, "tmpC", [128, T], F32, 2)
        xpb = Rot(es, nc, "xpb", [128, 3 + TP + NB2 * 11], F32, 2)
        pT = Rot(es, nc, "pT", [128, 2, TP], BF16, 2)
        sm1 = Rot(es, nc, "sm1", [128, 512], F32, 2)
        sm2 = Rot(es, nc, "sm2", [128, 512], F32, 2)
        smb = Rot(es, nc, "smb", [128, 512], BF16, 4)
        col = Rot(es, nc, "col", [128, 64], F32, 6)
        kpg = Rot(es, nc, "kpg", [128, 512], F32, 2); vpg = Rot(es, nc, "vpg", [128, 512], F32, 2)
        kpb = Rot(es, nc, "kpb", [128, 512], BF16, 2); vpb = Rot(es, nc, "vpb", [128, 4, 130], BF16, 2)
        stg = Rot(es, nc, "stg", [128, 256], F32, 2)
        q_xs = Rot(es, nc, "q_xs", [128, 256], F32, 2); q_Bt = Rot(es, nc, "q_Bt", [128, 128], BF16, 2)
        q_xdt = Rot(es, nc, "q_xdt", [128, 256], BF16, 2); q_xdte = Rot(es, nc, "q_xdte", [128, 256], BF16, 2)
        q_Sb = Rot(es, nc, "q_Sb", [128, 256], BF16, 1); q_y = Rot(es, nc, "q_y", [128, 256], F32, 1)
        q_Dm = Rot(es, nc, "q_Dm", [128, 512], F32, 1); q_Lm = Rot(es, nc, "q_Lm", [128, 512], F32, 1)
        q_Wm = Rot(es, nc, "q_Wm", [128, 512], BF16, 1); q_zz = Rot(es, nc, "q_zz", [128, 256], F32, 1)
        q_yb = Rot(es, nc, "q_yb", [128, 256], BF16, 1)
        PSD = [es.enter_context(nc.psum_tensor(f"psd{i}", [128, 2, 512], F32)) for i in range(4)]
        r_PSD = [[Res(), Res()] for _ in range(4)]
        cnt = {"dn": 0, "sc": 0}
        def dn():
            i = cnt["dn"] % 2; cnt["dn"] += 1
            return PSD[i], r_PSD[i]
        def sc():
            i = cnt["sc"] % 4; cnt["sc"] += 1
            return PSD[2 + i // 2][:, i % 2, :], r_PSD[2 + i // 2][i % 2]

        ident = cm[:, 0, :]; tri = cm[:, 1, :]; bones = cm[:, 2, :]; Rm = cm[:, 3, :]; negm = cm[:, 4, :]
        identb = cmb[:, 0, :]; trib = cmb[:, 1, :]
        C_LN1 = 0; C_LN2 = 32; C_CW = 64; C_CB = 192; C_QW = 224; C_KW = 225; C_DTB = 226; C_FLAG = 227; C_CBIAS = 228; C_ONE = 229
        OFF_SUB = 0; OFF_AL = 128; OFF_DS = 160; OFF_LAM = 192; C_SN = 240
        all_ones = sb("all_ones", [128, 128])

        def emit(S):
            ws.start(S)
            I = S.op
            S.dma("sp", "c0", cp[:], cpk.ap(), W=[r_c]); S.dma("sp", "c0", cm[:], cmat.ap(), W=[r_c])
            S.dma("sp", "c0", rp[:], rowp.ap().partition_broadcast(128), W=[r_c])
            S.dma("sp", "c0", t8f[:], tri8.ap(), W=[r_c])
            S.dma("sp", "c0", idx[:], ptab.ap().partition_broadcast(128), W=[r_idx])
            S.dma("sp", "c0", convo[:], st_conv.ap(), W=[r_convo])
            I("dve", "tensor_copy", cmb[:, 0, :], cm[:, 0, :], R=[r_c], W=[r_c])
            I("dve", "tensor_copy", cmb[:, 1, :], cm[:, 1, :], R=[r_c], W=[r_c])
            I("dve", "tensor_copy", t8[:], t8f[:], R=[r_c], W=[r_c])
            I("dve", "memset", all_ones[:], 1.0, W=[r_c])
            I("dve", "memset", ST[:], 0.0, W=r_ST)
            I("dve", "memset", halo[:], 0.0, W=r_halo)
            I("dve", "memset", VA[:], 1.0, W=r_VA)
            I("dve", "memset", KT[:], 0.0, W=r_KT)
            for v in vsm: I("dve", "memset", v[:], 1.0, W=[r_vsm])
            for rr in (vpb.t): pass
            for i in range(2): I("dve", "memset", vpb.t[i][:], 1.0, W=[vpb.r[i]])
            I("act", "activation", rp[:, OFF_AL:OFF_AL + 32], rp[:, OFF_AL:OFF_AL + 32], AF.Exp, R=[r_c], W=[r_c])
            I("dve", "tensor_scalar", rp[:, OFF_AL:OFF_AL + 32], rp[:, OFF_AL:OFF_AL + 32], -1.0, None, ALU.mult, R=[r_c], W=[r_c])
            L0 = OFF_LAM
            I("dve", "tensor_tensor", lamt[:, 0:1].to_broadcast([128, 1]) if False else rp[:, L0:L0 + 64], rp[:, L0:L0 + 64], rp[:, L0 + 64:L0 + 128], ALU.mult, R=[r_c], W=[r_c])
            I("dve", "tensor_tensor", rp[:, L0 + 128:L0 + 192], rp[:, L0 + 128:L0 + 192], rp[:, L0 + 192:L0 + 256], ALU.mult, R=[r_c], W=[r_c])
            I("dve", "reduce_sum", lamt[:, 1:2], rp[:, L0:L0 + 64], mybir.AxisListType.X, R=[r_c], W=[r_lam])
            I("dve", "reduce_sum", lamt[:, 2:3], rp[:, L0 + 128:L0 + 192], mybir.AxisListType.X, R=[r_c], W=[r_lam])
            I("act", "activation", lamt[:, 1:3], lamt[:, 1:3], AF.Exp, R=[r_lam], W=[r_lam])
            I("dve", "tensor_tensor", lamt[:, 0:1], lamt[:, 2:3], lamt[:, 1:2], ALU.subtract, R=[r_lam], W=[r_lam])
            I("dve", "tensor_scalar", lamt[:, 0:1], lamt[:, 0:1], -LAM_INIT, None, ALU.add, R=[r_lam], W=[r_lam])
            I("dve", "tensor_copy", idxf[:], idx[:], R=[r_idx], W=[r_idx])
            I("dve", "tensor_scalar", idxf[:], idxf[:], 128.0, cp[:, C_ONE + 1:C_ONE + 2], ALU.mult, ALU.add, R=[r_idx, r_c], W=[r_idx])
            I("dve", "tensor_copy", idx[:], idxf[:], R=[r_idx], W=[r_idx])
            I("dve", "memset", smacc[:], 0.0, W=[r_smacc])

            def load_x(src_list, Tn, lnc):
                pss, rps = sc()
                pss2, rps2 = sc()
                for c in range(NCH):
                    xs_, rx = tmpA.get()
                    for (src, c0, n) in src_list:
                        S.dma("sp", "xld", xs_[:, c0:c0 + n], src[c * 128:(c + 1) * 128, :], W=[rx])
                    I("dve", "tensor_scalar", xn[:, c, 0:Tn], xs_[:, 0:Tn], cp[:, lnc + c:lnc + c + 1], None, ALU.mult, R=[rx, r_c], W=[r_xn[c]])
                    sq, rq = tmpB.get()
                    I("act", "activation", sq[:, 0:Tn], xs_[:, 0:Tn], AF.Square, R=[rx], W=[rq])
                    I("pe", "matmul", pss[:, 0:min(Tn, 512)], all_ones[:], sq[:, 0:min(Tn, 512)], start=(c == 0), stop=(c == NCH - 1), R=[rq, r_c], W=[rps], inc=(Tn <= 512))
                    if Tn > 512:
                        I("pe", "matmul", pss2[:, 0:Tn - 512], all_ones[:], sq[:, 512:Tn], start=(c == 0), stop=(c == NCH - 1), R=[rq, r_c], W=[rps2])
                fin_rstd(rstd, r_rstd, pss, rps, pss2, rps2, Tn)

            def fin_rstd(dst, rdst, pss, rps, pss2, rps2, Tn):
                I("act", "activation", dst[:, 0:min(Tn, 512)], pss[:, 0:min(Tn, 512)], AF.Sqrt, bias=cp[:, C_ONE + 2:C_ONE + 3], scale=1.0 / D, R=[rps, r_c], W=[rdst])
                if Tn > 512:
                    I("act", "activation", dst[:, 512:Tn], pss2[:, 0:Tn - 512], AF.Sqrt, bias=cp[:, C_ONE + 2:C_ONE + 3], scale=1.0 / D, R=[rps2, r_c], W=[rdst])
                I("dve", "reciprocal", dst[:, 0:Tn], dst[:, 0:Tn], R=[rdst], W=[rdst])

            def dense(srcs, rhsbuf, r_rhs, Tn, M=128):
                ps, rps = dn()
                nb = len(srcs)
                for bi, (key, src) in enumerate(srcs):
                    wb, wr = ws.get(key, src)
                    for k in range(32):
                        kc = bi * 32 + k
                        st = (kc == 0); sp = (kc == nb * 32 - 1)
                        I("pe", "matmul", ps[0:M, 0, 0:min(Tn, 512)], wb[:, k, 0:M], rhsbuf[:, kc, 0:min(Tn, 512)], start=st, stop=sp,
                          R=[wr, r_rhs[kc]], W=[rps[0]], inc=(sp and Tn <= 512))
                        if Tn > 512:
                            I("pe", "matmul", ps[0:M, 1, 0:Tn - 512], wb[:, k, 0:M], rhsbuf[:, kc, 512:Tn], start=st, stop=sp,
                              R=[wr, r_rhs[kc]], W=[rps[1]], inc=sp)
                return ps, rps

            def halves(Tn):
                hs = [(0, 0, min(Tn, 512))]
                if Tn > 512: hs.append((1, 512, Tn - 512))
                return hs

            def scaled(ps, rps, Tn, M=128):
                a, ra = tmpA.get()
                for (h, c0, n) in halves(Tn):
                    I("dve", "tensor_tensor", a[0:M, c0:c0 + n], ps[0:M, h, 0:n], rstd[0:M, c0:c0 + n], ALU.mult, R=[rps[h], r_rstd], W=[ra])
                return a, ra

            def norm_rope(a, ra, Tn, wc):
                o, ro = tmpC.get()
                for (h, c0, n) in halves(Tn):
                    sq, rq = tmpB.get()
                    I("act", "activation", sq[:, 0:n], a[:, c0:c0 + n], AF.Square, R=[ra], W=[rq])
                    p1, rp1 = sc()
                    I("pe", "matmul", p1[:, 0:n], bones, sq[:, 0:n], start=True, stop=True, R=[rq, r_c], W=[rp1])
                    I("act", "activation", sq[:, 0:n], p1[:, 0:n], AF.Sqrt, bias=cp[:, C_ONE + 2:C_ONE + 3], scale=1.0 / 64, R=[rp1, r_c], W=[rq])
                    I("dve", "reciprocal", sq[:, 0:n], sq[:, 0:n], R=[rq], W=[rq])
                    I("dve", "scalar_tensor_tensor", sq[:, 0:n], a[:, c0:c0 + n], cp[:, wc:wc + 1], sq[:, 0:n], ALU.mult, ALU.mult, R=[ra, rq, r_c], W=[rq])
                    p2, rp2 = sc()
                    I("pe", "matmul", p2[:, 0:n], Rm, sq[:, 0:n], start=True, stop=True, R=[rq, r_c], W=[rp2])
                    I("dve", "tensor_tensor", o[:, c0:c0 + n], p2[:, 0:n], rope[:, 1, c0:c0 + n], ALU.mult, R=[rp2, r_rope], W=[ro])
                    I("dve", "tensor_tensor", sq[:, 0:n], sq[:, 0:n], rope[:, 0, c0:c0 + n], ALU.mult, R=[rq, r_rope], W=[rq])
                    I("dve", "tensor_tensor", o[:, c0:c0 + n], o[:, c0:c0 + n], sq[:, 0:n], ALU.add, R=[rq, ro], W=[ro])
                return o, ro

            def transpose_to(dst_ap, src_ap, rsrc, rdst, npart_in, nfree_in, bf=False, usedn=False, scale=None):
                if usedn:
                    pp_, rr_ = dn(); p = pp_[:, 0, :]; rpp = rr_[0]
                else:
                    p, rpp = sc()
                kw = {} if scale is None else {"scale": scale}
                if bf:
                    pv = p.bitcast(BF16)
                    I("pe", "transpose", pv[0:nfree_in, 0:npart_in], src_ap, identb[0:npart_in, 0:npart_in], R=[rsrc, r_c], W=[rpp])
                    I("act", "activation", dst_ap, pv[0:nfree_in, 0:npart_in], AF.Copy, R=[rpp, r_c], W=[rdst], **kw)
                else:
                    I("pe", "transpose", p[0:nfree_in, 0:npart_in], src_ap, ident[0:npart_in, 0:npart_in], R=[rsrc, r_c], W=[rpp])
                    I("act", "activation", dst_ap, p[0:nfree_in, 0:npart_in], AF.Copy, R=[rpp], W=[rdst])

            def run_pass(kind, p):
                own = (kind == "own")
                Tn = T if own else TP
                xsrc = xT_own if own else xT_ctx
                tok0 = p * TP
                kpos0 = (1024 if own else 0) + tok0
                srcl = [(xsrc[:, tok0:tok0 + TP], 0, TP)]
                if own: srcl.append((xT_smp[:, p * TS:(p + 1) * TS], TP, TS))
                S.dma("sp", "rope", rope[:, :, 0:TP], ropes[:, :, 1 if own else 0, tok0:tok0 + TP], W=[r_rope])
                if own:
                    for b2 in range(NB2):
                        S.dma("sp", "rope", rope[:, :, TP + 8 * b2:TP + 8 * b2 + 8], ropes[:, :, 1, 1024:1032], W=[r_rope])
                load_x(srcl, Tn, C_LN1)

                ps, rps = dense([(72, w_in[72][:, :, 0:32])], xn, r_xn, Tn, M=32)
                a, ra = scaled(ps, rps, Tn, M=32)
                I("act", "activation", a[0:32, 0:Tn], a[0:32, 0:Tn], AF.Exp, bias=cp[0:32, C_DTB:C_DTB + 1], scale=1.0, R=[ra, r_c], W=[ra])
                I("act", "activation", dtT[:, 0:Tn], a[0:32, 0:Tn], AF.Ln, bias=cp[0:32, C_ONE:C_ONE + 1], scale=1.0, R=[ra, r_c], W=[r_dtT])
                nchunk = 4
                segs = [(i * 128, 128) for i in range(NQT)] + ([(TP + 8 * b_, 8) for b_ in range(NB2)] if own else [])
                for si, (c0, L) in enumerate(segs):
                    transpose_to(dtk[0:L, si, :], dtT[:, c0:c0 + L], r_dtT, r_dtk, 32, L)

                if STAGE <= 1: return
                for j in range(4):
                    ps, rps = dense([(16 + j, w_in[16 + j])], xn, r_xn, Tn)
                    a, ra = scaled(ps, rps, Tn)
                    o, ro = norm_rope(a, ra, Tn, C_KW)
                    I("act", "activation", KT[:, j, kpos0:kpos0 + TP], o[:, 0:TP], AF.Copy, R=[ro], W=[r_KT[j]])
                    if own:
                        I("act", "activation", ksm[:, j, :], o[:, TP:T], AF.Copy, R=[ro], W=[r_ksm])
                        S.out_toks.append(S.dma("sp", "ko", kT_own[j * 128:(j + 1) * 128, tok0:tok0 + TP], o[:, 0:TP], R=[ro]))
                        S.out_toks.append(S.dma("sp", "ko", kT_smp[j * 128:(j + 1) * 128, p * TS:(p + 1) * TS], o[:, TP:T], R=[ro]))
                    ps, rps = dense([(20 + j, w_in[20 + j])], xn, r_xn, Tn)
                    a, ra = scaled(ps, rps, Tn)
                    if own:
                        S.out_toks.append(S.dma("sp", "vo", vT_own[j * 128:(j + 1) * 128, tok0:tok0 + TP], a[:, 0:TP], R=[ra]))
                        S.out_toks.append(S.dma("sp", "vo", vT_smp[j * 128:(j + 1) * 128, p * TS:(p + 1) * TS], a[:, TP:T], R=[ra]))
                    for tt in range(NQT):
                        transpose_to(VA[:, kpos0 // 128 + tt, j, 0:128], a[:, tt * 128:(tt + 1) * 128], ra, r_VA[j], 128, 128)
                    if own:
                        for b2 in range(NB2):
                            transpose_to(vsm[b2][:, j, 0:128], a[:, TP + 8 * b2:TP + 8 * b2 + 8], ra, r_vsm, 128, 8)
                    if not own or STAGE <= 2: continue
                    for hl in range(4):
                        hh = 4 * j + hl
                        ps, rps = dense([(hh, w_in[hh])], xn, r_xn, Tn)
                        a, ra = scaled(ps, rps, Tn)
                        o, ro = norm_rope(a, ra, Tn, C_QW)
                        I("act", "activation", qT[:, hl, :], o[:, 0:TP], AF.Copy, R=[ro], W=[r_qT[hl]])
                        for b2 in range(NB2): I("act", "activation", qsm[:, b2, hh, :], o[:, TP + 8 * b2:TP + 8 * b2 + 8], AF.Copy, R=[ro], W=[r_qsm])
                    nst = kpos0 // 128 + NQT
                    for hl in range(4):
                        hh = 4 * j + hl
                        Oa = [PSD[2][:, 0, :], PSD[2][:, 1, :], PSD[3][:, 0, :], PSD[3][:, 1, :]]
                        rOa = [r_PSD[2][0], r_PSD[2][1], r_PSD[3][0], r_PSD[3][1]]
                        for st in range(nst):
                            oi = st - kpos0 // 128
                            q0 = max(oi, 0) * 128
                            nq = TP - q0
                            ps, rps = dn()
                            for c in range(2):
                                I("pe", "matmul", ps[:, c, 0:nq], KT[c * 64:(c + 1) * 64, j, st * 128:(st + 1) * 128], qT[c * 64:(c + 1) * 64, hl, q0:TP],
                                  start=True, stop=True, R=[r_KT[j], r_qT[hl]], W=[rps[c]])
                            pt, rpt = pT.get()
                            bias_ap = cp[:, C_CBIAS:C_CBIAS + 1] if st < 8 else cp[:, C_ONE + 3:C_ONE + 4]
                            for c in range(2):
                                I("act", "activation", pt[:, c, 0:nq], ps[:, c, 0:nq], AF.Exp, bias=bias_ap, scale=0.125, R=[rps[c], r_c], W=[rpt])
                            if oi >= 0:
                                for c in range(2):
                                    I("dve", "tensor_tensor", pt[:, c, 0:128], pt[:, c, 0:128], trib, ALU.mult, R=[rpt, r_c], W=[rpt])
                            for qi in range(max(oi, 0), NQT):
                                last = (st == kpos0 // 128 + qi)
                                for c in range(2):
                                    I("pe", "matmul", Oa[qi][:, c * 130:c * 130 + 129], pt[:, c, qi * 128 - q0:(qi + 1) * 128 - q0], VA[:, st, j, 0:129],
                                      start=(st == 0 and c == 0), stop=(last and c == 1), skip_group_check=True, R=[rpt, r_VA[j]], W=[rOa[qi]], inc=(c == 1))
                        for qi in range(NQT):
                            attn_finish(Oa[qi][:, 0:130], Oa[qi][:, 130:260], rOa[qi], 128, mix[:, hh, qi * 128:(qi + 1) * 128], r_mix[hh])

                if own and STAGE >= 4:
                    for b2 in range(NB2):
                        sample_attention(p, b2)
                if STAGE <= 4: return

                for g in range(8):
                    tiles = [("xs", 0, 40 + 2 * g), ("xs", 1, 41 + 2 * g), ("B", 0, 56 + g)]
                    tiles += [("C", 0, 64 + g)]
                    if own: tiles += [("z", 0, 24 + 2 * g), ("z", 1, 25 + 2 * g)]
                    for (kd, il, wt) in tiles:
                        ps, rps = dense([(wt, w_in[wt])], xn, r_xn, Tn)
                        if kd == "z":
                            for (h, c0, n) in halves(Tn):
                                I("dve", "tensor_tensor", zT[:, il, c0:c0 + n], ps[:, h, 0:n], rstd[:, c0:c0 + n], ALU.mult, R=[rps[h], r_rstd], W=[r_zT[il]])
                            continue
                        ci = wt - 40
                        xp, rxp = xpb.get()
                        I("dve", "tensor_copy", xp[:, 0:3], halo[:, ci, :], R=[r_halo[ci]], W=[rxp])
                        I("dve", "tensor_tensor", xp[:, 3:3 + TP], ps[:, 0, 0:TP], rstd[:, 0:TP], ALU.mult, R=[rps[0], r_rstd], W=[rxp])
                        I("dve", "tensor_copy", halo[:, ci, :], xp[:, TP:TP + 3], R=[rxp], W=[r_halo[ci]])
                        dst = {"xs": xsT[:, il, :], "B": BT, "C": CTt}[kd]
                        rd = {"xs": r_xsT[il], "B": r_BT, "C": r_CT}[kd]
                        def conv(o0, n, d0):
                            acc, racc = tmpB.get()
                            I("dve", "tensor_scalar", acc[:, 0:n], xp[:, o0:o0 + n], cp[:, C_CW + ci * 4:C_CW + ci * 4 + 1], cp[:, C_CB + ci:C_CB + ci + 1], ALU.mult, ALU.add, R=[rxp, r_c], W=[racc])
                            for k in range(1, 4):
                                I("dve", "scalar_tensor_tensor", acc[:, 0:n], xp[:, o0 + k:o0 + k + n], cp[:, C_CW + ci * 4 + k:C_CW + ci * 4 + k + 1], acc[:, 0:n], ALU.mult, ALU.add, R=[rxp, racc, r_c], W=[racc])
                            I("act", "activation", dst[:, d0:d0 + n], acc[:, 0:n], AF.Silu, R=[racc], W=[rd])
                        conv(0, TP, 0)
                        if own:
                            for b2 in range(NB2):
                                bb = NB2 * p + b2
                                o0 = 3 + TP + 11 * b2
                                I("dve", "tensor_copy", xp[:, o0:o0 + 3], convo[:, bb, ci, :], R=[r_convo], W=[rxp])
                                I("dve", "tensor_tensor", xp[:, o0 + 3:o0 + 11], ps[:, 0, TP + 8 * b2:TP + 8 * b2 + 8], rstd[:, TP + 8 * b2:TP + 8 * b2 + 8], ALU.mult, R=[rps[0], r_rstd], W=[rxp])
                                I("dve", "tensor_copy", convo[:, bb, ci, :], xp[:, o0 + 8:o0 + 11], R=[rxp], W=[r_convo])
                                conv(o0, 8, TP + 8 * b2)
                    ssd_group(kind, p, g, segs)

                if not own: return
                if STAGE == 5:
                    S.out_toks.append(S.dma("sp", "dbg", dbg_mix[p], mix[:, :, :], R=r_mix))
                    S.out_toks.append(S.dma("sp", "dbg", dbg_dtk.ap(), dtk[:, :, :], R=[r_dtk]))
                    S.out_toks.append(S.dma("sp", "dbg", dbg_xs.ap(), xsT[:, :, :], R=r_xsT))
                    S.out_toks.append(S.dma("sp", "dbg", dbg_B.ap(), BT[:, :], R=[r_BT]))
                    S.out_toks.append(S.dma("sp", "dbg", dbg_C.ap(), CTt[:, :], R=[r_CT]))
                    return
                pss, rpsa = sc(); pss2, rpsb = sc()
                for ct in range(32):
                    ps, rps = dense([(73 + ct, w_out[ct])], mix, r_mix, T)
                    xr, rxr = tmpA.get()
                    S.dma("sp", "xld", xr[:, 0:TP], xT_own[ct * 128:(ct + 1) * 128, tok0:tok0 + TP], W=[rxr])
                    S.dma("sp", "xld", xr[:, TP:T], xT_smp[ct * 128:(ct + 1) * 128, p * TS:(p + 1) * TS], W=[rxr])
                    for (h, c0, n) in halves(T):
                        I("dve", "tensor_tensor", xr[:, c0:c0 + n], xr[:, c0:c0 + n], ps[:, h, 0:n], ALU.add, R=[rxr, rps[h]], W=[rxr])
                    S.dma("sp", "yb", yT_own[ct * 128:(ct + 1) * 128, tok0:tok0 + TP], xr[:, 0:TP], R=[rxr], W=[r_yb[ct]])
                    I("act", "activation", smacc[:, ct, :], xr[:, TP:T], AF.Copy, R=[rxr], W=[r_smacc])
                    I("dve", "tensor_scalar", xn[:, ct, :], xr[:, :], cp[:, C_LN2 + ct:C_LN2 + ct + 1], None, ALU.mult, R=[rxr, r_c], W=[r_xn[ct]])
                    sq, rq = tmpB.get()
                    I("act", "activation", sq[:, :], xr[:, :], AF.Square, R=[rxr], W=[rq])
                    I("pe", "matmul", pss[:, 0:T], all_ones[:], sq[:, 0:T], start=(ct == 0), stop=(ct == 31), R=[rq, r_c], W=[rpsa])
                fin_rstd(rstd2, r_rstd2, pss, rpsa, pss2, rpsb, T)
                if STAGE <= 6:
                    S.out_toks.append(S.dma("sp", "yso", yT_smp.ap().rearrange("(c k) t -> k c t", k=128)[:, :, p * TS:(p + 1) * TS], smacc[:], R=[r_smacc]))
                    return
                for fc in range(4):
                    for i in range(32):
                        ps, rps = dense([(105 + fc * 32 + i, w_up[fc * 32 + i])], xn, r_xn, T)
                        t1, rt1 = tmpA.get()
                        for (h, c0, n) in halves(T):
                            I("dve", "tensor_tensor", t1[:, c0:c0 + n], ps[:, h, 0:n], rstd2[:, c0:c0 + n], ALU.mult, R=[rps[h], r_rstd2], W=[rt1])
                        I("dve", "scalar_tensor_tensor", mix[:, i, :], t1[:, :], 0.0, t1[:, :], ALU.max, ALU.mult, R=[rt1], W=[r_mix[i]])
                    for ct in range(32):
                        ps, rps = dense([(233 + ct * 4 + fc, w_down[ct, fc])], mix, r_mix, T)
                        yr, ryr = tmpC.get()
                        S.dma("sp", "yld", yr[:, 0:TP], yT_own[ct * 128:(ct + 1) * 128, tok0:tok0 + TP], R=[r_yb[ct]], W=[ryr])
                        I("dve", "tensor_tensor", yr[:, 0:TP], yr[:, 0:TP], ps[:, 0, 0:TP], ALU.add, R=[ryr, rps[0]], W=[ryr])
                        tk = S.dma("sp", "yb", yT_own[ct * 128:(ct + 1) * 128, tok0:tok0 + TP], yr[:, 0:TP], R=[ryr], W=[r_yb[ct]])
                        I("dve", "tensor_tensor", smacc[:, ct, :], smacc[:, ct, :], ps[:, 0, TP:T], ALU.add, R=[rps[0], r_smacc], W=[r_smacc])
                        if fc == 3: S.out_toks.append(tk)
                S.out_toks.append(S.dma("sp", "yso", yT_smp.ap().rearrange("(c k) t -> k c t", k=128)[:, :, p * TS:(p + 1) * TS], smacc[:], R=[r_smacc]))

            r_yb = [Res() for _ in range(32)]

            def attn_finish(O0, O1, rO, M, dstT, rdst):
                cc, rcc = col.get()
                I("dve", "reciprocal", cc[0:M, 0:1], O0[0:M, 128:129], R=[rO], W=[rcc])
                I("dve", "reciprocal", cc[0:M, 1:2], O1[0:M, 128:129], R=[rO], W=[rcc])
                I("dve", "tensor_tensor", cc[0:M, 1:2], cc[0:M, 1:2], lamt[0:M, 0:1], ALU.mult, R=[rcc, r_lam], W=[rcc])
                o, ro = sm1.get()
                I("dve", "tensor_scalar", o[0:M, 0:128], O0[0:M, 0:128], cc[0:M, 0:1], None, ALU.mult, R=[rO, rcc], W=[ro])
                I("dve", "scalar_tensor_tensor", o[0:M, 0:128], O1[0:M, 0:128], cc[0:M, 1:2], o[0:M, 0:128], ALU.mult, ALU.add, R=[rO, rcc, ro], W=[ro])
                jk, rjk = sm2.get()
                I("act", "activation", jk[0:M, 0:128], o[0:M, 0:128], AF.Square, accum_out=cc[0:M, 2:3], R=[ro], W=[rjk, rcc])
                I("act", "activation", cc[0:M, 2:3], cc[0:M, 2:3], AF.Sqrt, bias=cp[0:M, C_ONE + 2:C_ONE + 3], scale=1.0 / 128, R=[rcc, r_c], W=[rcc])
                I("dve", "reciprocal", cc[0:M, 2:3], cc[0:M, 2:3], R=[rcc], W=[rcc])
                I("dve", "tensor_scalar", cc[0:M, 2:3], cc[0:M, 2:3], 1.0 - LAM_INIT, None, ALU.mult, R=[rcc], W=[rcc])
                ob, rob = smb.get()
                I("dve", "scalar_tensor_tensor", ob[0:M, 0:128], o[0:M, 0:128], cc[0:M, 2:3], rp[0:M, OFF_SUB:OFF_SUB + 128], ALU.mult, ALU.mult, R=[ro, rcc, r_c], W=[rob])
                if dstT is not None:
                    transpose_to(dstT, ob[0:M, 0:128], rob, rdst, M, 128, bf=True, usedn=True)
                return ob, rob

            def sample_attention(p, b2):
                bb = NB2 * p + b2
                Oc = [PSD[2], PSD[3]]
                for pg in range(NPG_RUN + 1):
                    new = (pg == NPG_RUN)
                    L = 8 if new else 128
                    if not new:
                        kf, rkf = kpg.get(); vf, rvf = vpg.get()
                        if GMODE == 0:
                            S.dma("pool", "kg", kf[:, :], ckT.ap()[:, :], W=[rkf], R=[r_idx], ind=idx[:, bb * 64 + pg:bb * 64 + pg + 1])
                            S.dma("pool", "vg", vf[:, :], cv.ap()[:, :], W=[rvf], R=[r_idx], ind=idx[:, bb * 64 + pg:bb * 64 + pg + 1])
                        else:
                            S.dma("sp", "kg", kf[:, :], ckT.ap()[pg * 128:(pg + 1) * 128, :], W=[rkf])
                            S.dma("sp", "vg", vf[:, :], cv.ap()[pg * 128:(pg + 1) * 128, :], W=[rvf])
                        kb, rkb = kpb.get(); vb, rvb = vpb.get()
                        I("act", "activation", kb[:, :], kf[:, :], AF.Copy, R=[rkf], W=[rkb])
                        I("dve", "tensor_copy", vb[:, :, 0:128], vf[:, :].rearrange("p (j d) -> p j d", j=4), R=[rvf], W=[rvb])
                    ps, rps = dn()
                    for j in range(4):
                        for c in range(2):
                            lhs = ksm[c * 64:(c + 1) * 64, j, 8 * b2:8 * b2 + 8] if new else kb[c * 64:(c + 1) * 64, j * 128:(j + 1) * 128]
                            I("pe", "matmul", ps[0:L, c, j * 32:(j + 1) * 32], lhs, qsm[c * 64:(c + 1) * 64, b2, 4 * j:4 * j + 4, :].rearrange("p h t -> p (h t)"),
                              start=True, stop=True, R=[(r_ksm if new else rkb), r_qsm], W=[rps[c]], inc=(j == 3))
                    pt, rpt = smb.get()
                    I("act", "activation", pt[0:L, 0:256].rearrange("p (c n) -> p c n", c=2), ps[0:L, :, 0:128], AF.Exp, bias=cp[0:L, C_ONE + 3:C_ONE + 4], scale=0.125, R=[rps[0], rps[1], r_c], W=[rpt])
                    if new and not (SA_SKIP & 4):
                        I("dve", "tensor_tensor", pt[0:8, 0:256], pt[0:8, 0:256], t8[:, :], ALU.mult, R=[rpt, r_c], W=[rpt])
                    for j in range(4):
                        if SA_SKIP & 2: break
                        for c in range(2):
                            rhs = vsm[b2][:, j, 0:129] if new else vb[:, j, 0:129]
                            I("pe", "matmul", Oc[c][0:32, j // 2, (j % 2) * 130:(j % 2) * 130 + 129], pt[0:L, c * 128 + j * 32:c * 128 + (j + 1) * 32], rhs,
                              start=(pg == 0 and j % 2 == 0), stop=(new and j % 2 == 1), skip_group_check=True, R=[rpt, (r_vsm if new else rvb)], W=[r_PSD[2 + c][j // 2]], inc=(j == 3 and c == 1))
                if SA_CUT == 1: return
                for j in range(4):
                    O0 = Oc[0][:, j // 2, (j % 2) * 130:(j % 2) * 130 + 130]; O1 = Oc[1][:, j // 2, (j % 2) * 130:(j % 2) * 130 + 130]
                    rr = Res(); rr.w = None
                    ob, rob = attn_finish2(O0, O1, r_PSD[2][j // 2], r_PSD[3][j // 2])
                    if SA_CUT == 2: continue
                    pp_, rr_ = dn(); pp = pp_[:, 0, :]; rpp = rr_[0]
                    pv = pp.bitcast(BF16)
                    I("pe", "transpose", pv[:, 0:32], ob[0:32, 0:128], identb[0:32, 0:32], R=[rob, r_c], W=[rpp])
                    for hl in range(4):
                        I("act", "activation", mix[:, 4 * j + hl, TP + 8 * b2:TP + 8 * b2 + 8], pv[:, hl * 8:hl * 8 + 8], AF.Copy, R=[rpp], W=[r_mix[4 * j + hl]])

            def attn_finish2(O0, O1, rO0, rO1):
                M = 32
                cc, rcc = col.get()
                I("dve", "reciprocal", cc[0:M, 0:1], O0[0:M, 128:129], R=[rO0], W=[rcc])
                I("dve", "reciprocal", cc[0:M, 1:2], O1[0:M, 128:129], R=[rO1], W=[rcc])
                I("dve", "tensor_tensor", cc[0:M, 1:2], cc[0:M, 1:2], lamt[0:M, 0:1], ALU.mult, R=[rcc, r_lam], W=[rcc])
                o, ro = sm1.get()
                I("dve", "tensor_scalar", o[0:M, 0:128], O0[0:M, 0:128], cc[0:M, 0:1], None, ALU.mult, R=[rO0, rcc], W=[ro])
                I("dve", "scalar_tensor_tensor", o[0:M, 0:128], O1[0:M, 0:128], cc[0:M, 1:2], o[0:M, 0:128], ALU.mult, ALU.add, R=[rO1, rcc, ro], W=[ro])
                jk, rjk = sm2.get()
                I("act", "activation", jk[0:M, 0:128], o[0:M, 0:128], AF.Square, accum_out=cc[0:M, 2:3], R=[ro], W=[rjk, rcc])
                I("act", "activation", cc[0:M, 2:3], cc[0:M, 2:3], AF.Sqrt, bias=cp[0:M, C_ONE + 2:C_ONE + 3], scale=1.0 / 128, R=[rcc, r_c], W=[rcc])
                I("dve", "reciprocal", cc[0:M, 2:3], cc[0:M, 2:3], R=[rcc], W=[rcc])
                I("dve", "tensor_scalar", cc[0:M, 2:3], cc[0:M, 2:3], 1.0 - LAM_INIT, None, ALU.mult, R=[rcc], W=[rcc])
                ob, rob = smb.get()
                I("dve", "scalar_tensor_tensor", ob[0:M, 0:128], o[0:M, 0:128], cc[0:M, 2:3], rp[0:M, OFF_SUB:OFF_SUB + 128], ALU.mult, ALU.mult, R=[ro, rcc, r_c], W=[rob])
                return ob, rob

            def ssd_group(kind, p, g, segs):
                own = (kind == "own")
                A = rp[:, OFF_AL + 4 * g:OFF_AL + 4 * g + 4]
                for si, (c0, L) in enumerate(segs):
                    smp = (si >= NQT)
                    bb = NB2 * p + (si - NQT) if smp else None
                    if smp:
                        stt, rst = stg.get()
                        S.dma("sp", "stl", stt[:, :], st_ssm[bb][:, g * 256:(g + 1) * 256], W=[rst])
                        Sg = stt[:, :]
                    else:
                        Sg = ST[:, g * 256:(g + 1) * 256]; rst = r_ST[g]
                    dt4 = dtk[0:L, si, 4 * g:4 * g + 4]
                    xs, rxs = q_xs.get()
                    for il in range(2):
                        transpose_to(xs[0:L, il * 128:(il + 1) * 128], xsT[:, il, c0:c0 + L], r_xsT[il], rxs, 128, L, bf=True)
                    Bt, rBt = q_Bt.get()
                    transpose_to(Bt[0:L, 0:128], BT[:, c0:c0 + L], r_BT, rBt, 128, L, bf=True)
                    cc, rcc = col.get()
                    I("dve", "tensor_tensor", cc[0:L, 0:4], dt4, A[0:L, :], ALU.mult, R=[r_dtk, r_c], W=[rcc])
                    p1, rp1 = sc()
                    I("pe", "matmul", p1[0:L, 0:4], tri[0:L, 0:L], cc[0:L, 0:4], start=True, stop=True, R=[rcc, r_c], W=[rp1])
                    I("pe", "matmul", p1[:, 4:8], all_ones[0:L, :], cc[0:L, 0:4], start=True, stop=True, R=[rcc, r_c], W=[rp1])
                    I("dve", "tensor_copy", cc[0:L, 4:8], p1[0:L, 0:4], R=[rp1], W=[rcc])
                    I("dve", "tensor_copy", cc[:, 8:12], p1[:, 4:8], R=[rp1], W=[rcc])
                    I("dve", "tensor_tensor", cc[0:L, 12:16], cc[0:L, 8:12], cc[0:L, 4:8], ALU.subtract, R=[rcc], W=[rcc])
                    I("act", "activation", cc[0:L, 12:16], cc[0:L, 12:16], AF.Exp, R=[rcc], W=[rcc])
                    I("act", "activation", cc[:, 16:20], cc[:, 8:12], AF.Exp, R=[rcc], W=[rcc])
                    I("act", "activation", cc[0:L, 20:24], cc[0:L, 4:8], AF.Exp, R=[rcc], W=[rcc])
                    I("dve", "tensor_scalar", cc[0:L, 24:28], cc[0:L, 4:8], -1.0, None, ALU.mult, R=[rcc], W=[rcc])
                    xdt, rxdt = q_xdt.get(); xdte, rxdte = q_xdte.get()
                    for h4 in range(4):
                        I("dve", "tensor_scalar", xdt[0:L, h4 * 64:(h4 + 1) * 64], xs[0:L, h4 * 64:(h4 + 1) * 64], dt4[:, h4:h4 + 1], None, ALU.mult, R=[rxs, r_dtk], W=[rxdt])
                        I("dve", "tensor_scalar", xdte[0:L, h4 * 64:(h4 + 1) * 64], xdt[0:L, h4 * 64:(h4 + 1) * 64], cc[0:L, 12 + h4:13 + h4], None, ALU.mult, R=[rxdt, rcc], W=[rxdte])
                    if own:
                        Sb, rSb = q_Sb.get()
                        I("act", "activation", Sb[:, 0:256], Sg, AF.Copy, R=[rst], W=[rSb])
                        pyo, rpyo = sc()
                        I("pe", "matmul", pyo[0:L, 0:256], CTt[:, c0:c0 + L], Sb[:, 0:256], start=True, stop=True, R=[r_CT, rSb], W=[rpyo])
                        y, ry = q_y.get()
                        for h4 in range(4):
                            I("dve", "tensor_scalar", y[0:L, h4 * 64:(h4 + 1) * 64], pyo[0:L, h4 * 64:(h4 + 1) * 64], cc[0:L, 20 + h4:21 + h4], None, ALU.mult, R=[rpyo, rcc], W=[ry])
                        pcb, rpcb = sc()
                        I("pe", "matmul", pcb[0:L, 0:L], BT[:, c0:c0 + L], CTt[:, c0:c0 + L], start=True, stop=True, R=[r_BT, r_CT], W=[rpcb])
                        Dm, rDm = q_Dm.get()
                        for h4 in range(4):
                            I("dve", "tensor_scalar", Dm[0:L, h4 * 128:h4 * 128 + L], tri[0:L, 0:L], cc[0:L, h4:h4 + 1], None, ALU.mult, R=[rcc, r_c], W=[rDm])
                        pcr, rpcr = sc()
                        for h4 in range(4):
                            I("pe", "matmul", pcr[0:L, h4 * 128:h4 * 128 + L], all_ones[0:L, 0:L], Dm[0:L, h4 * 128:h4 * 128 + L], start=True, stop=True, R=[rDm, r_c], W=[rpcr], inc=(h4 == 3))
                        Lm, rLm = q_Lm.get()
                        Wm, rWm = q_Wm.get()
                        pyd, rpyd = sc()
                        for h4 in range(4):
                            I("dve", "tensor_tensor", Lm[0:L, h4 * 128:h4 * 128 + L], pcr[0:L, h4 * 128:h4 * 128 + L], negm[0:L, 0:L], ALU.add, R=[rpcr, r_c], W=[rLm])
                            I("act", "activation", Lm[0:L, h4 * 128:h4 * 128 + L], Lm[0:L, h4 * 128:h4 * 128 + L], AF.Exp, bias=cc[0:L, 24 + h4:25 + h4], scale=1.0, R=[rLm, rcc], W=[rLm])
                            I("dve", "tensor_tensor", Wm[0:L, h4 * 128:h4 * 128 + L], Lm[0:L, h4 * 128:h4 * 128 + L], pcb[0:L, 0:L], ALU.mult, R=[rLm, rpcb], W=[rWm])
                            I("pe", "matmul", pyd[0:L, h4 * 64:(h4 + 1) * 64], Wm[0:L, h4 * 128:h4 * 128 + L], xdt[0:L, h4 * 64:(h4 + 1) * 64], start=True, stop=True, R=[rWm, rxdt], W=[rpyd])
                        I("dve", "tensor_tensor", y[0:L, 0:256], y[0:L, 0:256], pyd[0:L, 0:256], ALU.add, R=[ry, rpyd], W=[ry])
                        for h4 in range(4):
                            I("dve", "scalar_tensor_tensor", y[0:L, h4 * 64:(h4 + 1) * 64], xs[0:L, h4 * 64:(h4 + 1) * 64], rp[0:L, OFF_DS + 4 * g + h4:OFF_DS + 4 * g + h4 + 1], y[0:L, h4 * 64:(h4 + 1) * 64], ALU.mult, ALU.add, R=[rxs, r_c, ry], W=[ry])
                        zz, rzz = q_zz.get()
                        for il in range(2):
                            pz, rpz = sc()
                            pzv = pz.bitcast(BF16)
                            I("pe", "transpose", pzv[0:L, 0:128], zT[:, il, c0:c0 + L], identb, R=[r_zT[il], r_c], W=[rpz])
                            I("act", "activation", zz[0:L, il * 128:(il + 1) * 128], pzv[0:L, 0:128], AF.Silu, R=[rpz], W=[rzz])
                        I("dve", "tensor_tensor", y[0:L, 0:256], y[0:L, 0:256], zz[0:L, 0:256], ALU.mult, R=[ry, rzz], W=[ry])
                        I("act", "activation", zz[0:L, 0:256], y[0:L, 0:256], AF.Square, accum_out=cc[0:L, 28:29], R=[ry], W=[rzz, rcc])
                        I("act", "activation", cc[0:L, 28:29], cc[0:L, 28:29], AF.Sqrt, bias=cp[0:L, C_ONE + 2:C_ONE + 3], scale=1.0 / 256, R=[rcc, r_c], W=[rcc])
                        I("dve", "reciprocal", cc[0:L, 28:29], cc[0:L, 28:29], R=[rcc], W=[rcc])
                        yb, ryb = q_yb.get()
                        I("dve", "tensor_scalar", yb[0:L, 0:256], y[0:L, 0:256], cc[0:L, 28:29], None, ALU.mult, R=[ry, rcc], W=[ryb])
                        for il in range(2):
                            transpose_to(mix[:, 16 + 2 * g + il, c0:c0 + L], yb[0:L, il * 128:(il + 1) * 128], ryb, r_mix[16 + 2 * g + il], L, 128, bf=True, scale=cp[:, C_SN + 2 * g + il:C_SN + 2 * g + il + 1])
                    pst, rpst = sc()
                    I("pe", "matmul", pst[:, 0:256], Bt[0:L, 0:128], xdte[0:L, 0:256], start=True, stop=True, R=[rBt, rxdte], W=[rpst])
                    for h4 in range(4):
                        I("dve", "scalar_tensor_tensor", Sg[:, h4 * 64:(h4 + 1) * 64], Sg[:, h4 * 64:(h4 + 1) * 64], cc[:, 16 + h4:17 + h4], pst[:, h4 * 64:(h4 + 1) * 64], ALU.mult, ALU.add, R=[rst, rcc, rpst], W=[rst])
                    if smp:
                        S.out_toks.append(S.dma("sp", "sto", ssm_smp[bb][:, g * 256:(g + 1) * 256], Sg, R=[rst]))

            for p in range(NP_CTX):
                run_pass("ctx", p)
            I("dve", "tensor_scalar", ST[:, :], ST[:, :], cp[:, C_FLAG:C_FLAG + 1], None, ALU.mult, R=r_ST + [r_c], W=r_ST)
            I("dve", "tensor_scalar", halo[:, :, :], halo[:, :, :], cp[:, C_FLAG:C_FLAG + 1], None, ALU.mult, R=r_halo + [r_c], W=r_halo)
            for p in range(NP_OWN):
                run_pass("own", p)
            S.out_toks.append(S.dma("sp", "fin", ssm_fin.ap(), ST[:, :], R=r_ST))
            S.out_toks.append(S.dma("sp", "fin", conv_fin.ap(), halo[:, :, :], R=r_halo))
            S.out_toks.append(S.dma("sp", "fin", conv_smp.ap(), convo[:, :, :, :], R=[r_convo]))
            for t in S.out_toks: S.wait_tok("sp", t)

        dry = Sched(nc, es, dry=True)
        emit(dry)
        ws.planned = True
        for k in cnt: cnt[k] = 0
        for obj in list(locals().values()):
            pass
        S = Sched(nc, es)
        _reset_all(locals())
        emit(S)
        S.build()
    return nc


def _reset_all(ns):
    def rs(o):
        if isinstance(o, Res): o.w = None; o.r = {}
        elif isinstance(o, Rot):
            o.i = 0
            for r in o.r: rs(r)
        elif isinstance(o, (list, tuple)):
            for x in o: rs(x)
    for v in ns.values(): rs(v)


def _rope_tab(pos):
    half = 8
    inv = np.exp(-math.log(500000.0) * np.arange(half, dtype=np.float32) * 2.0 / 16).astype(np.float32)
    ang = pos.astype(np.float32)[None, :] * inv[:, None]
    cos = np.ones((128, len(pos)), np.float32); sin = np.zeros((128, len(pos)), np.float32)
    for m in range(128):
        d = m % 64
        if d < 16:
            cos[m] = np.cos(ang[d % 8]); sin[m] = np.sin(ang[d % 8])
    return cos, sin


_PROG = None


def _prep(x_prompt, x_sample, cache_k, cache_v, state_ssm, state_conv, page_table,
          ln1_w, w_in, q_norm_w, k_norm_w, lambda_q1, lambda_k1, lambda_q2, lambda_k2,
          subln_w, conv_w, conv_b, dt_bias, a_log, d_skip, ssm_norm_w, w_out,
          ln2_w, w_up, w_down, cores=range(8)):
    f = lambda a: np.ascontiguousarray(np.asarray(a))
    x_prompt = f(x_prompt); x_sample = f(x_sample)
    w = np.zeros((D, 73 * 128), np.float32); w[:, :DIN] = np.asarray(w_in)[0]
    w_in_l = f(w.reshape(32, 128, 73, 128).transpose(2, 1, 0, 3))
    w_out_l = f(np.asarray(w_out)[0].reshape(32, 128, 32, 128).transpose(2, 1, 0, 3))
    w_up_l = f(np.asarray(w_up)[0].reshape(32, 128, 128, 128).transpose(2, 1, 0, 3))
    w_down_l = f(np.asarray(w_down)[0].reshape(4, 32, 128, 32, 128).transpose(3, 0, 2, 1, 4))
    cpk = np.zeros((128, 320), np.float32)
    cpk[:, 240:256] = np.asarray(ssm_norm_w)[0].reshape(16, 128).T
    cpk[:, 0:32] = np.asarray(ln1_w)[0].reshape(32, 128).T
    cpk[:, 32:64] = np.asarray(ln2_w)[0].reshape(32, 128).T
    cpk[:, 64:192] = np.asarray(conv_w)[0].reshape(4, 32, 128).transpose(2, 1, 0).reshape(128, 128)
    cpk[:, 192:224] = np.asarray(conv_b)[0].reshape(32, 128).T
    cpk[:, 224] = np.tile(np.asarray(q_norm_w)[0], 2); cpk[:, 225] = np.tile(np.asarray(k_norm_w)[0], 2)
    cpk[0:32, 226] = np.asarray(dt_bias)[0]
    cpk[:, 229] = 1.0; cpk[:, 230] = np.arange(128); cpk[:, 231] = EPS; cpk[:, 232] = 0.0
    cmat = np.zeros((128, 5, 128), np.float32)
    cmat[:, 0] = np.eye(128)
    cmat[:, 1] = np.triu(np.ones((128, 128)))
    cmat[:, 2] = np.kron(np.eye(2), np.ones((64, 64)))
    Rm = np.zeros((128, 128), np.float32)
    for m in range(128):
        d = m % 64
        if d < 8: Rm[m + 8, m] = -1.0
        elif d < 16: Rm[m - 8, m] = 1.0
    cmat[:, 3] = Rm
    cmat[:, 4] = (cmat[:, 1] - 1.0) * 30000.0
    tri8 = np.zeros((8, 256), np.float32)
    for s in range(8):
        for t in range(8):
            if s <= t: tri8[s, t::8] = 1.0
    rowp = np.concatenate([np.asarray(subln_w)[0], np.asarray(a_log)[0], np.asarray(d_skip)[0],
                           np.asarray(lambda_q1)[0], np.asarray(lambda_k1)[0], np.asarray(lambda_q2)[0], np.asarray(lambda_k2)[0]]).astype(np.float32)[None, :]
    ck = np.asarray(cache_k)[0]; cvv = np.asarray(cache_v)[0]
    ckT = f(ck.transpose(0, 3, 2, 1).reshape(2560 * 128, 512)) if False else f(ck.reshape(2560, 128, 4, 128).transpose(0, 3, 2, 1).reshape(2560 * 128, 512))
    cvf = f(cvv.reshape(2560 * 128, 512))
    pt = np.asarray(page_table).astype(np.int32)
    sssm = np.asarray(state_ssm)[0]; sconv = np.asarray(state_conv)[0]
    cs_smp, sn_smp = _rope_tab(PAST + np.arange(8))
    in_maps = []
    for c in cores:
        b, hf = c // 2, c % 2
        xo = f(x_prompt[b, hf * 1024:(hf + 1) * 1024].T)
        xc = f(x_prompt[b, 0:1024].T)
        xs = f(x_sample[4 * c:4 * c + 4].reshape(32, D).T)
        cpc = cpk.copy(); cpc[:, 227] = float(hf); cpc[:, 228] = 0.0 if hf else NEG
        ropes = np.zeros((128, 2, 2, 1032), np.float32)
        cc_, sc_ = _rope_tab(np.arange(1024)); ropes[:, 0, 0, :1024] = cc_; ropes[:, 1, 0, :1024] = sc_
        co_, so_ = _rope_tab(hf * 1024 + np.arange(1024)); ropes[:, 0, 1, :1024] = co_; ropes[:, 1, 1, :1024] = so_
        ropes[:, 0, 1, 1024:] = cs_smp; ropes[:, 1, 1, 1024:] = sn_smp
        stl = f(sssm[4 * c:4 * c + 4].reshape(4, 2048, 128).transpose(0, 2, 1))
        scl = f(sconv[4 * c:4 * c + 4].reshape(4, 3, 32, 128).transpose(3, 0, 2, 1))
        in_maps.append(dict(xT_own=xo, xT_ctx=xc, xT_smp=xs, ckT=ckT, cv=cvf, st_ssm=stl, st_conv=scl,
                            ptab=f(pt[4 * c:4 * c + 4].reshape(1, 256)), w_in=w_in_l, w_out=w_out_l, w_up=w_up_l, w_down=w_down_l,
                            cpk=cpc, cmat=cmat, ropes=ropes, rowp=rowp, tri8=tri8))
    return in_maps


def kernel(**inputs):
    global _PROG
    in_maps = _prep(**inputs)
    if _PROG is None:
        _PROG = build_program()
    res = run_bass_kernel_spmd(_PROG, in_maps, core_ids=list(range(8))).results
    return _post(res)


def _post(res, cores=range(8)):
    y_p = np.zeros((4, 2048, D), np.float32); y_s = np.zeros((32, 8, D), np.float32)
    k_p = np.zeros((1, 4, 2048, 4, 128), np.float32); v_p = np.zeros_like(k_p)
    s_p = np.zeros((1, 4, 32, 64, 128), np.float32); c_p = np.zeros((1, 4, 3, 4096), np.float32)
    k_s = np.zeros((1, 32, 8, 4, 128), np.float32); v_s = np.zeros_like(k_s)
    s_s = np.zeros((1, 32, 32, 64, 128), np.float32); c_s = np.zeros((1, 32, 3, 4096), np.float32)
    for ci_, c in enumerate(cores):
        r = res[ci_]; b, hf = c // 2, c % 2
        sl = slice(hf * 1024, (hf + 1) * 1024)
        y_p[b, sl] = r["yT_own"].T
        y_s[4 * c:4 * c + 4] = r["yT_smp"].T.reshape(4, 8, D)
        k_p[0, b, sl] = r["kT_own"].T.reshape(1024, 4, 128); v_p[0, b, sl] = r["vT_own"].T.reshape(1024, 4, 128)
        k_s[0, 4 * c:4 * c + 4] = r["kT_smp"].T.reshape(4, 8, 4, 128); v_s[0, 4 * c:4 * c + 4] = r["vT_smp"].T.reshape(4, 8, 4, 128)
        if hf == 1:
            s_p[0, b] = r["ssm_fin"].T.reshape(32, 64, 128)
            c_p[0, b] = r["conv_fin"].transpose(2, 1, 0).reshape(3, 4096)
        s_s[0, 4 * c:4 * c + 4] = r["ssm_smp"].transpose(0, 2, 1).reshape(4, 32, 64, 128)
        c_s[0, 4 * c:4 * c + 4] = r["conv_smp"].transpose(1, 3, 2, 0).reshape(4, 3, 4096)
    return (y_p, y_s, k_p, v_p, s_p, c_p, k_s, v_s, s_s, c_s)
```

```python
import math, contextlib
import numpy as np
import concourse.bass as bass
import concourse.mybir as mybir
from concourse.bass_utils import run_bass_kernel_spmd

F32 = mybir.dt.float32; BF16 = mybir.dt.bfloat16; I32 = mybir.dt.int32
AF = mybir.ActivationFunctionType; ALU = mybir.AluOpType

D = 4096; DIN = 9248; DFF = 16384; NCH = 32
TP = 256; TS = 8; NPASS = 4; NB2 = 1; NQT = TP // 128
EPS = 1e-6
LAM_INIT = 0.8 - 0.6 * math.exp(0.0)
PAST = 8192; NPG = 64
WCACHE = True
STAGE = 99; NP_CTX = 4; NP_OWN = 4; NPG_RUN = 64; GMODE = 0; SA_CUT = 0; SA_SKIP = 0
NEG = -30000.0


class Res:
    __slots__ = ("w", "r")
    def __init__(self):
        self.w = None; self.r = {}


class Sched:
    ENG = ("pe", "act", "dve", "pool", "sp")
    def __init__(self, nc, es, dry=False):
        self.nc = nc; self.es = es; self.dry = dry
        self.q = {e: [] for e in self.ENG}
        self.sem = {e: (None if dry else es.enter_context(nc.semaphore("sem_" + e))) for e in self.ENG}
        if dry:
            self.sem = {e: ("dry", e) for e in self.ENG}
        self.cnt = {e: 0 for e in self.ENG}
        self.seen = {e: {} for e in self.ENG}
        self.dsem = {}
        self.out_toks = []
    def _deps(self, eng, reads, writes):
        need = {}
        def add(t):
            if t is None: return
            k = id(t[0])
            if k not in need or need[k][1] < t[1]: need[k] = t
        for r in reads: add(r.w)
        for w in writes:
            add(w.w)
            for t in w.r.values(): add(t)
        own = self.sem[eng]
        for k, (s, v) in need.items():
            if eng == "pe" and s is own: continue
            if self.seen[eng].get(k, 0) >= v: continue
            self.seen[eng][k] = v
            self.q[eng].append(("wait", s, v))
    def _upd(self, tok, reads, writes):
        k = id(tok[0])
        for r in reads:
            if k not in r.r or r.r[k][1] < tok[1]: r.r[k] = tok
        for w in writes:
            w.w = tok; w.r = {}
    def op(self, eng, name, *a, R=(), W=(), inc=True, **kw):
        self._deps(eng, R, W)
        if inc:
            self.cnt[eng] += 1
            tok = (self.sem[eng], self.cnt[eng])
            self.q[eng].append(("op_inc", name, a, kw))
        else:
            tok = (self.sem[eng], self.cnt[eng] + 1)
            self.q[eng].append(("op", name, a, kw))
        self._upd(tok, R, W)
        return tok
    def dma(self, eng, semname, out, in_, R=(), W=(), ind=None):
        nsl = 1 if semname.startswith("w") else 3
        if semname not in self.dsem:
            self.dsem[semname] = [[[("dry", semname, i) if self.dry else self.es.enter_context(self.nc.semaphore("d_%s%d" % (semname, i))), 0] for i in range(nsl)], 0]
        pool = self.dsem[semname]
        ent = pool[0][pool[1] % nsl]; pool[1] += 1
        if ent[1] > 0: self.wait_tok(eng, (ent[0], ent[1]))
        self._deps(eng, R, W)
        ent[1] += 16
        tok = (ent[0], ent[1])
        self.q[eng].append(("dma", out, in_, ent[0], ind))
        self._upd(tok, R, W)
        return tok
    def wait_tok(self, eng, tok):
        s, v = tok
        if self.seen[eng].get(id(s), 0) >= v: return
        self.seen[eng][id(s)] = v
        self.q[eng].append(("wait", s, v))
    def build(self):
        nc = self.nc
        block = self.es.enter_context(nc.Block())
        sem = self.sem
        def run(e, ename, lst):
            for it in lst:
                k = it[0]
                if k == "wait": e.wait_ge(it[1], it[2])
                elif k == "op_inc": getattr(e, it[1])(*it[2], **it[3]).then_inc(sem[ename], 1)
                elif k == "op": getattr(e, it[1])(*it[2], **it[3])
                elif k == "dma":
                    if it[4] is None:
                        e.dma_start(out=it[1], in_=it[2]).then_inc(it[3], 16)
                    else:
                        e.indirect_dma_start(out=it[1], out_offset=None, in_=it[2],
                                             in_offset=bass.IndirectOffsetOnAxis(ap=it[4], axis=0)).then_inc(it[3], 16)
        q = self.q
        @block.tensor
        def _(e): run(e, "pe", q["pe"])
        @block.scalar
        def _(e): run(e, "act", q["act"])
        @block.vector
        def _(e): run(e, "dve", q["dve"])
        @block.gpsimd
        def _(e): run(e, "pool", q["pool"])
        @block.sync
        def _(e): run(e, "sp", q["sp"])


class Rot:
    def __init__(self, es, nc, name, shape, dtype, n):
        self.t = [es.enter_context(nc.sbuf_tensor(f"{name}{i}", shape, dtype)) for i in range(n)]
        self.r = [Res() for _ in range(n)]
        self.i = 0
    def get(self):
        i = self.i % len(self.t); self.i += 1
        return self.t[i], self.r[i]


class WStream:
    NB = 6
    def __init__(self, es, nc):
        self.buf = [es.enter_context(nc.sbuf_tensor(f"wbuf{i}", [128, 32, 128], BF16)) for i in range(self.NB)]
        self.plan = []; self.planned = False
    def start(self, S):
        self.S = S; self.res = [Res() for _ in range(self.NB)]
        self.ci = 0; self.ii = 0
        self.cached = {}
    def _issue(self):
        if self.ii >= len(self.plan): return
        key, src = self.plan[self.ii]
        b = self.ii % self.NB
        M = src.shape[2]
        if WCACHE and key in self.cached:
            self.S.dma("pool", f"w{b}", self.buf[b][:, :, 0:M], self.scr[key][:, :, 0:M], R=[self.cached[key]], W=[self.res[b]])
        else:
            self.S.dma("pool", f"w{b}", self.buf[b][:, :, 0:M], src, W=[self.res[b]])
            if WCACHE:
                kr = Res(); self.cached[key] = kr
                self.S.dma("sp", "wbk", self.scr[key][:, :, 0:M], self.buf[b][:, :, 0:M], R=[self.res[b]], W=[kr])
        self.ii += 1
    def get(self, key, src):
        if not self.planned:
            self.plan.append((key, src))
            b = len(self.plan) % self.NB
            return self.buf[b], self.res[b]
        while self.ii < min(self.ci + self.NB - 1, len(self.plan)):
            self._issue()
        if self.ii <= self.ci: self._issue()
        b = self.ci % self.NB; self.ci += 1
        return self.buf[b], self.res[b]


def build_program():
    nc = bass.Bass("TRN2", target_bir_lowering=False)
    def din(name, shape, dt=F32): return nc.dram_tensor(name, list(shape), dt, kind="ExternalInput")
    def dout(name, shape, dt=F32): return nc.dram_tensor(name, list(shape), dt, kind="ExternalOutput")
    xT_own = din("xT_own", [D, 1024]); xT_ctx = din("xT_ctx", [D, 1024]); xT_smp = din("xT_smp", [D, 32])
    ckT = din("ckT", [2560 * 128, 512]); cv = din("cv", [2560 * 128, 512])
    st_ssm = din("st_ssm", [4, 128, 2048]); st_conv = din("st_conv", [128, 4, 32, 3])
    ptab = din("ptab", [1, 256], I32)
    w_in = din("w_in", [73, 128, 32, 128]); w_out = din("w_out", [32, 128, 32, 128])
    w_up = din("w_up", [128, 128, 32, 128]); w_down = din("w_down", [32, 4, 128, 32, 128])
    cpk = din("cpk", [128, 320])
    cmat = din("cmat", [128, 5, 128])
    ropes = din("ropes", [128, 2, 2, 1024 + 8])
    rowp = din("rowp", [1, 128 + 32 + 32 + 256])
    tri8 = din("tri8", [8, 256])
    yT_own = dout("yT_own", [D, 1024]); yT_smp = dout("yT_smp", [D, 32])
    kT_own = dout("kT_own", [512, 1024]); vT_own = dout("vT_own", [512, 1024])
    kT_smp = dout("kT_smp", [512, 32]); vT_smp = dout("vT_smp", [512, 32])
    ssm_fin = dout("ssm_fin", [128, 2048]); conv_fin = dout("conv_fin", [128, 32, 3])
    ssm_smp = dout("ssm_smp", [4, 128, 2048]); conv_smp = dout("conv_smp", [128, 4, 32, 3])
    dbg_mix = dout("dbg_mix", [4, 128, 32, TP + TS], BF16) if STAGE == 5 else None
    if STAGE == 5:
        dbg_dtk = dout("dbg_dtk", [128, NQT + NB2, 32]); dbg_xs = dout("dbg_xs", [128, 2, TP + TS], BF16)
        dbg_B = dout("dbg_B", [128, TP + TS], BF16); dbg_C = dout("dbg_C", [128, TP + TS], BF16)

    es = contextlib.ExitStack()
    with es:
        def sb(name, shape, dt=F32): return es.enter_context(nc.sbuf_tensor(name, list(shape), dt))
        T = TP + TS
        xn = sb("xn", [128, NCH, T], BF16); r_xn = [Res() for _ in range(NCH)]
        mix = sb("mix", [128, NCH, T], BF16); r_mix = [Res() for _ in range(NCH)]
        KT = sb("KT", [128, 4, 2048], BF16); r_KT = [Res() for _ in range(4)]
        VA = sb("VA", [128, 16, 4, 130], BF16); r_VA = [Res() for _ in range(4)]
        ST = sb("ST", [128, 2048]); r_ST = [Res() for _ in range(8)]
        halo = sb("halo", [128, 32, 3]); r_halo = [Res() for _ in range(32)]
        convo = sb("convo", [128, 4, 32, 3]); r_convo = Res()
        cp = sb("cp", [128, 320]); r_c = Res()
        cm = sb("cm", [128, 5, 128]); cmb = sb("cmb", [128, 2, 128], BF16)
        rp = sb("rp", [128, 128 + 32 + 32 + 256])
        t8 = sb("t8", [8, 256], BF16); t8f = sb("t8f", [8, 256])
        rope = sb("rope", [128, 2, T]); r_rope = Res()
        rstd = sb("rstd", [128, T]); r_rstd = Res()
        rstd2 = sb("rstd2", [128, T]); r_rstd2 = Res()
        qT = sb("qT", [128, 4, TP], BF16); r_qT = [Res() for _ in range(4)]
        qsm = sb("qsm", [128, NB2, 16, 8], BF16); r_qsm = Res()
        ksm = sb("ksm", [128, 4, TS], BF16); r_ksm = Res()
        vsm = [sb(f"vsm{i}", [8, 4, 130], BF16) for i in range(NB2)]; r_vsm = Res()
        zT = sb("zT", [128, 2, T], BF16); r_zT = [Res(), Res()]
        xsT = sb("xsT", [128, 2, T], BF16); r_xsT = [Res(), Res()]
        BT = sb("BT", [128, T], BF16); r_BT = Res()
        CTt = sb("CTt", [128, T], BF16); r_CT = Res()
        dtT = sb("dtT", [32, T]); r_dtT = Res()
        dtk = sb("dtk", [128, NQT + NB2, 32]); r_dtk = Res()
        lamt = sb("lamt", [128, 4]); r_lam = Res()
        idx = sb("idx", [128, 256], I32); idxf = sb("idxf", [128, 256]); r_idx = Res()
        smacc = sb("smacc", [128, NCH, TS]); r_smacc = Res()
        ones1 = sb("ones1", [128, 1])
        ws = WStream(es, nc)
        _scr = [nc.dram_tensor("wscr_in", [73, 128, 32, 128], BF16, kind="Internal"), nc.dram_tensor("wscr_out", [32, 128, 32, 128], BF16, kind="Internal"),
                nc.dram_tensor("wscr_up", [128, 128, 32, 128], BF16, kind="Internal"), nc.dram_tensor("wscr_dn", [128, 128, 32, 128], BF16, kind="Internal")]
        class _Scr:
            def __getitem__(self, k):
                if k < 73: return _scr[0][k]
                if k < 105: return _scr[1][k - 73]
                if k < 233: return _scr[2][k - 105]
                return _scr[3][k - 233]
        ws.scr = _Scr()
        tmpA = Rot(es, nc, "tmpA", [128, T], F32, 3)
        tmpB = Rot(es, nc, "tmpB", [128, T], F32, 3)
        tmpC = Rot(es, nc, "tmpC", [128, T], F32, 2)
        xpb = Rot(es, nc, "xpb", [128, 3 + TP + NB2 * 11], F32, 2)
        pT = Rot(es, nc, "pT", [128, 2, TP], BF16, 2)
        sm1 = Rot(es, nc, "sm1", [128, 512], F32, 2)
        sm2 = Rot(es, nc, "sm2", [128, 512], F32, 2)
        smb = Rot(es, nc, "smb", [128, 512], BF16, 4)
        col = Rot(es, nc, "col", [128, 64], F32, 6)
        kpg = Rot(es, nc, "kpg", [128, 512], F32, 2); vpg = Rot(es, nc, "vpg", [128, 512], F32, 2)
        kpb = Rot(es, nc, "kpb", [128, 512], BF16, 2); vpb = Rot(es, nc, "vpb", [128, 4, 130], BF16, 2)
        stg = Rot(es, nc, "stg", [128, 256], F32, 2)
        q_xs = Rot(es, nc, "q_xs", [128, 256], F32, 2); q_Bt = Rot(es, nc, "q_Bt", [128, 128], BF16, 2)
        q_xdt = Rot(es, nc, "q_xdt", [128, 256], BF16, 2); q_xdte = Rot(es, nc, "q_xdte", [128, 256], BF16, 2)
        q_Sb = Rot(es, nc, "q_Sb", [128, 256], BF16, 1); q_y = Rot(es, nc, "q_y", [128, 256], F32, 1)
        q_Dm = Rot(es, nc, "q_Dm", [128, 512], F32, 1); q_Lm = Rot(es, nc, "q_Lm", [128, 512], F32, 1)
        q_Wm = Rot(es, nc, "q_Wm", [128, 512], BF16, 1); q_zz = Rot(es, nc, "q_zz", [128, 256], F32, 1)
        q_yb = Rot(es, nc, "q_yb", [128, 256], BF16, 1)
        PSD = [es.enter_context(nc.psum_tensor(f"psd{i}", [128, 2, 512], F32)) for i in range(4)]
        r_PSD = [[Res(), Res()] for _ in range(4)]
        cnt = {"dn": 0, "sc": 0}
        def dn():
            i = cnt["dn"] % 2; cnt["dn"] += 1
            return PSD[i], r_PSD[i]
        def sc():
            i = cnt["sc"] % 4; cnt["sc"] += 1
            return PSD[2 + i // 2][:, i % 2, :], r_PSD[2 + i // 2][i % 2]

        ident = cm[:, 0, :]; tri = cm[:, 1, :]; bones = cm[:, 2, :]; Rm = cm[:, 3, :]; negm = cm[:, 4, :]
        identb = cmb[:, 0, :]; trib = cmb[:, 1, :]
        C_LN1 = 0; C_LN2 = 32; C_CW = 64; C_CB = 192; C_QW = 224; C_KW = 225; C_DTB = 226; C_FLAG = 227; C_CBIAS = 228; C_ONE = 229
        OFF_SUB = 0; OFF_AL = 128; OFF_DS = 160; OFF_LAM = 192; C_SN = 240
        all_ones = sb("all_ones", [128, 128])

        def emit(S):
            ws.start(S)
            I = S.op
            S.dma("sp", "c0", cp[:], cpk.ap(), W=[r_c]); S.dma("sp", "c0", cm[:], cmat.ap(), W=[r_c])
            S.dma("sp", "c0", rp[:], rowp.ap().partition_broadcast(128), W=[r_c])
            S.dma("sp", "c0", t8f[:], tri8.ap(), W=[r_c])
            S.dma("sp", "c0", idx[:], ptab.ap().partition_broadcast(128), W=[r_idx])
            S.dma("sp", "c0", convo[:], st_conv.ap(), W=[r_convo])
            I("dve", "tensor_copy", cmb[:, 0, :], cm[:, 0, :], R=[r_c], W=[r_c])
            I("dve", "tensor_copy", cmb[:, 1, :], cm[:, 1, :], R=[r_c], W=[r_c])
            I("dve", "tensor_copy", t8[:], t8f[:], R=[r_c], W=[r_c])
            I("dve", "memset", all_ones[:], 1.0, W=[r_c])
            I("dve", "memset", ST[:], 0.0, W=r_ST)
            I("dve", "memset", halo[:], 0.0, W=r_halo)
            I("dve", "memset", VA[:], 1.0, W=r_VA)
            I("dve", "memset", KT[:], 0.0, W=r_KT)
            for v in vsm: I("dve", "memset", v[:], 1.0, W=[r_vsm])
            for rr in (vpb.t): pass
            for i in range(2): I("dve", "memset", vpb.t[i][:], 1.0, W=[vpb.r[i]])
            I("act", "activation", rp[:, OFF_AL:OFF_AL + 32], rp[:, OFF_AL:OFF_AL + 32], AF.Exp, R=[r_c], W=[r_c])
            I("dve", "tensor_scalar", rp[:, OFF_AL:OFF_AL + 32], rp[:, OFF_AL:OFF_AL + 32], -1.0, None, ALU.mult, R=[r_c], W=[r_c])
            L0 = OFF_LAM
            I("dve", "tensor_tensor", lamt[:, 0:1].to_broadcast([128, 1]) if False else rp[:, L0:L0 + 64], rp[:, L0:L0 + 64], rp[:, L0 + 64:L0 + 128], ALU.mult, R=[r_c], W=[r_c])
            I("dve", "tensor_tensor", rp[:, L0 + 128:L0 + 192], rp[:, L0 + 128:L0 + 192], rp[:, L0 + 192:L0 + 256], ALU.mult, R=[r_c], W=[r_c])
            I("dve", "reduce_sum", lamt[:, 1:2], rp[:, L0:L0 + 64], mybir.AxisListType.X, R=[r_c], W=[r_lam])
            I("dve", "reduce_sum", lamt[:, 2:3], rp[:, L0 + 128:L0 + 192], mybir.AxisListType.X, R=[r_c], W=[r_lam])
            I("act", "activation", lamt[:, 1:3], lamt[:, 1:3], AF.Exp, R=[r_lam], W=[r_lam])
            I("dve", "tensor_tensor", lamt[:, 0:1], lamt[:, 2:3], lamt[:, 1:2], ALU.subtract, R=[r_lam], W=[r_lam])
            I("dve", "tensor_scalar", lamt[:, 0:1], lamt[:, 0:1], -LAM_INIT, None, ALU.add, R=[r_lam], W=[r_lam])
            I("dve", "tensor_copy", idxf[:], idx[:], R=[r_idx], W=[r_idx])
            I("dve", "tensor_scalar", idxf[:], idxf[:], 128.0, cp[:, C_ONE + 1:C_ONE + 2], ALU.mult, ALU.add, R=[r_idx, r_c], W=[r_idx])
            I("dve", "tensor_copy", idx[:], idxf[:], R=[r_idx], W=[r_idx])
            I("dve", "memset", smacc[:], 0.0, W=[r_smacc])

            def load_x(src_list, Tn, lnc):
                pss, rps = sc()
                pss2, rps2 = sc()
                for c in range(NCH):
                    xs_, rx = tmpA.get()
                    for (src, c0, n) in src_list:
                        S.dma("sp", "xld", xs_[:, c0:c0 + n], src[c * 128:(c + 1) * 128, :], W=[rx])
                    I("dve", "tensor_scalar", xn[:, c, 0:Tn], xs_[:, 0:Tn], cp[:, lnc + c:lnc + c + 1], None, ALU.mult, R=[rx, r_c], W=[r_xn[c]])
                    sq, rq = tmpB.get()
                    I("act", "activation", sq[:, 0:Tn], xs_[:, 0:Tn], AF.Square, R=[rx], W=[rq])
                    I("pe", "matmul", pss[:, 0:min(Tn, 512)], all_ones[:], sq[:, 0:min(Tn, 512)], start=(c == 0), stop=(c == NCH - 1), R=[rq, r_c], W=[rps], inc=(Tn <= 512))
                    if Tn > 512:
                        I("pe", "matmul", pss2[:, 0:Tn - 512], all_ones[:], sq[:, 512:Tn], start=(c == 0), stop=(c == NCH - 1), R=[rq, r_c], W=[rps2])
                fin_rstd(rstd, r_rstd, pss, rps, pss2, rps2, Tn)

            def fin_rstd(dst, rdst, pss, rps, pss2, rps2, Tn):
                I("act", "activation", dst[:, 0:min(Tn, 512)], pss[:, 0:min(Tn, 512)], AF.Sqrt, bias=cp[:, C_ONE + 2:C_ONE + 3], scale=1.0 / D, R=[rps, r_c], W=[rdst])
                if Tn > 512:
                    I("act", "activation", dst[:, 512:Tn], pss2[:, 0:Tn - 512], AF.Sqrt, bias=cp[:, C_ONE + 2:C_ONE + 3], scale=1.0 / D, R=[rps2, r_c], W=[rdst])
                I("dve", "reciprocal", dst[:, 0:Tn], dst[:, 0:Tn], R=[rdst], W=[rdst])

            def dense(srcs, rhsbuf, r_rhs, Tn, M=128):
                ps, rps = dn()
                nb = len(srcs)
                for bi, (key, src) in enumerate(srcs):
                    wb, wr = ws.get(key, src)
                    for k in range(32):
                        kc = bi * 32 + k
                        st = (kc == 0); sp = (kc == nb * 32 - 1)
                        I("pe", "matmul", ps[0:M, 0, 0:min(Tn, 512)], wb[:, k, 0:M], rhsbuf[:, kc, 0:min(Tn, 512)], start=st, stop=sp,
                          R=[wr, r_rhs[kc]], W=[rps[0]], inc=(sp and Tn <= 512))
                        if Tn > 512:
                            I("pe", "matmul", ps[0:M, 1, 0:Tn - 512], wb[:, k, 0:M], rhsbuf[:, kc, 512:Tn], start=st, stop=sp,
                              R=[wr, r_rhs[kc]], W=[rps[1]], inc=sp)
                return ps, rps

            def halves(Tn):
                hs = [(0, 0, min(Tn, 512))]
                if Tn > 512: hs.append((1, 512, Tn - 512))
                return hs

            def scaled(ps, rps, Tn, M=128):
                a, ra = tmpA.get()
                for (h, c0, n) in halves(Tn):
                    I("dve", "tensor_tensor", a[0:M, c0:c0 + n], ps[0:M, h, 0:n], rstd[0:M, c0:c0 + n], ALU.mult, R=[rps[h], r_rstd], W=[ra])
                return a, ra

            def norm_rope(a, ra, Tn, wc):
                o, ro = tmpC.get()
                for (h, c0, n) in halves(Tn):
                    sq, rq = tmpB.get()
                    I("act", "activation", sq[:, 0:n], a[:, c0:c0 + n], AF.Square, R=[ra], W=[rq])
                    p1, rp1 = sc()
                    I("pe", "matmul", p1[:, 0:n], bones, sq[:, 0:n], start=True, stop=True, R=[rq, r_c], W=[rp1])
                    I("act", "activation", sq[:, 0:n], p1[:, 0:n], AF.Sqrt, bias=cp[:, C_ONE + 2:C_ONE + 3], scale=1.0 / 64, R=[rp1, r_c], W=[rq])
                    I("dve", "reciprocal", sq[:, 0:n], sq[:, 0:n], R=[rq], W=[rq])
                    I("dve", "scalar_tensor_tensor", sq[:, 0:n], a[:, c0:c0 + n], cp[:, wc:wc + 1], sq[:, 0:n], ALU.mult, ALU.mult, R=[ra, rq, r_c], W=[rq])
                    p2, rp2 = sc()
                    I("pe", "matmul", p2[:, 0:n], Rm, sq[:, 0:n], start=True, stop=True, R=[rq, r_c], W=[rp2])
                    I("dve", "tensor_tensor", o[:, c0:c0 + n], p2[:, 0:n], rope[:, 1, c0:c0 + n], ALU.mult, R=[rp2, r_rope], W=[ro])
                    I("dve", "tensor_tensor", sq[:, 0:n], sq[:, 0:n], rope[:, 0, c0:c0 + n], ALU.mult, R=[rq, r_rope], W=[rq])
                    I("dve", "tensor_tensor", o[:, c0:c0 + n], o[:, c0:c0 + n], sq[:, 0:n], ALU.add, R=[rq, ro], W=[ro])
                return o, ro

            def transpose_to(dst_ap, src_ap, rsrc, rdst, npart_in, nfree_in, bf=False, usedn=False, scale=None):
                if usedn:
                    pp_, rr_ = dn(); p = pp_[:, 0, :]; rpp = rr_[0]
                else:
                    p, rpp = sc()
                kw = {} if scale is None else {"scale": scale}
                if bf:
                    pv = p.bitcast(BF16)
                    I("pe", "transpose", pv[0:nfree_in, 0:npart_in], src_ap, identb[0:npart_in, 0:npart_in], R=[rsrc, r_c], W=[rpp])
                    I("act", "activation", dst_ap, pv[0:nfree_in, 0:npart_in], AF.Copy, R=[rpp, r_c], W=[rdst], **kw)
                else:
                    I("pe", "transpose", p[0:nfree_in, 0:npart_in], src_ap, ident[0:npart_in, 0:npart_in], R=[rsrc, r_c], W=[rpp])
                    I("act", "activation", dst_ap, p[0:nfree_in, 0:npart_in], AF.Copy, R=[rpp], W=[rdst])

            def run_pass(kind, p):
                own = (kind == "own")
                Tn = T if own else TP
                xsrc = xT_own if own else xT_ctx
                tok0 = p * TP
                kpos0 = (1024 if own else 0) + tok0
                srcl = [(xsrc[:, tok0:tok0 + TP], 0, TP)]
                if own: srcl.append((xT_smp[:, p * TS:(p + 1) * TS], TP, TS))
                S.dma("sp", "rope", rope[:, :, 0:TP], ropes[:, :, 1 if own else 0, tok0:tok0 + TP], W=[r_rope])
                if own:
                    for b2 in range(NB2):
                        S.dma("sp", "rope", rope[:, :, TP + 8 * b2:TP + 8 * b2 + 8], ropes[:, :, 1, 1024:1032], W=[r_rope])
                load_x(srcl, Tn, C_LN1)

                ps, rps = dense([(72, w_in[72][:, :, 0:32])], xn, r_xn, Tn, M=32)
                a, ra = scaled(ps, rps, Tn, M=32)
                I("act", "activation", a[0:32, 0:Tn], a[0:32, 0:Tn], AF.Exp, bias=cp[0:32, C_DTB:C_DTB + 1], scale=1.0, R=[ra, r_c], W=[ra])
                I("act", "activation", dtT[:, 0:Tn], a[0:32, 0:Tn], AF.Ln, bias=cp[0:32, C_ONE:C_ONE + 1], scale=1.0, R=[ra, r_c], W=[r_dtT])
                nchunk = 4
                segs = [(i * 128, 128) for i in range(NQT)] + ([(TP + 8 * b_, 8) for b_ in range(NB2)] if own else [])
                for si, (c0, L) in enumerate(segs):
                    transpose_to(dtk[0:L, si, :], dtT[:, c0:c0 + L], r_dtT, r_dtk, 32, L)

                if STAGE <= 1: return
                for j in range(4):
                    ps, rps = dense([(16 + j, w_in[16 + j])], xn, r_xn, Tn)
                    a, ra = scaled(ps, rps, Tn)
                    o, ro = norm_rope(a, ra, Tn, C_KW)
                    I("act", "activation", KT[:, j, kpos0:kpos0 + TP], o[:, 0:TP], AF.Copy, R=[ro], W=[r_KT[j]])
                    if own:
                        I("act", "activation", ksm[:, j, :], o[:, TP:T], AF.Copy, R=[ro], W=[r_ksm])
                        S.out_toks.append(S.dma("sp", "ko", kT_own[j * 128:(j + 1) * 128, tok0:tok0 + TP], o[:, 0:TP], R=[ro]))
                        S.out_toks.append(S.dma("sp", "ko", kT_smp[j * 128:(j + 1) * 128, p * TS:(p + 1) * TS], o[:, TP:T], R=[ro]))
                    ps, rps = dense([(20 + j, w_in[20 + j])], xn, r_xn, Tn)
                    a, ra = scaled(ps, rps, Tn)
                    if own:
                        S.out_toks.append(S.dma("sp", "vo", vT_own[j * 128:(j + 1) * 128, tok0:tok0 + TP], a[:, 0:TP], R=[ra]))
                        S.out_toks.append(S.dma("sp", "vo", vT_smp[j * 128:(j + 1) * 128, p * TS:(p + 1) * TS], a[:, TP:T], R=[ra]))
                    for tt in range(NQT):
                        transpose_to(VA[:, kpos0 // 128 + tt, j, 0:128], a[:, tt * 128:(tt + 1) * 128], ra, r_VA[j], 128, 128)
                    if own:
                        for b2 in range(NB2):
                            transpose_to(vsm[b2][:, j, 0:128], a[:, TP + 8 * b2:TP + 8 * b2 + 8], ra, r_vsm, 128, 8)
                    if not own or STAGE <= 2: continue
                    for hl in range(4):
                        hh = 4 * j + hl
                        ps, rps = dense([(hh, w_in[hh])], xn, r_xn, Tn)
                        a, ra = scaled(ps, rps, Tn)
                        o, ro = norm_rope(a, ra, Tn, C_QW)
                        I("act", "activation", qT[:, hl, :], o[:, 0:TP], AF.Copy, R=[ro], W=[r_qT[hl]])
                        for b2 in range(NB2): I("act", "activation", qsm[:, b2, hh, :], o[:, TP + 8 * b2:TP + 8 * b2 + 8], AF.Copy, R=[ro], W=[r_qsm])
                    nst = kpos0 // 128 + NQT
                    for hl in range(4):
                        hh = 4 * j + hl
                        Oa = [PSD[2][:, 0, :], PSD[2][:, 1, :], PSD[3][:, 0, :], PSD[3][:, 1, :]]
                        rOa = [r_PSD[2][0], r_PSD[2][1], r_PSD[3][0], r_PSD[3][1]]
                        for st in range(nst):
                            oi = st - kpos0 // 128
                            q0 = max(oi, 0) * 128
                            nq = TP - q0
                            ps, rps = dn()
                            for c in range(2):
                                I("pe", "matmul", ps[:, c, 0:nq], KT[c * 64:(c + 1) * 64, j, st * 128:(st + 1) * 128], qT[c * 64:(c + 1) * 64, hl, q0:TP],
                                  start=True, stop=True, R=[r_KT[j], r_qT[hl]], W=[rps[c]])
                            pt, rpt = pT.get()
                            bias_ap = cp[:, C_CBIAS:C_CBIAS + 1] if st < 8 else cp[:, C_ONE + 3:C_ONE + 4]
                            for c in range(2):
                                I("act", "activation", pt[:, c, 0:nq], ps[:, c, 0:nq], AF.Exp, bias=bias_ap, scale=0.125, R=[rps[c], r_c], W=[rpt])
                            if oi >= 0:
                                for c in range(2):
                                    I("dve", "tensor_tensor", pt[:, c, 0:128], pt[:, c, 0:128], trib, ALU.mult, R=[rpt, r_c], W=[rpt])
                            for qi in range(max(oi, 0), NQT):
                                last = (st == kpos0 // 128 + qi)
                                for c in range(2):
                                    I("pe", "matmul", Oa[qi][:, c * 130:c * 130 + 129], pt[:, c, qi * 128 - q0:(qi + 1) * 128 - q0], VA[:, st, j, 0:129],
                                      start=(st == 0 and c == 0), stop=(last and c == 1), skip_group_check=True, R=[rpt, r_VA[j]], W=[rOa[qi]], inc=(c == 1))
                        for qi in range(NQT):
                            attn_finish(Oa[qi][:, 0:130], Oa[qi][:, 130:260], rOa[qi], 128, mix[:, hh, qi * 128:(qi + 1) * 128], r_mix[hh])

                if own and STAGE >= 4:
                    for b2 in range(NB2):
                        sample_attention(p, b2)
                if STAGE <= 4: return

                for g in range(8):
                    tiles = [("xs", 0, 40 + 2 * g), ("xs", 1, 41 + 2 * g), ("B", 0, 56 + g)]
                    tiles += [("C", 0, 64 + g)]
                    if own: tiles += [("z", 0, 24 + 2 * g), ("z", 1, 25 + 2 * g)]
                    for (kd, il, wt) in tiles:
                        ps, rps = dense([(wt, w_in[wt])], xn, r_xn, Tn)
                        if kd == "z":
                            for (h, c0, n) in halves(Tn):
                                I("dve", "tensor_tensor", zT[:, il, c0:c0 + n], ps[:, h, 0:n], rstd[:, c0:c0 + n], ALU.mult, R=[rps[h], r_rstd], W=[r_zT[il]])
                            continue
                        ci = wt - 40
                        xp, rxp = xpb.get()
                        I("dve", "tensor_copy", xp[:, 0:3], halo[:, ci, :], R=[r_halo[ci]], W=[rxp])
                        I("dve", "tensor_tensor", xp[:, 3:3 + TP], ps[:, 0, 0:TP], rstd[:, 0:TP], ALU.mult, R=[rps[0], r_rstd], W=[rxp])
                        I("dve", "tensor_copy", halo[:, ci, :], xp[:, TP:TP + 3], R=[rxp], W=[r_halo[ci]])
                        dst = {"xs": xsT[:, il, :], "B": BT, "C": CTt}[kd]
                        rd = {"xs": r_xsT[il], "B": r_BT, "C": r_CT}[kd]
                        def conv(o0, n, d0):
                            acc, racc = tmpB.get()
                            I("dve", "tensor_scalar", acc[:, 0:n], xp[:, o0:o0 + n], cp[:, C_CW + ci * 4:C_CW + ci * 4 + 1], cp[:, C_CB + ci:C_CB + ci + 1], ALU.mult, ALU.add, R=[rxp, r_c], W=[racc])
                            for k in range(1, 4):
                                I("dve", "scalar_tensor_tensor", acc[:, 0:n], xp[:, o0 + k:o0 + k + n], cp[:, C_CW + ci * 4 + k:C_CW + ci * 4 + k + 1], acc[:, 0:n], ALU.mult, ALU.add, R=[rxp, racc, r_c], W=[racc])
                            I("act", "activation", dst[:, d0:d0 + n], acc[:, 0:n], AF.Silu, R=[racc], W=[rd])
                        conv(0, TP, 0)
                        if own:
                            for b2 in range(NB2):
                                bb = NB2 * p + b2
                                o0 = 3 + TP + 11 * b2
                                I("dve", "tensor_copy", xp[:, o0:o0 + 3], convo[:, bb, ci, :], R=[r_convo], W=[rxp])
                                I("dve", "tensor_tensor", xp[:, o0 + 3:o0 + 11], ps[:, 0, TP + 8 * b2:TP + 8 * b2 + 8], rstd[:, TP + 8 * b2:TP + 8 * b2 + 8], ALU.mult, R=[rps[0], r_rstd], W=[rxp])
                                I("dve", "tensor_copy", convo[:, bb, ci, :], xp[:, o0 + 8:o0 + 11], R=[rxp], W=[r_convo])
                                conv(o0, 8, TP + 8 * b2)
                    ssd_group(kind, p, g, segs)

                if not own: return
                if STAGE == 5:
                    S.out_toks.append(S.dma("sp", "dbg", dbg_mix[p], mix[:, :, :], R=r_mix))
                    S.out_toks.append(S.dma("sp", "dbg", dbg_dtk.ap(), dtk[:, :, :], R=[r_dtk]))
                    S.out_toks.append(S.dma("sp", "dbg", dbg_xs.ap(), xsT[:, :, :], R=r_xsT))
                    S.out_toks.append(S.dma("sp", "dbg", dbg_B.ap(), BT[:, :], R=[r_BT]))
                    S.out_toks.append(S.dma("sp", "dbg", dbg_C.ap(), CTt[:, :], R=[r_CT]))
                    return
                pss, rpsa = sc(); pss2, rpsb = sc()
                for ct in range(32):
                    ps, rps = dense([(73 + ct, w_out[ct])], mix, r_mix, T)
                    xr, rxr = tmpA.get()
                    S.dma("sp", "xld", xr[:, 0:TP], xT_own[ct * 128:(ct + 1) * 128, tok0:tok0 + TP], W=[rxr])
                    S.dma("sp", "xld", xr[:, TP:T], xT_smp[ct * 128:(ct + 1) * 128, p * TS:(p + 1) * TS], W=[rxr])
                    for (h, c0, n) in halves(T):
                        I("dve", "tensor_tensor", xr[:, c0:c0 + n], xr[:, c0:c0 + n], ps[:, h, 0:n], ALU.add, R=[rxr, rps[h]], W=[rxr])
                    S.dma("sp", "yb", yT_own[ct * 128:(ct + 1) * 128, tok0:tok0 + TP], xr[:, 0:TP], R=[rxr], W=[r_yb[ct]])
                    I("act", "activation", smacc[:, ct, :], xr[:, TP:T], AF.Copy, R=[rxr], W=[r_smacc])
                    I("dve", "tensor_scalar", xn[:, ct, :], xr[:, :], cp[:, C_LN2 + ct:C_LN2 + ct + 1], None, ALU.mult, R=[rxr, r_c], W=[r_xn[ct]])
                    sq, rq = tmpB.get()
                    I("act", "activation", sq[:, :], xr[:, :], AF.Square, R=[rxr], W=[rq])
                    I("pe", "matmul", pss[:, 0:T], all_ones[:], sq[:, 0:T], start=(ct == 0), stop=(ct == 31), R=[rq, r_c], W=[rpsa])
                fin_rstd(rstd2, r_rstd2, pss, rpsa, pss2, rpsb, T)
                if STAGE <= 6:
                    S.out_toks.append(S.dma("sp", "yso", yT_smp.ap().rearrange("(c k) t -> k c t", k=128)[:, :, p * TS:(p + 1) * TS], smacc[:], R=[r_smacc]))
                    return
                for fc in range(4):
                    for i in range(32):
                        ps, rps = dense([(105 + fc * 32 + i, w_up[fc * 32 + i])], xn, r_xn, T)
                        t1, rt1 = tmpA.get()
                        for (h, c0, n) in halves(T):
                            I("dve", "tensor_tensor", t1[:, c0:c0 + n], ps[:, h, 0:n], rstd2[:, c0:c0 + n], ALU.mult, R=[rps[h], r_rstd2], W=[rt1])
                        I("dve", "scalar_tensor_tensor", mix[:, i, :], t1[:, :], 0.0, t1[:, :], ALU.max, ALU.mult, R=[rt1], W=[r_mix[i]])
                    for ct in range(32):
                        ps, rps = dense([(233 + ct * 4 + fc, w_down[ct, fc])], mix, r_mix, T)
                        yr, ryr = tmpC.get()
                        S.dma("sp", "yld", yr[:, 0:TP], yT_own[ct * 128:(ct + 1) * 128, tok0:tok0 + TP], R=[r_yb[ct]], W=[ryr])
                        I("dve", "tensor_tensor", yr[:, 0:TP], yr[:, 0:TP], ps[:, 0, 0:TP], ALU.add, R=[ryr, rps[0]], W=[ryr])
                        tk = S.dma("sp", "yb", yT_own[ct * 128:(ct + 1) * 128, tok0:tok0 + TP], yr[:, 0:TP], R=[ryr], W=[r_yb[ct]])
                        I("dve", "tensor_tensor", smacc[:, ct, :], smacc[:, ct, :], ps[:, 0, TP:T], ALU.add, R=[rps[0], r_smacc], W=[r_smacc])
                        if fc == 3: S.out_toks.append(tk)
                S.out_toks.append(S.dma("sp", "yso", yT_smp.ap().rearrange("(c k) t -> k c t", k=128)[:, :, p * TS:(p + 1) * TS], smacc[:], R=[r_smacc]))

            r_yb = [Res() for _ in range(32)]

            def attn_finish(O0, O1, rO, M, dstT, rdst):
                cc, rcc = col.get()
                I("dve", "reciprocal", cc[0:M, 0:1], O0[0:M, 128:129], R=[rO], W=[rcc])
                I("dve", "reciprocal", cc[0:M, 1:2], O1[0:M, 128:129], R=[rO], W=[rcc])
                I("dve", "tensor_tensor", cc[0:M, 1:2], cc[0:M, 1:2], lamt[0:M, 0:1], ALU.mult, R=[rcc, r_lam], W=[rcc])
                o, ro = sm1.get()
                I("dve", "tensor_scalar", o[0:M, 0:128], O0[0:M, 0:128], cc[0:M, 0:1], None, ALU.mult, R=[rO, rcc], W=[ro])
                I("dve", "scalar_tensor_tensor", o[0:M, 0:128], O1[0:M, 0:128], cc[0:M, 1:2], o[0:M, 0:128], ALU.mult, ALU.add, R=[rO, rcc, ro], W=[ro])
                jk, rjk = sm2.get()
                I("act", "activation", jk[0:M, 0:128], o[0:M, 0:128], AF.Square, accum_out=cc[0:M, 2:3], R=[ro], W=[rjk, rcc])
                I("act", "activation", cc[0:M, 2:3], cc[0:M, 2:3], AF.Sqrt, bias=cp[0:M, C_ONE + 2:C_ONE + 3], scale=1.0 / 128, R=[rcc, r_c], W=[rcc])
                I("dve", "reciprocal", cc[0:M, 2:3], cc[0:M, 2:3], R=[rcc], W=[rcc])
                I("dve", "tensor_scalar", cc[0:M, 2:3], cc[0:M, 2:3], 1.0 - LAM_INIT, None, ALU.mult, R=[rcc], W=[rcc])
                ob, rob = smb.get()
                I("dve", "scalar_tensor_tensor", ob[0:M, 0:128], o[0:M, 0:128], cc[0:M, 2:3], rp[0:M, OFF_SUB:OFF_SUB + 128], ALU.mult, ALU.mult, R=[ro, rcc, r_c], W=[rob])
                if dstT is not None:
                    transpose_to(dstT, ob[0:M, 0:128], rob, rdst, M, 128, bf=True, usedn=True)
                return ob, rob

            def sample_attention(p, b2):
                bb = NB2 * p + b2
                Oc = [PSD[2], PSD[3]]
                for pg in range(NPG_RUN + 1):
                    new = (pg == NPG_RUN)
                    L = 8 if new else 128
                    if not new:
                        kf, rkf = kpg.get(); vf, rvf = vpg.get()
                        if GMODE == 0:
                            S.dma("pool", "kg", kf[:, :], ckT.ap()[:, :], W=[rkf], R=[r_idx], ind=idx[:, bb * 64 + pg:bb * 64 + pg + 1])
                            S.dma("pool", "vg", vf[:, :], cv.ap()[:, :], W=[rvf], R=[r_idx], ind=idx[:, bb * 64 + pg:bb * 64 + pg + 1])
                        else:
                            S.dma("sp", "kg", kf[:, :], ckT.ap()[pg * 128:(pg + 1) * 128, :], W=[rkf])
                            S.dma("sp", "vg", vf[:, :], cv.ap()[pg * 128:(pg + 1) * 128, :], W=[rvf])
                        kb, rkb = kpb.get(); vb, rvb = vpb.get()
                        I("act", "activation", kb[:, :], kf[:, :], AF.Copy, R=[rkf], W=[rkb])
                        I("dve", "tensor_copy", vb[:, :, 0:128], vf[:, :].rearrange("p (j d) -> p j d", j=4), R=[rvf], W=[rvb])
                    ps, rps = dn()
                    for j in range(4):
                        for c in range(2):
                            lhs = ksm[c * 64:(c + 1) * 64, j, 8 * b2:8 * b2 + 8] if new else kb[c * 64:(c + 1) * 64, j * 128:(j + 1) * 128]
                            I("pe", "matmul", ps[0:L, c, j * 32:(j + 1) * 32], lhs, qsm[c * 64:(c + 1) * 64, b2, 4 * j:4 * j + 4, :].rearrange("p h t -> p (h t)"),
                              start=True, stop=True, R=[(r_ksm if new else rkb), r_qsm], W=[rps[c]], inc=(j == 3))
                    pt, rpt = smb.get()
                    I("act", "activation", pt[0:L, 0:256].rearrange("p (c n) -> p c n", c=2), ps[0:L, :, 0:128], AF.Exp, bias=cp[0:L, C_ONE + 3:C_ONE + 4], scale=0.125, R=[rps[0], rps[1], r_c], W=[rpt])
                    if new and not (SA_SKIP & 4):
                        I("dve", "tensor_tensor", pt[0:8, 0:256], pt[0:8, 0:256], t8[:, :], ALU.mult, R=[rpt, r_c], W=[rpt])
                    for j in range(4):
                        if SA_SKIP & 2: break
                        for c in range(2):
                            rhs = vsm[b2][:, j, 0:129] if new else vb[:, j, 0:129]
                            I("pe", "matmul", Oc[c][0:32, j // 2, (j % 2) * 130:(j % 2) * 130 + 129], pt[0:L, c * 128 + j * 32:c * 128 + (j + 1) * 32], rhs,
                              start=(pg == 0 and j % 2 == 0), stop=(new and j % 2 == 1), skip_group_check=True, R=[rpt, (r_vsm if new else rvb)], W=[r_PSD[2 + c][j // 2]], inc=(j == 3 and c == 1))
                if SA_CUT == 1: return
                for j in range(4):
                    O0 = Oc[0][:, j // 2, (j % 2) * 130:(j % 2) * 130 + 130]; O1 = Oc[1][:, j // 2, (j % 2) * 130:(j % 2) * 130 + 130]
                    rr = Res(); rr.w = None
                    ob, rob = attn_finish2(O0, O1, r_PSD[2][j // 2], r_PSD[3][j // 2])
                    if SA_CUT == 2: continue
                    pp_, rr_ = dn(); pp = pp_[:, 0, :]; rpp = rr_[0]
                    pv = pp.bitcast(BF16)
                    I("pe", "transpose", pv[:, 0:32], ob[0:32, 0:128], identb[0:32, 0:32], R=[rob, r_c], W=[rpp])
                    for hl in range(4):
                        I("act", "activation", mix[:, 4 * j + hl, TP + 8 * b2:TP + 8 * b2 + 8], pv[:, hl * 8:hl * 8 + 8], AF.Copy, R=[rpp], W=[r_mix[4 * j + hl]])

            def attn_finish2(O0, O1, rO0, rO1):
                M = 32
                cc, rcc = col.get()
                I("dve", "reciprocal", cc[0:M, 0:1], O0[0:M, 128:129], R=[rO0], W=[rcc])
                I("dve", "reciprocal", cc[0:M, 1:2], O1[0:M, 128:129], R=[rO1], W=[rcc])
                I("dve", "tensor_tensor", cc[0:M, 1:2], cc[0:M, 1:2], lamt[0:M, 0:1], ALU.mult, R=[rcc, r_lam], W=[rcc])
                o, ro = sm1.get()
                I("dve", "tensor_scalar", o[0:M, 0:128], O0[0:M, 0:128], cc[0:M, 0:1], None, ALU.mult, R=[rO0, rcc], W=[ro])
                I("dve", "scalar_tensor_tensor", o[0:M, 0:128], O1[0:M, 0:128], cc[0:M, 1:2], o[0:M, 0:128], ALU.mult, ALU.add, R=[rO1, rcc, ro], W=[ro])
                jk, rjk = sm2.get()
                I("act", "activation", jk[0:M, 0:128], o[0:M, 0:128], AF.Square, accum_out=cc[0:M, 2:3], R=[ro], W=[rjk, rcc])
                I("act", "activation", cc[0:M, 2:3], cc[0:M, 2:3], AF.Sqrt, bias=cp[0:M, C_ONE + 2:C_ONE + 3], scale=1.0 / 128, R=[rcc, r_c], W=[rcc])
                I("dve", "reciprocal", cc[0:M, 2:3], cc[0:M, 2:3], R=[rcc], W=[rcc])
                I("dve", "tensor_scalar", cc[0:M, 2:3], cc[0:M, 2:3], 1.0 - LAM_INIT, None, ALU.mult, R=[rcc], W=[rcc])
                ob, rob = smb.get()
                I("dve", "scalar_tensor_tensor", ob[0:M, 0:128], o[0:M, 0:128], cc[0:M, 2:3], rp[0:M, OFF_SUB:OFF_SUB + 128], ALU.mult, ALU.mult, R=[ro, rcc, r_c], W=[rob])
                return ob, rob

            def ssd_group(kind, p, g, segs):
                own = (kind == "own")
                A = rp[:, OFF_AL + 4 * g:OFF_AL + 4 * g + 4]
                for si, (c0, L) in enumerate(segs):
                    smp = (si >= NQT)
                    bb = NB2 * p + (si - NQT) if smp else None
                    if smp:
                        stt, rst = stg.get()
                        S.dma("sp", "stl", stt[:, :], st_ssm[bb][:, g * 256:(g + 1) * 256], W=[rst])
                        Sg = stt[:, :]
                    else:
                        Sg = ST[:, g * 256:(g + 1) * 256]; rst = r_ST[g]
                    dt4 = dtk[0:L, si, 4 * g:4 * g + 4]
                    xs, rxs = q_xs.get()
                    for il in range(2):
                        transpose_to(xs[0:L, il * 128:(il + 1) * 128], xsT[:, il, c0:c0 + L], r_xsT[il], rxs, 128, L, bf=True)
                    Bt, rBt = q_Bt.get()
                    transpose_to(Bt[0:L, 0:128], BT[:, c0:c0 + L], r_BT, rBt, 128, L, bf=True)
                    cc, rcc = col.get()
                    I("dve", "tensor_tensor", cc[0:L, 0:4], dt4, A[0:L, :], ALU.mult, R=[r_dtk, r_c], W=[rcc])
                    p1, rp1 = sc()
                    I("pe", "matmul", p1[0:L, 0:4], tri[0:L, 0:L], cc[0:L, 0:4], start=True, stop=True, R=[rcc, r_c], W=[rp1])
                    I("pe", "matmul", p1[:, 4:8], all_ones[0:L, :], cc[0:L, 0:4], start=True, stop=True, R=[rcc, r_c], W=[rp1])
                    I("dve", "tensor_copy", cc[0:L, 4:8], p1[0:L, 0:4], R=[rp1], W=[rcc])
                    I("dve", "tensor_copy", cc[:, 8:12], p1[:, 4:8], R=[rp1], W=[rcc])
                    I("dve", "tensor_tensor", cc[0:L, 12:16], cc[0:L, 8:12], cc[0:L, 4:8], ALU.subtract, R=[rcc], W=[rcc])
                    I("act", "activation", cc[0:L, 12:16], cc[0:L, 12:16], AF.Exp, R=[rcc], W=[rcc])
                    I("act", "activation", cc[:, 16:20], cc[:, 8:12], AF.Exp, R=[rcc], W=[rcc])
                    I("act", "activation", cc[0:L, 20:24], cc[0:L, 4:8], AF.Exp, R=[rcc], W=[rcc])
                    I("dve", "tensor_scalar", cc[0:L, 24:28], cc[0:L, 4:8], -1.0, None, ALU.mult, R=[rcc], W=[rcc])
                    xdt, rxdt = q_xdt.get(); xdte, rxdte = q_xdte.get()
                    def v3(ap): return ap.rearrange("p (h x) -> p h x", h=4)
                    def bc(ap, n): return ap.unsqueeze(2).to_broadcast([L, 4, n])
                    I("dve", "tensor_tensor", v3(xdt[0:L, 0:256]), v3(xs[0:L, 0:256]), bc(dt4, 64), ALU.mult, R=[rxs, r_dtk], W=[rxdt])
                    I("dve", "tensor_tensor", v3(xdte[0:L, 0:256]), v3(xdt[0:L, 0:256]), bc(cc[0:L, 12:16], 64), ALU.mult, R=[rxdt, rcc], W=[rxdte])
                    if own:
                        Sb, rSb = q_Sb.get()
                        I("act", "activation", Sb[:, 0:256], Sg, AF.Copy, R=[rst], W=[rSb])
                        pyo, rpyo = sc()
                        I("pe", "matmul", pyo[0:L, 0:256], CTt[:, c0:c0 + L], Sb[:, 0:256], start=True, stop=True, R=[r_CT, rSb], W=[rpyo])
                        y, ry = q_y.get()
                        I("dve", "tensor_tensor", v3(y[0:L, 0:256]), v3(pyo[0:L, 0:256]), bc(cc[0:L, 20:24], 64), ALU.mult, R=[rpyo, rcc], W=[ry])
                        pcb, rpcb = sc()
                        I("pe", "matmul", pcb[0:L, 0:L], BT[:, c0:c0 + L], CTt[:, c0:c0 + L], start=True, stop=True, R=[r_BT, r_CT], W=[rpcb])
                        Dm, rDm = q_Dm.get()
                        def v4(ap): return ap.rearrange("p (h x) -> p h x", h=4)[:, :, 0:L]
                        I("dve", "tensor_tensor", v4(Dm[0:L, 0:512]), tri[0:L, 0:L].unsqueeze(1).to_broadcast([L, 4, L]), bc(cc[0:L, 0:4], L), ALU.mult, R=[rcc, r_c], W=[rDm])
                        pcr, rpcr = sc()
                        for h4 in range(4):
                            I("pe", "matmul", pcr[0:L, h4 * 128:h4 * 128 + L], all_ones[0:L, 0:L], Dm[0:L, h4 * 128:h4 * 128 + L], start=True, stop=True, R=[rDm, r_c], W=[rpcr], inc=(h4 == 3))
                        Lm, rLm = q_Lm.get()
                        Wm, rWm = q_Wm.get()
                        pyd, rpyd = sc()
                        I("dve", "tensor_tensor", v4(Lm[0:L, 0:512]), v4(pcr[0:L, 0:512]), negm[0:L, 0:L].unsqueeze(1).to_broadcast([L, 4, L]), ALU.add, R=[rpcr, r_c], W=[rLm])
                        I("dve", "tensor_tensor", v4(Lm[0:L, 0:512]), v4(Lm[0:L, 0:512]), bc(cc[0:L, 4:8], L), ALU.subtract, R=[rLm, rcc], W=[rLm])
                        I("act", "activation", v4(Lm[0:L, 0:512]), v4(Lm[0:L, 0:512]), AF.Exp, R=[rLm], W=[rLm])
                        I("dve", "tensor_tensor", v4(Wm[0:L, 0:512]), v4(Lm[0:L, 0:512]), pcb[0:L, 0:L].unsqueeze(1).to_broadcast([L, 4, L]), ALU.mult, R=[rLm, rpcb], W=[rWm])
                        for h4 in range(4):
                            I("pe", "matmul", pyd[0:L, h4 * 64:(h4 + 1) * 64], Wm[0:L, h4 * 128:h4 * 128 + L], xdt[0:L, h4 * 64:(h4 + 1) * 64], start=True, stop=True, R=[rWm, rxdt], W=[rpyd])
                        I("dve", "tensor_tensor", y[0:L, 0:256], y[0:L, 0:256], pyd[0:L, 0:256], ALU.add, R=[ry, rpyd], W=[ry])
                        for h4 in range(4):
                            I("dve", "scalar_tensor_tensor", y[0:L, h4 * 64:(h4 + 1) * 64], xs[0:L, h4 * 64:(h4 + 1) * 64], rp[0:L, OFF_DS + 4 * g + h4:OFF_DS + 4 * g + h4 + 1], y[0:L, h4 * 64:(h4 + 1) * 64], ALU.mult, ALU.add, R=[rxs, r_c, ry], W=[ry])
                        zz, rzz = q_zz.get()
                        for il in range(2):
                            pz, rpz = sc()
                            pzv = pz.bitcast(BF16)
                            I("pe", "transpose", pzv[0:L, 0:128], zT[:, il, c0:c0 + L], identb, R=[r_zT[il], r_c], W=[rpz])
                            I("act", "activation", zz[0:L, il * 128:(il + 1) * 128], pzv[0:L, 0:128], AF.Silu, R=[rpz], W=[rzz])
                        I("dve", "tensor_tensor", y[0:L, 0:256], y[0:L, 0:256], zz[0:L, 0:256], ALU.mult, R=[ry, rzz], W=[ry])
                        I("act", "activation", zz[0:L, 0:256], y[0:L, 0:256], AF.Square, accum_out=cc[0:L, 28:29], R=[ry], W=[rzz, rcc])
                        I("act", "activation", cc[0:L, 28:29], cc[0:L, 28:29], AF.Sqrt, bias=cp[0:L, C_ONE + 2:C_ONE + 3], scale=1.0 / 256, R=[rcc, r_c], W=[rcc])
                        I("dve", "reciprocal", cc[0:L, 28:29], cc[0:L, 28:29], R=[rcc], W=[rcc])
                        yb, ryb = q_yb.get()
                        I("dve", "tensor_scalar", yb[0:L, 0:256], y[0:L, 0:256], cc[0:L, 28:29], None, ALU.mult, R=[ry, rcc], W=[ryb])
                        for il in range(2):
                            transpose_to(mix[:, 16 + 2 * g + il, c0:c0 + L], yb[0:L, il * 128:(il + 1) * 128], ryb, r_mix[16 + 2 * g + il], L, 128, bf=True, scale=cp[:, C_SN + 2 * g + il:C_SN + 2 * g + il + 1])
                    pst, rpst = sc()
                    I("pe", "matmul", pst[:, 0:256], Bt[0:L, 0:128], xdte[0:L, 0:256], start=True, stop=True, R=[rBt, rxdte], W=[rpst])
                    I("dve", "tensor_tensor", v3(Sg), v3(Sg), cc[:, 16:20].unsqueeze(2).to_broadcast([128, 4, 64]), ALU.mult, R=[rst, rcc], W=[rst])
                    I("dve", "tensor_tensor", Sg, Sg, pst[:, 0:256], ALU.add, R=[rst, rpst], W=[rst])
                    if smp:
                        S.out_toks.append(S.dma("sp", "sto", ssm_smp[bb][:, g * 256:(g + 1) * 256], Sg, R=[rst]))

            for p in range(NP_CTX):
                run_pass("ctx", p)
            I("dve", "tensor_scalar", ST[:, :], ST[:, :], cp[:, C_FLAG:C_FLAG + 1], None, ALU.mult, R=r_ST + [r_c], W=r_ST)
            I("dve", "tensor_scalar", halo[:, :, :], halo[:, :, :], cp[:, C_FLAG:C_FLAG + 1], None, ALU.mult, R=r_halo + [r_c], W=r_halo)
            for p in range(NP_OWN):
                run_pass("own", p)
            S.out_toks.append(S.dma("sp", "fin", ssm_fin.ap(), ST[:, :], R=r_ST))
            S.out_toks.append(S.dma("sp", "fin", conv_fin.ap(), halo[:, :, :], R=r_halo))
            S.out_toks.append(S.dma("sp", "fin", conv_smp.ap(), convo[:, :, :, :], R=[r_convo]))
            for t in S.out_toks: S.wait_tok("sp", t)

        dry = Sched(nc, es, dry=True)
        emit(dry)
        ws.planned = True
        for k in cnt: cnt[k] = 0
        for obj in list(locals().values()):
            pass
        S = Sched(nc, es)
        _reset_all(locals())
        emit(S)
        S.build()
    return nc


def _reset_all(ns):
    def rs(o):
        if isinstance(o, Res): o.w = None; o.r = {}
        elif isinstance(o, Rot):
            o.i = 0
            for r in o.r: rs(r)
        elif isinstance(o, (list, tuple)):
            for x in o: rs(x)
    for v in ns.values(): rs(v)


def _rope_tab(pos):
    half = 8
    inv = np.exp(-math.log(500000.0) * np.arange(half, dtype=np.float32) * 2.0 / 16).astype(np.float32)
    ang = pos.astype(np.float32)[None, :] * inv[:, None]
    cos = np.ones((128, len(pos)), np.float32); sin = np.zeros((128, len(pos)), np.float32)
    for m in range(128):
        d = m % 64
        if d < 16:
            cos[m] = np.cos(ang[d % 8]); sin[m] = np.sin(ang[d % 8])
    return cos, sin


_PROG = None


def _prep(x_prompt, x_sample, cache_k, cache_v, state_ssm, state_conv, page_table,
          ln1_w, w_in, q_norm_w, k_norm_w, lambda_q1, lambda_k1, lambda_q2, lambda_k2,
          subln_w, conv_w, conv_b, dt_bias, a_log, d_skip, ssm_norm_w, w_out,
          ln2_w, w_up, w_down, cores=range(8)):
    f = lambda a: np.ascontiguousarray(np.asarray(a))
    x_prompt = f(x_prompt); x_sample = f(x_sample)
    w = np.zeros((D, 73 * 128), np.float32); w[:, :DIN] = np.asarray(w_in)[0]
    w_in_l = f(w.reshape(32, 128, 73, 128).transpose(2, 1, 0, 3))
    w_out_l = f(np.asarray(w_out)[0].reshape(32, 128, 32, 128).transpose(2, 1, 0, 3))
    w_up_l = f(np.asarray(w_up)[0].reshape(32, 128, 128, 128).transpose(2, 1, 0, 3))
    w_down_l = f(np.asarray(w_down)[0].reshape(4, 32, 128, 32, 128).transpose(3, 0, 2, 1, 4))
    cpk = np.zeros((128, 320), np.float32)
    cpk[:, 240:256] = np.asarray(ssm_norm_w)[0].reshape(16, 128).T
    cpk[:, 0:32] = np.asarray(ln1_w)[0].reshape(32, 128).T
    cpk[:, 32:64] = np.asarray(ln2_w)[0].reshape(32, 128).T
    cpk[:, 64:192] = np.asarray(conv_w)[0].reshape(4, 32, 128).transpose(2, 1, 0).reshape(128, 128)
    cpk[:, 192:224] = np.asarray(conv_b)[0].reshape(32, 128).T
    cpk[:, 224] = np.tile(np.asarray(q_norm_w)[0], 2); cpk[:, 225] = np.tile(np.asarray(k_norm_w)[0], 2)
    cpk[0:32, 226] = np.asarray(dt_bias)[0]
    cpk[:, 229] = 1.0; cpk[:, 230] = np.arange(128); cpk[:, 231] = EPS; cpk[:, 232] = 0.0
    cmat = np.zeros((128, 5, 128), np.float32)
    cmat[:, 0] = np.eye(128)
    cmat[:, 1] = np.triu(np.ones((128, 128)))
    cmat[:, 2] = np.kron(np.eye(2), np.ones((64, 64)))
    Rm = np.zeros((128, 128), np.float32)
    for m in range(128):
        d = m % 64
        if d < 8: Rm[m + 8, m] = -1.0
        elif d < 16: Rm[m - 8, m] = 1.0
    cmat[:, 3] = Rm
    cmat[:, 4] = (cmat[:, 1] - 1.0) * 30000.0
    tri8 = np.zeros((8, 256), np.float32)
    for s in range(8):
        for t in range(8):
            if s <= t: tri8[s, t::8] = 1.0
    rowp = np.concatenate([np.asarray(subln_w)[0], np.asarray(a_log)[0], np.asarray(d_skip)[0],
                           np.asarray(lambda_q1)[0], np.asarray(lambda_k1)[0], np.asarray(lambda_q2)[0], np.asarray(lambda_k2)[0]]).astype(np.float32)[None, :]
    ck = np.asarray(cache_k)[0]; cvv = np.asarray(cache_v)[0]
    ckT = f(ck.transpose(0, 3, 2, 1).reshape(2560 * 128, 512)) if False else f(ck.reshape(2560, 128, 4, 128).transpose(0, 3, 2, 1).reshape(2560 * 128, 512))
    cvf = f(cvv.reshape(2560 * 128, 512))
    pt = np.asarray(page_table).astype(np.int32)
    sssm = np.asarray(state_ssm)[0]; sconv = np.asarray(state_conv)[0]
    cs_smp, sn_smp = _rope_tab(PAST + np.arange(8))
    in_maps = []
    for c in cores:
        b, hf = c // 2, c % 2
        xo = f(x_prompt[b, hf * 1024:(hf + 1) * 1024].T)
        xc = f(x_prompt[b, 0:1024].T)
        xs = f(x_sample[4 * c:4 * c + 4].reshape(32, D).T)
        cpc = cpk.copy(); cpc[:, 227] = float(hf); cpc[:, 228] = 0.0 if hf else NEG
        ropes = np.zeros((128, 2, 2, 1032), np.float32)
        cc_, sc_ = _rope_tab(np.arange(1024)); ropes[:, 0, 0, :1024] = cc_; ropes[:, 1, 0, :1024] = sc_
        co_, so_ = _rope_tab(hf * 1024 + np.arange(1024)); ropes[:, 0, 1, :1024] = co_; ropes[:, 1, 1, :1024] = so_
        ropes[:, 0, 1, 1024:] = cs_smp; ropes[:, 1, 1, 1024:] = sn_smp
        stl = f(sssm[4 * c:4 * c + 4].reshape(4, 2048, 128).transpose(0, 2, 1))
        scl = f(sconv[4 * c:4 * c + 4].reshape(4, 3, 32, 128).transpose(3, 0, 2, 1))
        in_maps.append(dict(xT_own=xo, xT_ctx=xc, xT_smp=xs, ckT=ckT, cv=cvf, st_ssm=stl, st_conv=scl,
                            ptab=f(pt[4 * c:4 * c + 4].reshape(1, 256)), w_in=w_in_l, w_out=w_out_l, w_up=w_up_l, w_down=w_down_l,
                            cpk=cpc, cmat=cmat, ropes=ropes, rowp=rowp, tri8=tri8))
    return in_maps


def kernel(**inputs):
    global _PROG
    in_maps = _prep(**inputs)
    if _PROG is None:
        _PROG = build_program()
    res = run_bass_kernel_spmd(_PROG, in_maps, core_ids=list(range(8))).results
    return _post(res)


def _post(res, cores=range(8)):
    y_p = np.zeros((4, 2048, D), np.float32); y_s = np.zeros((32, 8, D), np.float32)
    k_p = np.zeros((1, 4, 2048, 4, 128), np.float32); v_p = np.zeros_like(k_p)
    s_p = np.zeros((1, 4, 32, 64, 128), np.float32); c_p = np.zeros((1, 4, 3, 4096), np.float32)
    k_s = np.zeros((1, 32, 8, 4, 128), np.float32); v_s = np.zeros_like(k_s)
    s_s = np.zeros((1, 32, 32, 64, 128), np.float32); c_s = np.zeros((1, 32, 3, 4096), np.float32)
    for ci_, c in enumerate(cores):
        r = res[ci_]; b, hf = c // 2, c % 2
        sl = slice(hf * 1024, (hf + 1) * 1024)
        y_p[b, sl] = r["yT_own"].T
        y_s[4 * c:4 * c + 4] = r["yT_smp"].T.reshape(4, 8, D)
        k_p[0, b, sl] = r["kT_own"].T.reshape(1024, 4, 128); v_p[0, b, sl] = r["vT_own"].T.reshape(1024, 4, 128)
        k_s[0, 4 * c:4 * c + 4] = r["kT_smp"].T.reshape(4, 8, 4, 128); v_s[0, 4 * c:4 * c + 4] = r["vT_smp"].T.reshape(4, 8, 4, 128)
        if hf == 1:
            s_p[0, b] = r["ssm_fin"].T.reshape(32, 64, 128)
            c_p[0, b] = r["conv_fin"].transpose(2, 1, 0).reshape(3, 4096)
        s_s[0, 4 * c:4 * c + 4] = r["ssm_smp"].transpose(0, 2, 1).reshape(4, 32, 64, 128)
        c_s[0, 4 * c:4 * c + 4] = r["conv_smp"].transpose(1, 3, 2, 0).reshape(4, 3, 4096)
    return (y_p, y_s, k_p, v_p, s_p, c_p, k_s, v_s, s_s, c_s)
```
